# Optimizing a Trainium2 kernel written in Bass

```python
import jax, jax.numpy as jnp
from jax import lax
import numpy as np

D_MODEL = 4096
BATCH = 4
SEQ = 4096
DEPTH = 2

HEAD_DIM = 128
CHUNK = 128
WINDOW = 128
ROPE_THETA = 10000.0
NORM_EPS = 1e-5
NEG_INF = -1e30
D_FF = 4 * D_MODEL
MIX_WIDTH = D_MODEL
SGU_WIDTH = MIX_WIDTH // 4
RET_WIDTH = MIX_WIDTH // 4
ATT_WIDTH = MIX_WIDTH - SGU_WIDTH - RET_WIDTH
N_SGU_GROUPS = SGU_WIDTH // HEAD_DIM
N_RET_HEADS = RET_WIDTH // HEAD_DIM
N_Q_HEADS = ATT_WIDTH // HEAD_DIM
N_KV_HEADS = N_Q_HEADS // 4
KV_WIDTH = N_KV_HEADS * HEAD_DIM
IN_WIDTHS = (SGU_WIDTH, SGU_WIDTH, RET_WIDTH, RET_WIDTH, RET_WIDTH, RET_WIDTH,
             ATT_WIDTH, KV_WIDTH, KV_WIDTH)
IN_WIDTH = sum(IN_WIDTHS)

kernel_name = "hybrid_sgu_retention_swa_encoder"


def rms_norm(x, g):
    xf = x.astype(jnp.float32)
    y = xf * lax.rsqrt(jnp.mean(xf * xf, axis=-1, keepdims=True) + NORM_EPS)
    return (y * g.astype(jnp.float32)).astype(x.dtype)


def rope_tables(seq, dtype):
    pos = jnp.arange(seq, dtype=jnp.float32)
    inv = ROPE_THETA ** (-jnp.arange(0, HEAD_DIM, 2, dtype=jnp.float32) / HEAD_DIM)
    ang = pos[:, None] * inv[None, :]
    ang = jnp.concatenate([ang, ang], axis=-1)
    return jnp.cos(ang).astype(dtype), jnp.sin(ang).astype(dtype)


def apply_rope(t, cos, sin):
    t1, t2 = jnp.split(t, 2, axis=-1)
    rot = jnp.concatenate([-t2, t1], axis=-1)
    return t * cos[None, :, None, :] + rot * sin[None, :, None, :]


def spatial_gating(u, v, ln_g, ln_b, w_s, b_s):
    B, S, _ = u.shape
    u = jax.nn.gelu(u, approximate=False)
    v = jax.nn.gelu(v, approximate=False)
    vf = v.astype(jnp.float32)
    mu = jnp.mean(vf, axis=-1, keepdims=True)
    var = jnp.mean(jnp.square(vf - mu), axis=-1, keepdims=True)
    vn = ((vf - mu) * lax.rsqrt(var + NORM_EPS) * ln_g.astype(jnp.float32)
          + ln_b.astype(jnp.float32)).astype(v.dtype)
    vc = vn.reshape(B, S // CHUNK, CHUNK, N_SGU_GROUPS, HEAD_DIM)
    s = jnp.einsum('gij,bnjgd->bnigd', w_s, vc) + b_s.T[None, None, :, :, None]
    return u * s.reshape(B, S, SGU_WIDTH)


def retention_direction(q, k, v, log_gamma, include_diag):
    B, S, H, D = q.shape
    N = S // CHUNK
    dt = q.dtype
    qc = q.reshape(B, N, CHUNK, H, D)
    kc = k.reshape(B, N, CHUNK, H, D)
    vc = v.reshape(B, N, CHUNK, H, D)
    idx = jnp.arange(CHUNK, dtype=jnp.float32)
    delta = idx[:, None] - idx[None, :]
    mask = (delta >= 0) if include_diag else (delta > 0)
    decay_in = jnp.where(mask[None],
                         jnp.exp(log_gamma[:, None, None] * jnp.maximum(delta, 0.0)[None]),
                         0.0)
    scores = jnp.einsum('bnihd,bnjhd->bnhij', qc, kc) * decay_in.astype(dt)[None, None]
    inner = jnp.einsum('bnhij,bnjhd->bnihd', scores, vc)
    w_k = jnp.exp(log_gamma[None, :] * (CHUNK - 1 - idx)[:, None]).astype(dt)
    kv = jnp.einsum('bnjhd,bnjhe->nbhde', kc * w_k[None, None, :, :, None], vc)
    kv = kv.astype(jnp.float32)
    chunk_decay = jnp.exp(log_gamma * CHUNK)[None, :, None, None]

    def step(state, kv_n):
        return chunk_decay * state + kv_n, state

    _, prev = lax.scan(step, jnp.zeros((B, H, D, D), jnp.float32), kv)
    w_q = jnp.exp(log_gamma[None, :] * (idx + 1.0)[:, None]).astype(dt)
    cross = jnp.einsum('bnihd,nbhde->bnihe', qc * w_q[None, None, :, :, None], prev.astype(dt))
    return (inner + cross).reshape(B, S, H, D)


def retention_mixer(q, k, v, g, log_decay_raw, cos, sin):
    B, S, H, D = q.shape
    q = apply_rope(q, cos, sin)
    k = apply_rope(k, cos, sin) * (D ** -0.5)
    log_gamma = -jnp.exp(log_decay_raw.astype(jnp.float32))
    fwd = retention_direction(q, k, v, log_gamma[0], True)
    bwd = jnp.flip(retention_direction(jnp.flip(q, 1), jnp.flip(k, 1), jnp.flip(v, 1),
                                       log_gamma[1], False), 1)
    rf = (fwd + bwd).astype(jnp.float32)
    rn = rf * lax.rsqrt(jnp.mean(rf * rf, axis=-1, keepdims=True) + NORM_EPS)
    return jax.nn.silu(g) * rn.reshape(B, S, H * D).astype(g.dtype)


def window_attention(q, k, v, sink, cos, sin):
    B, S, _, D = q.shape
    N = S // CHUNK
    G = N_Q_HEADS // N_KV_HEADS
    q = apply_rope(q, cos, sin) * (D ** -0.5)
    k = apply_rope(k, cos, sin)
    qb = q.reshape(B, N, CHUNK, N_KV_HEADS, G, D)

    def neighbours(t):
        tp = jnp.pad(t, ((0, 0), (CHUNK, CHUNK), (0, 0), (0, 0)))
        tp = tp.reshape(B, N + 2, CHUNK, N_KV_HEADS, D)
        return jnp.concatenate([tp[:, :-2], tp[:, 1:-1], tp[:, 2:]], axis=2)

    kb, vb = neighbours(k), neighbours(v)
    s = jnp.einsum('bnikgd,bnjkd->bnkgij', qb, kb).astype(jnp.float32)
    qi = jnp.arange(CHUNK)
    kj = jnp.arange(3 * CHUNK)
    rel = kj[None, :] - CHUNK - qi[:, None]
    kpos = (jnp.arange(N)[:, None] - 1) * CHUNK + kj[None, :]
    valid = (jnp.abs(rel) <= WINDOW)[None] & ((kpos >= 0) & (kpos < S))[:, None, :]
    s = jnp.where(valid[None, :, None, None], s, NEG_INF)
    sink_f = sink.astype(jnp.float32).reshape(N_KV_HEADS, G)[None, None, :, :, None, None]
    m = jnp.maximum(jnp.max(s, axis=-1, keepdims=True), sink_f)
    p = jnp.exp(s - m)
    p = p / (jnp.sum(p, axis=-1, keepdims=True) + jnp.exp(sink_f - m))
    o = jnp.einsum('bnkgij,bnjkd->bnikgd', p.astype(v.dtype), vb)
    return o.reshape(B, S, ATT_WIDTH)


def hybrid_layer(x, ln_mix_g, w_in, sgu_ln_g, sgu_ln_b, sgu_w, sgu_b, ret_log_decay,
                 attn_sink, w_out, ln_mlp_g, w_up, w_down, cos, sin):
    B, S, _ = x.shape
    h = rms_norm(x, ln_mix_g)
    proj = h @ w_in
    splits = [int(o) for o in np.cumsum(IN_WIDTHS)[:-1]]
    u, v, rq, rk, rv, rg, aq, ak, av = jnp.split(proj, splits, axis=-1)
    a_out = spatial_gating(u, v, sgu_ln_g, sgu_ln_b, sgu_w, sgu_b)
    r_out = retention_mixer(rq.reshape(B, S, N_RET_HEADS, HEAD_DIM),
                            rk.reshape(B, S, N_RET_HEADS, HEAD_DIM),
                            rv.reshape(B, S, N_RET_HEADS, HEAD_DIM),
                            rg, ret_log_decay, cos, sin)
    c_out = window_attention(aq.reshape(B, S, N_Q_HEADS, HEAD_DIM),
                             ak.reshape(B, S, N_KV_HEADS, HEAD_DIM),
                             av.reshape(B, S, N_KV_HEADS, HEAD_DIM),
                             attn_sink, cos, sin)
    x = x + jnp.concatenate([a_out, r_out, c_out], axis=-1) @ w_out
    h = rms_norm(x, ln_mlp_g)
    x = x + jnp.square(jax.nn.relu(h @ w_up)) @ w_down
    return x


def setup_inputs(seed: int = 0) -> dict:
    key = jax.random.key(seed)
    ks = jax.random.split(key, 16)
    f32 = jnp.float32
    nrm = lambda k, shape: jax.random.normal(k, shape, dtype=f32)
    x = nrm(ks[0], (BATCH, SEQ, D_MODEL))
    ln_mix_g = 1.0 + 0.02 * nrm(ks[1], (DEPTH, D_MODEL))
    w_in = nrm(ks[2], (DEPTH, D_MODEL, IN_WIDTH)) * (D_MODEL ** -0.5)
    sgu_ln_g = 1.0 + 0.02 * nrm(ks[3], (DEPTH, SGU_WIDTH))
    sgu_ln_b = 0.02 * nrm(ks[4], (DEPTH, SGU_WIDTH))
    sgu_w = nrm(ks[5], (DEPTH, N_SGU_GROUPS, CHUNK, CHUNK)) * (CHUNK ** -0.5)
    sgu_b = 1.0 + 0.02 * nrm(ks[6], (DEPTH, N_SGU_GROUPS, CHUNK))
    p = 2.0 ** (-5.0 - jnp.arange(N_RET_HEADS, dtype=f32))
    base = jnp.log(-jnp.log1p(-p))
    ret_log_decay = base[None, None, :] + 0.1 * nrm(ks[7], (DEPTH, 2, N_RET_HEADS))
    attn_sink = nrm(ks[8], (DEPTH, N_Q_HEADS))
    w_out = nrm(ks[9], (DEPTH, MIX_WIDTH, D_MODEL)) * (MIX_WIDTH ** -0.5)
    ln_mlp_g = 1.0 + 0.02 * nrm(ks[10], (DEPTH, D_MODEL))
    w_up = nrm(ks[11], (DEPTH, D_MODEL, D_FF)) * (D_MODEL ** -0.5)
    w_down = nrm(ks[12], (DEPTH, D_FF, D_MODEL)) * (D_FF ** -0.5)
    final_norm_g = 1.0 + 0.02 * nrm(ks[13], (D_MODEL,))
    return {"x": x, "ln_mix_g": ln_mix_g, "w_in": w_in, "sgu_ln_g": sgu_ln_g,
            "sgu_ln_b": sgu_ln_b, "sgu_w": sgu_w, "sgu_b": sgu_b,
            "ret_log_decay": ret_log_decay, "attn_sink": attn_sink, "w_out": w_out,
            "ln_mlp_g": ln_mlp_g, "w_up": w_up, "w_down": w_down,
            "final_norm_g": final_norm_g}


def reference(x, ln_mix_g, w_in, sgu_ln_g, sgu_ln_b, sgu_w, sgu_b, ret_log_decay,
              attn_sink, w_out, ln_mlp_g, w_up, w_down, final_norm_g):
    cos, sin = rope_tables(x.shape[1], x.dtype)
    for l in range(DEPTH):
        x = hybrid_layer(x, ln_mix_g[l], w_in[l], sgu_ln_g[l], sgu_ln_b[l], sgu_w[l],
                         sgu_b[l], ret_log_decay[l], attn_sink[l], w_out[l],
                         ln_mlp_g[l], w_up[l], w_down[l], cos, sin)
    return rms_norm(x, final_norm_g)
```

```python
import numpy as np
from contextlib import ExitStack
import concourse.bass as bass
import concourse.mybir as mybir
from concourse.bass_utils import run_bass_kernel_spmd

F32 = mybir.dt.float32
BF16 = mybir.dt.bfloat16
AF = mybir.ActivationFunctionType
ALU = mybir.AluOpType
AX = mybir.AxisListType

D = 4096
DFF = 16384
INW = 9216
EPS = 1e-5
CG_TM = [2, 3, 4, 5, 6, 7, 8, 9, 12, 13, 14, 15, 16, 17]
CG_FM = [0, 1, 10, 11]
TMW = 7168
TV0, RQ0, RK0, RV0, AQ0, AK0, AV0 = 0, 1024, 2048, 3072, 4096, 6144, 6656
C_PF, C_MF, C_PB, C_MB, C_IP1, C_I128M, C_C128, C_ID, C_MPREV, C_MNEXT, C_ONES = [i * 128 for i in range(11)]
C_127MP = 11 * 128
C_P = 11 * 128 + 1
C_EPS = 11 * 128 + 2
NCT = 11 * 128 + 3
S_G1, S_G2, S_DEC, S_SINK, S_LNG, S_LNB, S_BSB = 0, 32, 64, 80, 96, 96 + 1024, 96 + 2048
NS = 96 + 3072

ENGINES = ["pe", "act", "dve", "pool", "sp"]
DEBUG_LINES = False
NAMES = {}


class Op:
    __slots__ = ("eng", "fn", "deps", "dma", "signal", "sem", "val", "line")

    def __init__(self, eng, fn, deps, dma):
        self.eng, self.fn, self.deps, self.dma = eng, fn, deps, dma
        self.signal = False
        self.sem = None
        self.val = 0


class Prog:
    KDMA = 8

    def __init__(self):
        self.ops = []
        self.tw = {}
        self.tr = {}
        self.pending = {}
        self.last = {}
        self.dmas = {"sp": [], "pool": []}
        self.nbank = 0
        self.ctr = {}

    def count(self, name, mod):
        v = self.ctr.get(name, 0)
        self.ctr[name] = v + 1
        return v % mod

    def bank(self):
        b = self.nbank % 8
        self.nbank += 1
        return b

    def add(self, eng, fn, r=(), w=(), dma=False):
        idx = len(self.ops)
        deps = set()
        for t in r:
            x = self.tw.get(t)
            if x is not None:
                deps.add(x)
        for t in w:
            x = self.tw.get(t)
            if x is not None:
                deps.add(x)
            rd = self.tr.get(t)
            if rd:
                deps.update(rd[0].values())
                deps.update(rd[1])
        if eng in self.pending:
            deps |= self.pending.pop(eng)
        for d in deps:
            self.ops[d].signal = True
        self.ops.append(Op(eng, fn, deps, dma))
        if DEBUG_LINES:
            import sys as _s
            f = _s._getframe(1)
            while f.f_code.co_name in ("add", "dma", "mm", "trn", "act", "tt", "ts", "stt", "cp", "memset"):
                f = f.f_back
            self.ops[-1].line = f.f_lineno
        for t in w:
            self.tw[t] = idx
            self.tr[t] = [{}, []]
        for t in r:
            rd = self.tr.setdefault(t, [{}, []])
            if dma:
                rd[1].append(idx)
            else:
                rd[0][eng] = idx
        self.last[eng] = idx
        if dma:
            self.dmas[eng].append(idx)
        return idx

    def barrier(self):
        deps = set(v for k, v in self.last.items() if k != "pool")
        deps.update(self.dmas["sp"][-self.KDMA:])
        for e in ENGINES:
            self.pending[e] = set(deps) | self.pending.get(e, set())
        self.tw = {}
        self.tr = {}

    def dma(self, q, out, in_, r=(), w=()):
        return self.add(q, lambda e: e.dma_start(out=out, in_=in_), r, w, dma=True)

    def mm(self, out, lhsT, rhs, start, stop, r, w):
        return self.add("pe", lambda e: e.matmul(out, lhsT=lhsT, rhs=rhs, start=start, stop=stop), r, w)

    def trn(self, out, in_, ident, r, w):
        return self.add("pe", lambda e: e.transpose(out=out, in_=in_, identity=ident), r, w)

    def act(self, out, in_, func, r, w, **kw):
        return self.add("act", lambda e: e.activation(out=out, in_=in_, func=func, **kw), r, w)

    def tt(self, eng, out, in0, in1, op, r, w):
        return self.add(eng, lambda e: e.tensor_tensor(out=out, in0=in0, in1=in1, op=op), r, w)

    def ts(self, eng, out, in0, s1, s2, op0, op1, r, w):
        if op1 is None:
            return self.add(eng, lambda e: e.tensor_scalar(out=out, in0=in0, scalar1=s1, scalar2=None, op0=op0), r, w)
        return self.add(eng, lambda e: e.tensor_scalar(out=out, in0=in0, scalar1=s1, scalar2=s2, op0=op0, op1=op1), r, w)

    def stt(self, eng, out, in0, scalar, in1, op0, op1, r, w):
        return self.add(eng, lambda e: e.scalar_tensor_tensor(out=out, in0=in0, scalar=scalar, in1=in1, op0=op0, op1=op1), r, w)

    def cp(self, eng, out, in_, r, w):
        if eng == "act":
            return self.add("act", lambda e: e.activation(out=out, in_=in_, func=AF.Copy), r, w)
        return self.add(eng, lambda e: e.tensor_copy(out=out, in_=in_), r, w)

    def memset(self, eng, ap, val, r, w):
        return self.add(eng, lambda e: e.memset(ap, val), r, w)

    def emit(self, nc, es):
        esem = {e: es.enter_context(nc.semaphore("s_" + e)) for e in ["pe", "act", "dve", "pool"]}
        dsem = {q: [es.enter_context(nc.semaphore("d_%s%d" % (q, i))) for i in range(self.KDMA)] for q in self.dmas}
        cnt = {e: 0 for e in esem}
        dcnt = {q: 0 for q in self.dmas}
        ops = self.ops
        for op in ops:
            if op.dma:
                q = op.eng
                j = dcnt[q]
                op.sem = dsem[q][j % self.KDMA]
                op.val = 16 * (j // self.KDMA + 1)
                if j >= self.KDMA:
                    op.deps.add(self.dmas[q][j - self.KDMA])
                dcnt[q] = j + 1
            elif op.signal and op.eng in esem:
                cnt[op.eng] += 1
                op.sem = esem[op.eng]
                op.val = cnt[op.eng]
        streams = {e: [] for e in ENGINES}
        for op in ops:
            streams[op.eng].append(op)
        block = es.enter_context(nc.Block())
        names = {"pe": "tensor", "act": "scalar", "dve": "vector", "pool": "gpsimd", "sp": "sync"}

        def make_body(ename):
            def body(eng):
                waited = {}
                for op in streams[ename]:
                    need = {}
                    for d in op.deps:
                        dop = ops[d]
                        if dop.sem is None:
                            continue
                        if ename == "pe" and dop.eng == "pe":
                            continue
                        k = dop.sem
                        if need.get(k, (None, 0))[1] < dop.val:
                            need[k] = (dop.sem, dop.val)
                    for k, (sem, val) in need.items():
                        if waited.get(k, 0) < val:
                            eng.wait_ge(sem, val)
                            waited[k] = val
                    ins = op.fn(eng)
                    if DEBUG_LINES:
                        try:
                            NAMES[ins.ins.name] = op.line
                        except Exception:
                            pass
                    if op.dma:
                        ins.then_inc(op.sem, 16)
                    elif op.signal and op.sem is not None:
                        ins.then_inc(op.sem, 1)
            return body

        for e in ENGINES:
            getattr(block, names[e])(make_body(e))


class Arena:
    def __init__(self, ap, nelem):
        self.ap = ap
        self.n = nelem
        self.off = 0

    def reset(self):
        self.off = 0

    def get(self, shape, dt):
        per = int(np.prod(shape[1:]))
        nb = per * (4 if dt == F32 else 2)
        nb = (nb + 63) // 64 * 64
        ne = nb // 2
        assert self.off + ne <= self.n, ("arena overflow", self.off, ne, self.n)
        v = self.ap[:, self.off:self.off + (per * 2 if dt == F32 else per)]
        self.off += ne
        if dt == F32:
            v = v.bitcast(F32)
        if len(shape) == 3:
            v = v.rearrange("p (a b) -> p a b", a=shape[1])
        return v


def v3(ap, a):
    return ap.rearrange("p (a b) -> p a b", a=a)


def flat(ap):
    return ap.rearrange("p a b -> p (a b)")


class Ctx:
    pass


def build(NT, L):
    NTT = NT // 512
    NCH = NT // 128
    nc = bass.Bass("TRN2", target_bir_lowering=False)
    P = Prog()
    C = Ctx()
    dt_in = lambda name, shape, dt=F32: nc.dram_tensor(name, shape, dt, kind="ExternalInput").ap()
    x_in = dt_in("x", [NT, D])
    w_in = dt_in("w_in", [L, D, INW])
    w_out = dt_in("w_out", [L, D, D])
    w_up = dt_in("w_up", [L, D, DFF])
    w_down = dt_in("w_down", [L, DFF, D])
    d_small = dt_in("small", [L, 128, NS])
    d_wst = dt_in("wst", [L, 128, 1024])
    d_ctab = dt_in("ctab", [128, NCT])
    d_rope = dt_in("rope", [NT, 512])
    d_gf = dt_in("gf", [128, D])
    y_out = nc.dram_tensor("y", [NT, D], F32, kind="ExternalOutput").ap()
    scr = lambda name, shape, dt: nc.dram_tensor(name, shape, dt, kind="Internal").ap()
    wb_in = scr("wb_in", [L, D, INW], BF16)
    wb_out = scr("wb_out", [L, D, D], BF16)
    wb_up = scr("wb_up", [L, D, DFF], BF16)
    wb_down = scr("wb_down", [L, DFF, D], BF16)
    pj_tm = scr("pj_tm", [NT, TMW], BF16)
    pj_fm = scr("pj_fm", [16, 128, NT], BF16)
    d_S = scr("d_S", [2, NCH, 128, 1024], BF16)
    mixT = scr("mixT", [32, 128, NT], BF16)
    XB = scr("XB", [NT, D], F32)

    with ExitStack() as es:
        ARN = 86528
        arena_t = es.enter_context(nc.sbuf_tensor("arena", [128, ARN], BF16))
        A = Arena(arena_t, ARN)
        ctab = es.enter_context(nc.sbuf_tensor("ctab_s", [128, NCT], F32))
        small = es.enter_context(nc.sbuf_tensor("small_s", [128, NS], F32))
        cb = es.enter_context(nc.sbuf_tensor("cb", [128, 5 * 128 + 1024], BF16))
        wst_bf = es.enter_context(nc.sbuf_tensor("wst_bf", [128, 1024], BF16))
        lc = es.enter_context(nc.sbuf_tensor("lc", [128, 64], F32))
        dmat = es.enter_context(nc.sbuf_tensor("dmat", [128, 1024], F32))
        wqf = es.enter_context(nc.sbuf_tensor("wqf", [128, 1024], F32))
        wqb = es.enter_context(nc.sbuf_tensor("wqb", [128, 1024], F32))
        st = es.enter_context(nc.sbuf_tensor("st", [128, 32], F32))
        banks = [es.enter_context(nc.psum_tensor("ps%d" % i, [128, 512], F32)) for i in range(8)]
        IDENT = cb[:, 0:128]
        ONES = cb[:, 128:256]
        MPREV4 = cb[:, 256:768]
        MNEXT4 = cb[:, 768:1280]
        EPSC = ctab[:, C_EPS:C_EPS + 1]
        LG, DEC, SE, WF, WB = lc[:, 0:16], lc[:, 16:32], lc[:, 32:48], lc[:, 48:56], lc[:, 56:64]

        def psb(b):
            return banks[b][:]

        def psbf(b):
            return banks[b][:].bitcast(BF16)

        for l in range(L):
            for cg in range(18):
                P.dma("pool", wb_in[l, :, cg * 512:(cg + 1) * 512], w_in[l, :, cg * 512:(cg + 1) * 512], w=[("wb", l, "in", cg)])
            for cg in range(8):
                P.dma("pool", wb_out[l, :, cg * 512:(cg + 1) * 512], w_out[l, :, cg * 512:(cg + 1) * 512], w=[("wb", l, "out", cg)])
            for cg in range(32):
                P.dma("pool", wb_up[l, :, cg * 512:(cg + 1) * 512], w_up[l, :, cg * 512:(cg + 1) * 512], w=[("wb", l, "up", cg)])
            for rp in range(8):
                P.dma("pool", wb_down[l, rp * 2048:(rp + 1) * 2048, :], w_down[l, rp * 2048:(rp + 1) * 2048, :], w=[("wb", l, "down", rp)])
        wbtok = {}
        for l in range(L):
            for nm, n in (("in", 18), ("out", 8), ("up", 32), ("down", 8)):
                for i in range(n):
                    wbtok[("wb", l, nm, i)] = P.tw[("wb", l, nm, i)]

        def restore_wb():
            for k, v in wbtok.items():
                P.tw[k] = v

        P.dma("sp", ctab[:], d_ctab, w=["ctab"])
        P.cp("dve", IDENT, ctab[:, C_ID:C_ID + 128], ["ctab"], ["ident"])
        P.cp("dve", ONES, ctab[:, C_ONES:C_ONES + 128], ["ctab"], ["ones"])
        for g in range(4):
            P.cp("dve", MPREV4[:, g * 128:(g + 1) * 128], ctab[:, C_MPREV:C_MPREV + 128], ["ctab"], [("mprev", g)])
            P.cp("dve", MNEXT4[:, g * 128:(g + 1) * 128], ctab[:, C_MNEXT:C_MNEXT + 128], ["ctab"], [("mnext", g)])
        P.barrier()
        restore_wb()

        def layer_consts(l):
            A.reset()
            tmpw = A.get([128, 1024], F32)
            tmp1 = A.get([128, 128], F32)
            tmp2 = A.get([128, 128], F32)
            P.dma("sp", small[:], d_small[l], w=["small"])
            P.dma("sp", tmpw, d_wst[l], w=["tmpw"])
            P.cp("dve", wst_bf[:], tmpw, ["tmpw"], ["wst"])
            P.act(LG, small[:, S_DEC:S_DEC + 16], AF.Exp, ["small"], ["lg0"])
            P.ts("dve", LG, LG, -1.0, None, ALU.mult, None, ["lg0"], ["lg"])
            P.act(DEC, LG, AF.Exp, ["lg"], ["dec"], scale=128.0)
            P.act(SE, small[:, S_SINK:S_SINK + 16], AF.Exp, ["small"], ["se"])
            P.ts("dve", WF, LG[:, 0:8], ctab[:, C_127MP:C_127MP + 1], None, ALU.mult, None, ["lg"], ["wf0"])
            P.act(WF, WF, AF.Exp, ["wf0"], ["wf"])
            P.ts("dve", WB, LG[:, 8:16], ctab[:, C_P:C_P + 1], None, ALU.mult, None, ["lg"], ["wb0"])
            P.act(WB, WB, AF.Exp, ["wb0"], ["wbw"])
            for h in range(8):
                hs = slice(h * 128, (h + 1) * 128)
                P.act(tmp1, ctab[:, C_PF:C_PF + 128], AF.Exp, ["lg", ("t1",)], [("t1",)], scale=LG[:, h:h + 1])
                P.tt("dve", tmp1, tmp1, ctab[:, C_MF:C_MF + 128], ALU.mult, [("t1",)], [("t1",)])
                P.act(tmp2, ctab[:, C_PB:C_PB + 128], AF.Exp, ["lg", ("t2",)], [("t2",)], scale=LG[:, 8 + h:9 + h])
                P.tt("dve", tmp2, tmp2, ctab[:, C_MB:C_MB + 128], ALU.mult, [("t2",)], [("t2",)])
                P.tt("dve", dmat[:, hs], tmp1, tmp2, ALU.add, [("t1",), ("t2",)], [("dmat", h)])
                P.act(wqf[:, hs], ctab[:, C_IP1:C_IP1 + 128], AF.Exp, ["lg"], [("wqf", h)], scale=LG[:, h:h + 1])
                P.act(wqb[:, hs], ctab[:, C_I128M:C_I128M + 128], AF.Exp, ["lg"], [("wqb", h)], scale=LG[:, 8 + h:9 + h])
            P.barrier()
            restore_wb()

        def alloc_dense(nxs=2):
            A.reset()
            C.H = A.get([128, 32, 512], BF16)
            C.W = [A.get([128, 16, 512], BF16) for _ in range(3)]
            C.XS = [A.get([128, D], F32) for _ in range(nxs)]
            C.XN = A.get([128, D], BF16)
            C.JUNK = A.get([128, 512], BF16)

        def Htok_all():
            return [("H", c, b) for c in range(32) for b in range(4)]

        def norm_tile(src, row0, src_tok, gcol):
            for b in range(4):
                s = P.count("xs", len(C.XS))
                xs = C.XS[s]
                P.dma("sp", xs, src[row0 + b * 128:row0 + (b + 1) * 128, :], r=src_tok, w=[("xs", s)])
                P.memset("dve", st[:, 0:8], 0.0, [("ss",)], [("ss",)])
                for c in range(8):
                    P.act(C.JUNK, xs[:, c * 512:(c + 1) * 512], AF.Square, [("xs", s), ("ss",)], [("ssc", c)],
                          accum_out=st[:, c:c + 1])
                P.add("dve", lambda e: e.tensor_reduce(out=st[:, 8:9], in_=st[:, 0:8], axis=AX.X, op=ALU.add),
                      [("ssc", c) for c in range(8)], [("ss",), ("st8",)])
                P.act(st[:, 9:10], st[:, 8:9], AF.Ln, [("st8",)], [("st9",)], scale=1.0 / D, bias=EPSC)
                P.act(st[:, 10:11], st[:, 9:10], AF.Exp, [("st9",)], [("st10",)], scale=-0.5)
                P.act(C.XN, xs, AF.Copy, [("xs", s), ("st10",), ("xn",)], [("xn",)], scale=st[:, 10:11])
                bs = [P.bank() for _ in range(4)]
                for c in range(32):
                    P.trn(psbf(bs[c // 8])[:, (c % 8) * 128:(c % 8 + 1) * 128], C.XN[:, c * 128:(c + 1) * 128], IDENT,
                          [("xn",), "ident"], [("ps", bs[c // 8])])
                for q in range(4):
                    P.tt("dve", C.H[:, q * 8:(q + 1) * 8, b * 128:(b + 1) * 128], v3(psbf(bs[q]), 8),
                         small[:, gcol + q * 8:gcol + (q + 1) * 8].unsqueeze(2).to_broadcast([128, 8, 128]), ALU.mult,
                         [("ps", bs[q]), "small"], [("H", c, b) for c in range(q * 8, (q + 1) * 8)])

        def wload(wb2d, k0, c0, toks):
            s = P.count("W", 3)
            src = wb2d.rearrange("(kc p) n -> p kc n", p=128)[:, k0:k0 + 16, c0:c0 + 512]
            P.dma("sp", C.W[s], src, r=toks, w=[("W", s)])
            return s

        def dense_group(wb2d, kparts, k0s, c0, toks, mode, src, srcname):
            bs = [P.bank() for _ in range(4)]
            nk = kparts * 16
            for kp in range(kparts):
                s = wload(wb2d, k0s[kp], c0, toks)
                Wt = C.W[s]
                for j in range(4):
                    for kc in range(16):
                        k = kp * 16 + kc
                        if mode == "FM":
                            P.mm(psb(bs[j]), Wt[:, kc, j * 128:(j + 1) * 128], src[:, k, :], k == 0, k == nk - 1,
                                 [("W", s)] + [(srcname, k, b) for b in range(4)], [("ps", bs[j])])
                        else:
                            P.mm(psb(bs[j]), src[:, k, j * 128:(j + 1) * 128], Wt[:, kc, :], k == 0, k == nk - 1,
                                 [("W", s), (srcname, k, j)], [("ps", bs[j])])
            return bs

        def phase1(l, src, src_tok_fn):
            alloc_dense()
            TAB = [A.get([128, 4, 512], F32) for _ in range(2)]
            OTM = [A.get([128, 4, 512], BF16) for _ in range(2)]
            OFM = [A.get([128, 4, 512], BF16) for _ in range(2)]
            RT = [A.get([128, 4, 128], F32) for _ in range(2)]
            for tt in range(NTT):
                tsl = P.count("tab", 2)
                P.dma("sp", TAB[tsl], d_rope[tt * 512:(tt + 1) * 512, :].rearrange("(b p) c -> p b c", p=128), w=[("tab", tsl)])
                norm_tile(src, tt * 512, src_tok_fn(tt), S_G1)
                for cg in range(18):
                    mode = "FM" if cg in CG_FM else "TM"
                    bs = dense_group(wb_in[l], 2, [0, 16], cg * 512, [("wb", l, "in", cg)], mode, C.H, "H")
                    if mode == "FM":
                        so = P.count("ofm", 2)
                        fn = AF.Gelu if cg < 2 else AF.Silu
                        for j in range(4):
                            P.act(OFM[so][:, j, :], psb(bs[j]), fn, [("ps", bs[j]), ("ofm", so)], [("ofmw", so, j)])
                        ch0 = {0: 0, 1: 4, 10: 8, 11: 12}[cg]
                        P.dma("sp", pj_fm[ch0:ch0 + 4, :, tt * 512:(tt + 1) * 512].rearrange("c p t -> p c t"), OFM[so],
                              r=[("ofmw", so, j) for j in range(4)], w=[("ofm", so), ("pjfm", tt)])
                        continue
                    so = P.count("otm", 2)
                    ti = CG_TM.index(cg)
                    for j in range(4):
                        dst = OTM[so][:, j, :]
                        rd = [("ps", bs[j]), ("otm", so)]
                        wr = [("otmw", so, j)]
                        if cg in (2, 3):
                            P.act(dst, psb(bs[j]), AF.Gelu, rd, wr)
                        elif cg in (8, 9, 17):
                            if j % 2 == 0:
                                P.cp("act", dst, psb(bs[j]), rd, wr)
                            else:
                                P.cp("dve", dst, psb(bs[j]), rd, wr)
                        else:
                            scaled = cg in (6, 7, 12, 13, 14, 15)
                            tc = 2 if scaled else 0
                            cosb = TAB[tsl][:, j, tc * 128:(tc + 1) * 128].unsqueeze(1).to_broadcast([128, 4, 128])
                            sinb = TAB[tsl][:, j, (tc + 1) * 128:(tc + 2) * 128]
                            sin_lo = sinb[:, 0:64].unsqueeze(1).to_broadcast([128, 4, 64])
                            sin_hi = sinb[:, 64:128].unsqueeze(1).to_broadcast([128, 4, 64])
                            pv = v3(psb(bs[j]), 4)
                            ra, rb = RT[0], RT[1]
                            P.tt("dve", ra, pv, cosb, ALU.mult, [("ps", bs[j]), ("tab", tsl), ("ra",)], [("ra",)])
                            P.tt("dve", rb[:, :, 0:64], pv[:, :, 64:128], sin_lo, ALU.mult,
                                 [("ps", bs[j]), ("tab", tsl), ("rb",)], [("rb0",)])
                            P.tt("dve", rb[:, :, 64:128], pv[:, :, 0:64], sin_hi, ALU.mult,
                                 [("ps", bs[j]), ("tab", tsl), ("rb",)], [("rb1",)])
                            P.tt("dve", v3(dst, 4), ra, rb, ALU.add, rd[1:] + [("ra",), ("rb0",), ("rb1",)], wr + [("ra",), ("rb",)])
                    P.dma("sp", pj_tm[tt * 512:(tt + 1) * 512, ti * 512:(ti + 1) * 512].rearrange("(b p) c -> p b c", p=128), OTM[so],
                          r=[("otmw", so, j) for j in range(4)], w=[("otm", so), ("pjtm", tt)])
            P.barrier()
            restore_wb()

        def phase2(l):
            A.reset()
            KVT = [A.get([128, 2048], BF16) for _ in range(2)]
            KW = A.get([128, 8, 128], BF16)
            SST = A.get([128, 8, 128], F32)
            SOUT = [A.get([128, 1024], BF16) for _ in range(2)]
            for di in range(2):
                wv = WF if di == 0 else WB
                order = list(range(NCH)) if di == 0 else list(range(NCH - 1, -1, -1))
                P.memset("dve", flat(SST), 0.0, [("S", h) for h in range(8)], [("S", h) for h in range(8)])
                for n in order:
                    s = P.count("kvt", 2)
                    P.dma("sp", KVT[s], pj_tm[n * 128:(n + 1) * 128, RK0:RK0 + 2048], w=[("kvt", s)])
                    so = P.count("sout", 2)
                    P.cp("act", SOUT[so], flat(SST), [("S", h) for h in range(8)] + [("sout", so)], [("soutw", so)])
                    P.dma("sp", d_S[di, n], SOUT[so], r=[("soutw", so)], w=[("sout", so), ("dS", di, n)])
                    P.tt("dve", KW, v3(KVT[s][:, 0:1024], 8), wv.unsqueeze(2).to_broadcast([128, 8, 128]), ALU.mult,
                         [("kvt", s), ("kw",)], [("kw",)])
                    bs = [P.bank(), P.bank()]
                    for h in range(8):
                        P.mm(psb(bs[h // 4])[:, (h % 4) * 128:(h % 4 + 1) * 128], KW[:, h, :],
                             KVT[s][:, 1024 + h * 128:1024 + (h + 1) * 128], True, True,
                             [("kw",), ("kvt", s)], [("ps", bs[h // 4])])
                    for h in range(8):
                        P.stt("dve", SST[:, h, :], SST[:, h, :], DEC[:, di * 8 + h:di * 8 + h + 1],
                              psb(bs[h // 4])[:, (h % 4) * 128:(h % 4 + 1) * 128], ALU.mult, ALU.add,
                              [("ps", bs[h // 4]), ("S", h)], [("S", h)])
            P.barrier()
            restore_wb()
            A.reset()
            TM = [A.get([128, TMW], BF16) for _ in range(2)]
            FMt = [A.get([128, 16, 512], BF16) for _ in range(2)]
            OUT = [A.get([128, 32, 512], BF16) for _ in range(1)]
            SF = [A.get([128, 8, 128], BF16) for _ in range(2)]
            SB = [A.get([128, 8, 128], BF16) for _ in range(2)]
            AKV = [A.get([128, 1024], BF16) for _ in range(4)]
            AKT = [A.get([128, 4, 128], BF16) for _ in range(4)]
            VNF = A.get([128, 1024], F32)
            VN = A.get([128, 1024], BF16)
            J2 = A.get([128, 1024], BF16)
            QT = A.get([128, 8, 128], BF16)
            KT = A.get([128, 8, 128], BF16)
            QWF = A.get([128, 8, 128], BF16)
            QWB = A.get([128, 8, 128], BF16)
            SMT = A.get([128, 8, 128], BF16)
            SQ = A.get([128, 1024], BF16)
            RSTD = A.get([128, 1024], F32)
            TMPF = [A.get([128, 4, 128], F32) for _ in range(2)]
            AQT = A.get([128, 16, 128], BF16)
            PT = [A.get([128, 512], BF16) for _ in range(6)]
            DEN = [A.get([128, 4, 128], F32) for _ in range(2)]
            S_LN = 16

            def prepare(m):
                r = m % 4
                P.dma("sp", AKV[r], pj_tm[m * 128:(m + 1) * 128, AK0:AK0 + 1024], w=[("akv", r)])
                b = P.bank()
                for kh in range(4):
                    P.trn(psbf(b)[:, kh * 128:(kh + 1) * 128], AKV[r][:, kh * 128:(kh + 1) * 128], IDENT,
                          [("akv", r), "ident"], [("ps", b)])
                P.cp("act", flat(AKT[r]), psbf(b)[:, 0:512], [("ps", b)], [("akt", r)])

            prepare(0)
            for n in range(NCH):
                if n + 1 < NCH:
                    prepare(n + 1)
                g4 = n % 4
                ncs = slice(g4 * 128, (g4 + 1) * 128)
                if g4 == 0:
                    fs = P.count("fmt", 2)
                    os_ = P.count("out", 1)
                    P.dma("sp", FMt[fs], pj_fm[:, :, n * 128:n * 128 + 512].rearrange("c p t -> p c t"), w=[("fmt", fs)])
                ts_ = P.count("tm", 2)
                TMs = TM[ts_]
                P.dma("sp", TMs, pj_tm[n * 128:(n + 1) * 128, :], w=[("tm", ts_)])
                ss_ = P.count("sfb", 2)
                P.dma("sp", flat(SF[ss_]), d_S[0, n], w=[("sf", ss_)])
                P.dma("sp", flat(SB[ss_]), d_S[1, n], w=[("sb", ss_)])
                outw = [("outw", os_, n % 4, i) for i in range(8)]
                v = TMs[:, TV0:TV0 + 1024]
                c0 = S_LN
                P.memset("dve", st[:, c0:c0 + 2], 0.0, [("ln",)], [("ln0",)])
                P.act(J2, v, AF.Copy, [("tm", ts_), ("ln0",)], [("lna",)], accum_out=st[:, c0:c0 + 1])
                P.act(J2, v, AF.Square, [("tm", ts_), ("ln0",)], [("lnb",)], accum_out=st[:, c0 + 1:c0 + 2])
                P.ts("dve", st[:, c0 + 2:c0 + 4], st[:, c0:c0 + 2], 1.0 / 1024, None, ALU.mult, None, [("lna",), ("lnb",)], [("ln1",)])
                P.tt("dve", st[:, c0 + 4:c0 + 5], st[:, c0 + 2:c0 + 3], st[:, c0 + 2:c0 + 3], ALU.mult, [("ln1",)], [("ln2",)])
                P.tt("dve", st[:, c0 + 5:c0 + 6], st[:, c0 + 3:c0 + 4], st[:, c0 + 4:c0 + 5], ALU.subtract, [("ln1",), ("ln2",)], [("ln3",)])
                P.act(st[:, c0 + 8:c0 + 9], st[:, c0 + 5:c0 + 6], AF.Ln, [("ln3",)], [("ln3b",)], bias=EPSC)
                P.act(st[:, c0 + 6:c0 + 7], st[:, c0 + 8:c0 + 9], AF.Exp, [("ln3b",)], [("ln4",)], scale=-0.5)
                P.stt("dve", st[:, c0 + 7:c0 + 8], st[:, c0 + 2:c0 + 3], -1.0, st[:, c0 + 6:c0 + 7], ALU.mult, ALU.mult,
                      [("ln1",), ("ln4",)], [("ln5",)])
                P.act(VNF, v, AF.Identity, [("tm", ts_), ("ln4",), ("ln5",), ("vnf",)], [("vnf",)],
                      scale=st[:, c0 + 6:c0 + 7], bias=st[:, c0 + 7:c0 + 8])
                P.tt("dve", VNF, VNF, small[:, S_LNG:S_LNG + 1024], ALU.mult, [("vnf",), "small"], [("vnf",), ("ln",)])
                P.tt("dve", VN, VNF, small[:, S_LNB:S_LNB + 1024], ALU.add, [("vnf",), "small", ("vn",)], [("vn",)])
                bs = [P.bank(), P.bank()]
                for g in range(8):
                    P.mm(psb(bs[g // 4])[:, (g % 4) * 128:(g % 4 + 1) * 128], VN[:, g * 128:(g + 1) * 128],
                         wst_bf[:, g * 128:(g + 1) * 128], True, True, [("vn",), "wst"], [("ps", bs[g // 4])])
                for gb in range(2):
                    tf = TMPF[gb]
                    P.tt("dve", flat(tf), psb(bs[gb]), small[:, S_BSB + gb * 512:S_BSB + (gb + 1) * 512], ALU.add,
                         [("ps", bs[gb]), "small", ("tmpf", gb)], [("tmpf", gb)])
                    P.tt("dve", OUT[os_][:, gb * 4:(gb + 1) * 4, ncs], tf, FMt[fs][:, gb * 4:(gb + 1) * 4, ncs], ALU.mult,
                         [("tmpf", gb), ("fmt", fs), ("out", os_)], [outw[gb]])
                bq, bk = P.bank(), P.bank()
                for h in range(8):
                    P.trn(psbf(bq)[:, h * 128:(h + 1) * 128], TMs[:, RQ0 + h * 128:RQ0 + (h + 1) * 128], IDENT,
                          [("tm", ts_), "ident"], [("ps", bq)])
                for h in range(8):
                    P.trn(psbf(bk)[:, h * 128:(h + 1) * 128], TMs[:, RK0 + h * 128:RK0 + (h + 1) * 128], IDENT,
                          [("tm", ts_), "ident"], [("ps", bk)])
                P.cp("act", flat(QT), psbf(bq), [("ps", bq), ("qt",)], [("qt",)])
                P.cp("dve", flat(KT), psbf(bk), [("ps", bk), ("kt",)], [("kt",)])
                P.tt("dve", flat(QWF), flat(QT), wqf[:], ALU.mult, [("qt",), ("qwf",)] + [("wqf", h) for h in range(8)], [("qwf",)])
                P.tt("dve", flat(QWB), flat(QT), wqb[:], ALU.mult, [("qt",), ("qwb",)] + [("wqb", h) for h in range(8)], [("qwb",)])
                bsc = [P.bank(), P.bank()]
                for h in range(8):
                    P.mm(psb(bsc[h // 4])[:, (h % 4) * 128:(h % 4 + 1) * 128], KT[:, h, :], QT[:, h, :], True, True,
                         [("kt",), ("qt",)], [("ps", bsc[h // 4])])
                for hb in range(2):
                    P.tt("dve", flat(SMT[:, hb * 4:(hb + 1) * 4, :]), psb(bsc[hb]), dmat[:, hb * 512:(hb + 1) * 512], ALU.mult,
                         [("ps", bsc[hb]), ("smt", hb)] + [("dmat", h) for h in range(8)], [("smt", hb)])
                br = [P.bank(), P.bank()]
                for h in range(8):
                    o = psb(br[h // 4])[:, (h % 4) * 128:(h % 4 + 1) * 128]
                    wr = [("ps", br[h // 4])]
                    P.mm(o, TMs[:, RV0 + h * 128:RV0 + (h + 1) * 128], SMT[:, h, :], True, False, [("tm", ts_), ("smt", h // 4)], wr)
                    P.mm(o, SF[ss_][:, h, :], QWF[:, h, :], False, False, [("sf", ss_), ("qwf",)], wr)
                    P.mm(o, SB[ss_][:, h, :], QWB[:, h, :], False, True, [("sb", ss_), ("qwb",)], wr)
                bm = [P.bank(), P.bank()]
                for hb in range(2):
                    hs = slice(hb * 512, (hb + 1) * 512)
                    P.act(SQ[:, hs], psb(br[hb]), AF.Square, [("ps", br[hb]), ("sq", hb)], [("sq", hb)])
                    P.mm(psb(bm[hb]), ONES, SQ[:, hs], True, True, ["ones", ("sq", hb)], [("ps", bm[hb])])
                    P.act(RSTD[:, hs], psb(bm[hb]), AF.Ln, [("ps", bm[hb]), ("rstd", hb)], [("rstd", hb)], scale=1.0 / 128, bias=EPSC)
                    P.act(RSTD[:, hs], RSTD[:, hs], AF.Exp, [("rstd", hb)], [("rstd", hb)], scale=-0.5)
                    tf = TMPF[hb]
                    P.tt("dve", flat(tf), psb(br[hb]), RSTD[:, hs], ALU.mult, [("ps", br[hb]), ("rstd", hb), ("tmpf", hb)], [("tmpf", hb)])
                    P.tt("dve", OUT[os_][:, 8 + hb * 4:8 + (hb + 1) * 4, ncs], tf, FMt[fs][:, 8 + hb * 4:8 + (hb + 1) * 4, ncs], ALU.mult,
                         [("tmpf", hb), ("fmt", fs), ("out", os_)], [outw[2 + hb]])
                ba = [P.bank(), P.bank()]
                for hq in range(16):
                    P.trn(psbf(ba[hq // 8])[:, (hq % 8) * 128:(hq % 8 + 1) * 128], TMs[:, AQ0 + hq * 128:AQ0 + (hq + 1) * 128], IDENT,
                          [("tm", ts_), "ident"], [("ps", ba[hq // 8])])
                P.cp("act", flat(AQT[:, 0:8, :]), psbf(ba[0]), [("ps", ba[0]), ("aqt", 0)], [("aqt", 0)])
                P.cp("dve", flat(AQT[:, 8:16, :]), psbf(ba[1]), [("ps", ba[1]), ("aqt", 1)], [("aqt", 1)])
                for kh in range(4):
                    ms = [m for m in (n - 1, n, n + 1) if 0 <= m < NCH]
                    pts = []
                    for m in ms:
                        b = P.bank()
                        pi = P.count("pt", 6)
                        pts.append(pi)
                        P.mm(psb(b), AKT[m % 4][:, kh, :], flat(AQT[:, kh * 4:(kh + 1) * 4, :]), True, True,
                             [("akt", m % 4), ("aqt", kh // 2)], [("ps", b)])
                        if m == n:
                            P.act(PT[pi], psb(b), AF.Exp, [("ps", b), ("pt", pi)], [("pt", pi)])
                        else:
                            P.act(PT[pi], psb(b), AF.Exp, [("ps", b), ("pt", pi)], [("pt0", pi)])
                            mk = MPREV4 if m < n else MNEXT4
                            mt = [("mprev" if m < n else "mnext", g) for g in range(4)]
                            P.tt("dve", PT[pi], PT[pi], mk, ALU.mult, [("pt0", pi)] + mt, [("pt", pi)])
                    bo, bd = P.bank(), P.bank()
                    for i, m in enumerate(ms):
                        P.mm(psb(bo), AKV[m % 4][:, 512 + kh * 128:512 + (kh + 1) * 128], PT[pts[i]], i == 0, i == len(ms) - 1,
                             [("akv", m % 4), ("pt", pts[i])], [("ps", bo)])
                    for i, m in enumerate(ms):
                        P.mm(psb(bd), ONES, PT[pts[i]], i == 0, i == len(ms) - 1, ["ones", ("pt", pts[i])], [("ps", bd)])
                    dn = DEN[kh % 2]
                    for g in range(4):
                        P.ts("dve", dn[:, g, :], psb(bd)[:, g * 128:(g + 1) * 128], SE[:, kh * 4 + g:kh * 4 + g + 1], None, ALU.add, None,
                             [("ps", bd), "se", ("den", kh % 2)], [("dena", kh % 2, g)])
                    P.add("dve", lambda e, dn=dn: e.reciprocal(out=flat(dn), in_=flat(dn)),
                          [("dena", kh % 2, g) for g in range(4)], [("den", kh % 2)])
                    P.tt("dve", OUT[os_][:, 16 + kh * 4:16 + (kh + 1) * 4, ncs], v3(psb(bo), 4), dn, ALU.mult,
                         [("ps", bo), ("den", kh % 2), ("out", os_)], [outw[4 + kh]])
                if g4 == 3:
                    P.dma("sp", mixT[:, :, (n - 3) * 128:(n + 1) * 128].rearrange("c p t -> p c t"), OUT[os_],
                          r=[("outw", os_, q, i) for q in range(4) for i in range(8)], w=[("out", os_), ("mix", n // 4)])
            P.barrier()
            restore_wb()

        def phase3(l, res_is_x):
            alloc_dense(1)
            AT = A.get([128, 32, 512], BF16)
            XR = [A.get([128, 4, 512], F32) for _ in range(2)]
            RL = [A.get([128, 512], F32) for _ in range(2)]

            def resid_evac(bs, src, src_tok, tt, cg):
                s = P.count("xr", 2)
                view = lambda t: t[tt * 512:(tt + 1) * 512, cg * 512:(cg + 1) * 512].rearrange("(b p) c -> p b c", p=128)
                P.dma("sp", XR[s], view(src), r=src_tok, w=[("xr", s)])
                for j in range(4):
                    P.tt("dve", XR[s][:, j, :], psb(bs[j]), XR[s][:, j, :], ALU.add, [("ps", bs[j]), ("xr", s)], [("xrw", s, j)])
                P.dma("sp", view(XB), XR[s], r=[("xrw", s, j) for j in range(4)], w=[("xr", s), ("XB", tt, cg)])

            for tt in range(NTT):
                P.dma("sp", C.H, mixT[:, :, tt * 512:(tt + 1) * 512].rearrange("c p t -> p c t"), w=Htok_all())
                for cg in range(8):
                    bs = dense_group(wb_out[l], 2, [0, 16], cg * 512, [("wb", l, "out", cg)], "TM", C.H, "H")
                    if res_is_x:
                        resid_evac(bs, x_in, [], tt, cg)
                    else:
                        resid_evac(bs, XB, [("XB", tt, cg)], tt, cg)
                norm_tile(XB, tt * 512, [("XB", tt, cg) for cg in range(8)], S_G2)
                for q in range(4):
                    for cg in range(8):
                        bs = dense_group(wb_up[l], 2, [0, 16], q * 4096 + cg * 512, [("wb", l, "up", q * 8 + cg)], "FM", C.H, "H")
                        for j in range(4):
                            rs = P.count("relu", 2)
                            P.act(RL[rs], psb(bs[j]), AF.Relu, [("ps", bs[j]), ("rl", rs)], [("rl", rs)])
                            P.tt("dve", AT[:, cg * 4 + j, :], RL[rs], RL[rs], ALU.mult,
                                 [("rl", rs)], [("AT", cg * 4 + j, b) for b in range(4)])
                    for cg in range(8):
                        bs = dense_group(wb_down[l], 2, [q * 32, q * 32 + 16], cg * 512,
                                         [("wb", l, "down", q * 2), ("wb", l, "down", q * 2 + 1)], "TM", AT, "AT")
                        resid_evac(bs, XB, [("XB", tt, cg)], tt, cg)
            P.barrier()
            restore_wb()

        def final_norm():
            A.reset()
            GF = A.get([128, D], F32)
            XS = [A.get([128, D], F32) for _ in range(2)]
            JK = A.get([128, 512], BF16)
            P.dma("sp", GF, d_gf, w=["gf"])
            for blk in range(NCH):
                s = P.count("fxs", 2)
                xs = XS[s]
                P.dma("sp", xs, XB[blk * 128:(blk + 1) * 128, :], w=[("fxs", s)])
                P.memset("dve", st[:, 0:8], 0.0, [("ss",)], [("ss",)])
                for c in range(8):
                    P.act(JK, xs[:, c * 512:(c + 1) * 512], AF.Square, [("fxs", s), ("ss",)], [("ssc", c)], accum_out=st[:, c:c + 1])
                P.add("dve", lambda e: e.tensor_reduce(out=st[:, 8:9], in_=st[:, 0:8], axis=AX.X, op=ALU.add),
                      [("ssc", c) for c in range(8)], [("ss",), ("st8",)])
                P.act(st[:, 9:10], st[:, 8:9], AF.Ln, [("st8",)], [("st9",)], scale=1.0 / D, bias=EPSC)
                P.act(st[:, 10:11], st[:, 9:10], AF.Exp, [("st9",)], [("st10",)], scale=-0.5)
                P.act(xs, xs, AF.Copy, [("fxs", s), ("st10",)], [("fxs", s)], scale=st[:, 10:11])
                P.tt("dve", xs, xs, GF, ALU.mult, [("fxs", s), "gf"], [("fxs", s)])
                P.dma("sp", y_out[blk * 128:(blk + 1) * 128, :], xs, r=[("fxs", s)], w=[("fxs", s)])
            P.barrier()

        for l in range(L):
            layer_consts(l)
            if l == 0:
                phase1(l, x_in, lambda tt: [])
            else:
                phase1(l, XB, lambda tt: [])
            phase2(l)
            phase3(l, res_is_x=(l == 0))
        final_norm()
        for e in ENGINES:
            P.add(e, lambda eng: eng.nop(), [], [])
        P.emit(nc, es)
    return nc


def _ctab():
    p = np.arange(128, dtype=np.float32)[:, None]
    i = np.arange(128, dtype=np.float32)[None, :]
    t = np.zeros((128, NCT), np.float32)
    t[:, C_PF:C_PF + 128] = np.maximum(i - p, 0)
    t[:, C_MF:C_MF + 128] = (i >= p)
    t[:, C_PB:C_PB + 128] = np.maximum(p - i, 0)
    t[:, C_MB:C_MB + 128] = (p > i)
    t[:, C_IP1:C_IP1 + 128] = i + 1
    t[:, C_I128M:C_I128M + 128] = 128 - i
    t[:, C_C128:C_C128 + 128] = 128.0
    t[:, C_ID:C_ID + 128] = (i == p)
    t[:, C_MPREV:C_MPREV + 128] = (p >= i)
    t[:, C_MNEXT:C_MNEXT + 128] = (p <= i)
    t[:, C_ONES:C_ONES + 128] = 1.0
    t[:, C_127MP] = 127 - p[:, 0]
    t[:, C_P] = p[:, 0]
    t[:, C_EPS] = EPS
    return t


def _rope(NT):
    pos = np.arange(NT, dtype=np.float32)
    inv = (np.float32(10000.0) ** (-np.arange(0, 128, 2, dtype=np.float32) / np.float32(128))).astype(np.float32)
    ang = (pos[:, None] * inv[None, :]).astype(np.float32)
    ang = np.concatenate([ang, ang], -1)
    cos = np.cos(ang).astype(np.float32)
    sin = np.sin(ang).astype(np.float32)
    sins = np.concatenate([-sin[:, :64], sin[:, 64:]], -1)
    sc = np.float32(128 ** -0.5)
    return np.ascontiguousarray(np.concatenate([cos, sins, cos * sc, sins * sc], -1).astype(np.float32))


def _host_params(inp, L):
    small = np.zeros((L, 128, NS), np.float32)
    wst = np.zeros((L, 128, 1024), np.float32)
    for l in range(L):
        small[l, :, S_G1:S_G1 + 32] = inp["ln_mix_g"][l].reshape(32, 128).T
        small[l, :, S_G2:S_G2 + 32] = inp["ln_mlp_g"][l].reshape(32, 128).T
        small[l, :, S_DEC:S_DEC + 16] = inp["ret_log_decay"][l].reshape(1, 16)
        small[l, :, S_SINK:S_SINK + 16] = inp["attn_sink"][l].reshape(1, 16)
        small[l, :, S_LNG:S_LNG + 1024] = inp["sgu_ln_g"][l][None, :]
        small[l, :, S_LNB:S_LNB + 1024] = inp["sgu_ln_b"][l][None, :]
        small[l, :, S_BSB:S_BSB + 1024] = inp["sgu_b"][l].reshape(1, 1024)
        wst[l] = np.transpose(inp["sgu_w"][l], (2, 0, 1)).reshape(128, 1024)
    gf = np.ascontiguousarray(np.broadcast_to(inp["final_norm_g"][None, :], (128, D))).astype(np.float32)
    return small, wst, gf


_NC_CACHE = {}


def run(inp, NT, L, seqs, n_cores):
    key = (NT, L)
    if key not in _NC_CACHE:
        _NC_CACHE[key] = build(NT, L)
    nc = _NC_CACHE[key]
    small, wst, gf = _host_params(inp, L)
    ctab = _ctab()
    rope = _rope(NT)
    f = lambda a: np.ascontiguousarray(np.asarray(a, dtype=np.float32))
    shared = {"w_in": f(inp["w_in"][:L]), "w_out": f(inp["w_out"][:L]), "w_up": f(inp["w_up"][:L]),
              "w_down": f(inp["w_down"][:L]), "small": small, "wst": wst, "ctab": ctab, "rope": rope, "gf": gf}
    in_maps = []
    for c in range(n_cores):
        m = dict(shared)
        m["x"] = f(inp["x"][seqs[c]])
        in_maps.append(m)
    res = run_bass_kernel_spmd(nc, in_maps, core_ids=list(range(n_cores)))
    return [np.asarray(r["y"]) for r in res.results]


def kernel(x, ln_mix_g, w_in, sgu_ln_g, sgu_ln_b, sgu_w, sgu_b, ret_log_decay, attn_sink, w_out,
           ln_mlp_g, w_up, w_down, final_norm_g):
    inp = dict(x=np.asarray(x), ln_mix_g=np.asarray(ln_mix_g), w_in=np.asarray(w_in), sgu_ln_g=np.asarray(sgu_ln_g),
               sgu_ln_b=np.asarray(sgu_ln_b), sgu_w=np.asarray(sgu_w), sgu_b=np.asarray(sgu_b),
               ret_log_decay=np.asarray(ret_log_decay), attn_sink=np.asarray(attn_sink), w_out=np.asarray(w_out),
               ln_mlp_g=np.asarray(ln_mlp_g), w_up=np.asarray(w_up), w_down=np.asarray(w_down),
               final_norm_g=np.asarray(final_norm_g))
    B, S, _ = inp["x"].shape
    seqs = [c % B for c in range(8)]
    ys = run(inp, S, 2, seqs, 8)
    return np.stack([ys[b] for b in range(B)], 0).astype(np.float32)
```

```python
import numpy as np
from contextlib import ExitStack
import concourse.bass as bass
import concourse.mybir as mybir
from concourse.bass_utils import run_bass_kernel_spmd

F32 = mybir.dt.float32
BF16 = mybir.dt.bfloat16
AF = mybir.ActivationFunctionType
ALU = mybir.AluOpType
AX = mybir.AxisListType

D = 4096
DFF = 16384
INW = 9216
EPS = 1e-5
CG_TM = [2, 3, 4, 5, 6, 7, 8, 9, 12, 13, 14, 15, 16, 17]
CG_FM = [0, 1, 10, 11]
TMW = 7168
TV0, RQ0, RK0, RV0, AQ0, AK0, AV0 = 0, 1024, 2048, 3072, 4096, 6144, 6656
C_PF, C_MF, C_PB, C_MB, C_IP1, C_I128M, C_C128, C_ID, C_MPREV, C_MNEXT, C_ONES = [i * 128 for i in range(11)]
C_127MP = 11 * 128
C_P = 11 * 128 + 1
C_EPS = 11 * 128 + 2
NCT = 11 * 128 + 3
S_G1, S_G2, S_DEC, S_SINK, S_LNG, S_LNB, S_BSB = 0, 32, 64, 80, 96, 96 + 1024, 96 + 2048
NS = 96 + 3072

ENGINES = ["pe", "act", "dve", "pool", "sp"]
DEBUG_LINES = False
NAMES = {}


class Op:
    __slots__ = ("eng", "fn", "deps", "dma", "signal", "sem", "val", "line")

    def __init__(self, eng, fn, deps, dma):
        self.eng, self.fn, self.deps, self.dma = eng, fn, deps, dma
        self.signal = False
        self.sem = None
        self.val = 0


class Prog:
    KDMA = 8

    def __init__(self):
        self.ops = []
        self.tw = {}
        self.tr = {}
        self.pending = {}
        self.last = {}
        self.dmas = {"sp": [], "pool": [], "act": []}
        self.nbank = 0
        self.ctr = {}

    def count(self, name, mod):
        v = self.ctr.get(name, 0)
        self.ctr[name] = v + 1
        return v % mod

    def bank(self):
        b = self.nbank % 8
        self.nbank += 1
        return b

    def add(self, eng, fn, r=(), w=(), dma=False):
        idx = len(self.ops)
        deps = set()
        for t in r:
            x = self.tw.get(t)
            if x is not None:
                deps.add(x)
        for t in w:
            x = self.tw.get(t)
            if x is not None:
                deps.add(x)
            rd = self.tr.get(t)
            if rd:
                deps.update(rd[0].values())
                deps.update(rd[1])
        if eng in self.pending:
            deps |= self.pending.pop(eng)
        for d in deps:
            self.ops[d].signal = True
        self.ops.append(Op(eng, fn, deps, dma))
        if DEBUG_LINES:
            import sys as _s
            f = _s._getframe(1)
            while f.f_code.co_name in ("add", "dma", "mm", "trn", "act", "tt", "ts", "stt", "cp", "memset"):
                f = f.f_back
            self.ops[-1].line = f.f_lineno
        for t in w:
            self.tw[t] = idx
            self.tr[t] = [{}, []]
        for t in r:
            rd = self.tr.setdefault(t, [{}, []])
            if dma:
                rd[1].append(idx)
            else:
                rd[0][eng] = idx
        self.last[eng] = idx
        if dma:
            self.dmas[eng].append(idx)
        return idx

    def barrier(self):
        deps = set(v for k, v in self.last.items() if k != "pool")
        for q in ("sp", "act"):
            deps.update(self.dmas[q][-self.KDMA:])
        for e in ENGINES:
            self.pending[e] = set(deps) | self.pending.get(e, set())
        self.tw = {}
        self.tr = {}

    def dma(self, q, out, in_, r=(), w=()):
        return self.add(q, lambda e: e.dma_start(out=out, in_=in_), r, w, dma=True)

    def mm(self, out, lhsT, rhs, start, stop, r, w):
        return self.add("pe", lambda e: e.matmul(out, lhsT=lhsT, rhs=rhs, start=start, stop=stop), r, w)

    def trn(self, out, in_, ident, r, w):
        return self.add("pe", lambda e: e.transpose(out=out, in_=in_, identity=ident), r, w)

    def act(self, out, in_, func, r, w, **kw):
        return self.add("act", lambda e: e.activation(out=out, in_=in_, func=func, **kw), r, w)

    def tt(self, eng, out, in0, in1, op, r, w):
        return self.add(eng, lambda e: e.tensor_tensor(out=out, in0=in0, in1=in1, op=op), r, w)

    def ts(self, eng, out, in0, s1, s2, op0, op1, r, w):
        if op1 is None:
            return self.add(eng, lambda e: e.tensor_scalar(out=out, in0=in0, scalar1=s1, scalar2=None, op0=op0), r, w)
        return self.add(eng, lambda e: e.tensor_scalar(out=out, in0=in0, scalar1=s1, scalar2=s2, op0=op0, op1=op1), r, w)

    def stt(self, eng, out, in0, scalar, in1, op0, op1, r, w):
        return self.add(eng, lambda e: e.scalar_tensor_tensor(out=out, in0=in0, scalar=scalar, in1=in1, op0=op0, op1=op1), r, w)

    def cp(self, eng, out, in_, r, w):
        if eng == "act":
            return self.add("act", lambda e: e.activation(out=out, in_=in_, func=AF.Copy), r, w)
        return self.add(eng, lambda e: e.tensor_copy(out=out, in_=in_), r, w)

    def memset(self, eng, ap, val, r, w):
        return self.add(eng, lambda e: e.memset(ap, val), r, w)

    def emit(self, nc, es):
        esem = {e: es.enter_context(nc.semaphore("s_" + e)) for e in ["pe", "act", "dve", "pool"]}
        dsem = {q: [es.enter_context(nc.semaphore("d_%s%d" % (q, i))) for i in range(self.KDMA)] for q in self.dmas}
        cnt = {e: 0 for e in esem}
        dcnt = {q: 0 for q in self.dmas}
        ops = self.ops
        for op in ops:
            if op.dma:
                q = op.eng
                j = dcnt[q]
                op.sem = dsem[q][j % self.KDMA]
                op.val = 16 * (j // self.KDMA + 1)
                if j >= self.KDMA:
                    op.deps.add(self.dmas[q][j - self.KDMA])
                dcnt[q] = j + 1
            elif op.signal and op.eng in esem:
                cnt[op.eng] += 1
                op.sem = esem[op.eng]
                op.val = cnt[op.eng]
        streams = {e: [] for e in ENGINES}
        for op in ops:
            streams[op.eng].append(op)
        block = es.enter_context(nc.Block())
        names = {"pe": "tensor", "act": "scalar", "dve": "vector", "pool": "gpsimd", "sp": "sync"}

        def make_body(ename):
            def body(eng):
                waited = {}
                for op in streams[ename]:
                    need = {}
                    for d in op.deps:
                        dop = ops[d]
                        if dop.sem is None:
                            continue
                        if ename == "pe" and dop.eng == "pe":
                            continue
                        k = dop.sem
                        if need.get(k, (None, 0))[1] < dop.val:
                            need[k] = (dop.sem, dop.val)
                    for k, (sem, val) in need.items():
                        if waited.get(k, 0) < val:
                            eng.wait_ge(sem, val)
                            waited[k] = val
                    ins = op.fn(eng)
                    if DEBUG_LINES:
                        try:
                            NAMES[ins.ins.name] = op.line
                        except Exception:
                            pass
                    if op.dma:
                        ins.then_inc(op.sem, 16)
                    elif op.signal and op.sem is not None:
                        ins.then_inc(op.sem, 1)
            return body

        for e in ENGINES:
            getattr(block, names[e])(make_body(e))


class Arena:
    def __init__(self, ap, nelem):
        self.ap = ap
        self.n = nelem
        self.off = 0

    def reset(self):
        self.off = 0

    def get(self, shape, dt):
        per = int(np.prod(shape[1:]))
        nb = per * (4 if dt == F32 else 2)
        nb = (nb + 63) // 64 * 64
        ne = nb // 2
        assert self.off + ne <= self.n, ("arena overflow", self.off, ne, self.n)
        v = self.ap[:, self.off:self.off + (per * 2 if dt == F32 else per)]
        self.off += ne
        if dt == F32:
            v = v.bitcast(F32)
        if len(shape) == 3:
            v = v.rearrange("p (a b) -> p a b", a=shape[1])
        return v


def v3(ap, a):
    return ap.rearrange("p (a b) -> p a b", a=a)


def flat(ap):
    return ap.rearrange("p a b -> p (a b)")


class Ctx:
    pass


def build(NT, L):
    NTT = NT // 512
    NCH = NT // 128
    nc = bass.Bass("TRN2", target_bir_lowering=False)
    P = Prog()
    C = Ctx()
    dt_in = lambda name, shape, dt=F32: nc.dram_tensor(name, shape, dt, kind="ExternalInput").ap()
    x_in = dt_in("x", [NT, D])
    w_in = dt_in("w_in", [L, D, INW])
    w_out = dt_in("w_out", [L, D, D])
    w_up = dt_in("w_up", [L, D, DFF])
    w_down = dt_in("w_down", [L, DFF, D])
    d_small = dt_in("small", [L, 128, NS])
    d_wst = dt_in("wst", [L, 128, 1024])
    d_ctab = dt_in("ctab", [128, NCT])
    d_rope = dt_in("rope", [NT, 512])
    d_gf = dt_in("gf", [128, D])
    y_out = nc.dram_tensor("y", [NT, D], F32, kind="ExternalOutput").ap()
    scr = lambda name, shape, dt: nc.dram_tensor(name, shape, dt, kind="Internal").ap()
    wb_in = scr("wb_in", [L, D, INW], BF16)
    wb_out = scr("wb_out", [L, D, D], BF16)
    wb_up = scr("wb_up", [L, D, DFF], BF16)
    wb_down = scr("wb_down", [L, DFF, D], BF16)
    pj_tm = scr("pj_tm", [NT, TMW], BF16)
    pj_fm = scr("pj_fm", [16, 128, NT], BF16)
    d_S = scr("d_S", [2, NCH, 128, 1024], BF16)
    mixT = scr("mixT", [32, 128, NT], BF16)
    XB = scr("XB", [NT, D], F32)

    with ExitStack() as es:
        ARN = 86528
        arena_t = es.enter_context(nc.sbuf_tensor("arena", [128, ARN], BF16))
        A = Arena(arena_t, ARN)
        ctab = es.enter_context(nc.sbuf_tensor("ctab_s", [128, NCT], F32))
        small = es.enter_context(nc.sbuf_tensor("small_s", [128, NS], F32))
        cb = es.enter_context(nc.sbuf_tensor("cb", [128, 5 * 128 + 1024], BF16))
        wst_bf = es.enter_context(nc.sbuf_tensor("wst_bf", [128, 1024], BF16))
        lc = es.enter_context(nc.sbuf_tensor("lc", [128, 64], F32))
        dmat = es.enter_context(nc.sbuf_tensor("dmat", [128, 1024], F32))
        wqf = es.enter_context(nc.sbuf_tensor("wqf", [128, 1024], F32))
        wqb = es.enter_context(nc.sbuf_tensor("wqb", [128, 1024], F32))
        st = es.enter_context(nc.sbuf_tensor("st", [128, 32], F32))
        banks = [es.enter_context(nc.psum_tensor("ps%d" % i, [128, 512], F32)) for i in range(8)]
        IDENT = cb[:, 0:128]
        ONES = cb[:, 128:256]
        MPREV4 = cb[:, 256:768]
        MNEXT4 = cb[:, 768:1280]
        EPSC = ctab[:, C_EPS:C_EPS + 1]
        LG, DEC, SE, WF, WB = lc[:, 0:16], lc[:, 16:32], lc[:, 32:48], lc[:, 48:56], lc[:, 56:64]

        def psb(b):
            return banks[b][:]

        def psbf(b):
            return banks[b][:].bitcast(BF16)

        wbtok = {}

        def cast_list(l, names):
            out = []
            if "in" in names:
                for cg in range(18):
                    out.append((wb_in[l, :, cg * 512:(cg + 1) * 512], w_in[l, :, cg * 512:(cg + 1) * 512], ("wb", l, "in", cg)))
            if "out" in names:
                for cg in range(8):
                    out.append((wb_out[l, :, cg * 512:(cg + 1) * 512], w_out[l, :, cg * 512:(cg + 1) * 512], ("wb", l, "out", cg)))
            if "up" in names:
                for cg in range(32):
                    out.append((wb_up[l, :, cg * 512:(cg + 1) * 512], w_up[l, :, cg * 512:(cg + 1) * 512], ("wb", l, "up", cg)))
            if "down" in names:
                for rp in range(8):
                    out.append((wb_down[l, rp * 2048:(rp + 1) * 2048, :], w_down[l, rp * 2048:(rp + 1) * 2048, :], ("wb", l, "down", rp)))
            return out

        def emit_casts(items, after=()):
            for dst, src, tok in items:
                wbtok[tok] = P.dma("pool", dst, src, r=list(after), w=[tok])

        emit_casts(cast_list(0, ["in"]))

        def restore_wb():
            for k, v in wbtok.items():
                P.tw[k] = v

        P.dma("sp", ctab[:], d_ctab, w=["ctab"])
        P.cp("dve", IDENT, ctab[:, C_ID:C_ID + 128], ["ctab"], ["ident"])
        P.cp("dve", ONES, ctab[:, C_ONES:C_ONES + 128], ["ctab"], ["ones"])
        for g in range(4):
            P.ts("dve", MPREV4[:, g * 128:(g + 1) * 128], ctab[:, C_MPREV:C_MPREV + 128], 30000.0, -30000.0, ALU.mult, ALU.add, ["ctab"], [("mprev", g)])
            P.ts("dve", MNEXT4[:, g * 128:(g + 1) * 128], ctab[:, C_MNEXT:C_MNEXT + 128], 30000.0, -30000.0, ALU.mult, ALU.add, ["ctab"], [("mnext", g)])
        P.barrier()
        restore_wb()

        def layer_consts(l):
            A.reset()
            tmpw = A.get([128, 1024], F32)
            tmp1 = A.get([128, 128], F32)
            tmp2 = A.get([128, 128], F32)
            P.dma("sp", small[:], d_small[l], w=["small"])
            P.dma("sp", tmpw, d_wst[l], w=["tmpw"])
            P.cp("dve", wst_bf[:], tmpw, ["tmpw"], ["wst"])
            P.act(LG, small[:, S_DEC:S_DEC + 16], AF.Exp, ["small"], ["lg0"])
            P.ts("dve", LG, LG, -1.0, None, ALU.mult, None, ["lg0"], ["lg"])
            P.act(DEC, LG, AF.Exp, ["lg"], ["dec"], scale=128.0)
            P.act(SE, small[:, S_SINK:S_SINK + 16], AF.Exp, ["small"], ["se"])
            P.ts("dve", WF, LG[:, 0:8], ctab[:, C_127MP:C_127MP + 1], None, ALU.mult, None, ["lg"], ["wf0"])
            P.act(WF, WF, AF.Exp, ["wf0"], ["wf"])
            P.ts("dve", WB, LG[:, 8:16], ctab[:, C_P:C_P + 1], None, ALU.mult, None, ["lg"], ["wb0"])
            P.act(WB, WB, AF.Exp, ["wb0"], ["wbw"])
            for h in range(8):
                hs = slice(h * 128, (h + 1) * 128)
                P.act(tmp1, ctab[:, C_PF:C_PF + 128], AF.Exp, ["lg", ("t1",)], [("t1",)], scale=LG[:, h:h + 1])
                P.tt("dve", tmp1, tmp1, ctab[:, C_MF:C_MF + 128], ALU.mult, [("t1",)], [("t1",)])
                P.act(tmp2, ctab[:, C_PB:C_PB + 128], AF.Exp, ["lg", ("t2",)], [("t2",)], scale=LG[:, 8 + h:9 + h])
                P.tt("dve", tmp2, tmp2, ctab[:, C_MB:C_MB + 128], ALU.mult, [("t2",)], [("t2",)])
                P.tt("dve", dmat[:, hs], tmp1, tmp2, ALU.add, [("t1",), ("t2",)], [("dmat", h)])
                P.act(wqf[:, hs], ctab[:, C_IP1:C_IP1 + 128], AF.Exp, ["lg"], [("wqf", h)], scale=LG[:, h:h + 1])
                P.act(wqb[:, hs], ctab[:, C_I128M:C_I128M + 128], AF.Exp, ["lg"], [("wqb", h)], scale=LG[:, 8 + h:9 + h])
            P.barrier()
            restore_wb()

        def alloc_dense(nxs=2):
            A.reset()
            C.H = A.get([128, 32, 512], BF16)
            C.W = [A.get([128, 16, 512], BF16) for _ in range(3)]
            C.XS = [A.get([128, D], F32) for _ in range(nxs)]
            C.XN = A.get([128, D], BF16)
            C.JUNK = A.get([128, 512], BF16)

        def Htok_all():
            return [("H", c, b) for c in range(32) for b in range(4)]

        def norm_tile(src, row0, src_tok, gcol):
            for b in range(4):
                s = P.count("xs", len(C.XS))
                xs = C.XS[s]
                P.dma("sp", xs, src[row0 + b * 128:row0 + (b + 1) * 128, :], r=src_tok, w=[("xs", s)])
                P.memset("dve", st[:, 0:8], 0.0, [("ss",)], [("ss",)])
                for c in range(8):
                    P.act(C.JUNK, xs[:, c * 512:(c + 1) * 512], AF.Square, [("xs", s), ("ss",)], [("ssc", c)],
                          accum_out=st[:, c:c + 1])
                P.add("dve", lambda e: e.tensor_reduce(out=st[:, 8:9], in_=st[:, 0:8], axis=AX.X, op=ALU.add),
                      [("ssc", c) for c in range(8)], [("ss",), ("st8",)])
                P.act(st[:, 9:10], st[:, 8:9], AF.Ln, [("st8",)], [("st9",)], scale=1.0 / D, bias=EPSC)
                P.act(st[:, 10:11], st[:, 9:10], AF.Exp, [("st9",)], [("st10",)], scale=-0.5)
                P.act(C.XN, xs, AF.Copy, [("xs", s), ("st10",), ("xn",)], [("xn",)], scale=st[:, 10:11])
                bs = [P.bank() for _ in range(4)]
                for c in range(32):
                    P.trn(psbf(bs[c // 8])[:, (c % 8) * 128:(c % 8 + 1) * 128], C.XN[:, c * 128:(c + 1) * 128], IDENT,
                          [("xn",), "ident"], [("ps", bs[c // 8])])
                for q in range(4):
                    P.tt("dve", C.H[:, q * 8:(q + 1) * 8, b * 128:(b + 1) * 128], v3(psbf(bs[q]), 8),
                         small[:, gcol + q * 8:gcol + (q + 1) * 8].unsqueeze(2).to_broadcast([128, 8, 128]), ALU.mult,
                         [("ps", bs[q]), "small"], [("H", c, b) for c in range(q * 8, (q + 1) * 8)])

        def wload(wb2d, k0, c0, toks):
            s = P.count("W", 3)
            src = wb2d.rearrange("(kc p) n -> p kc n", p=128)[:, k0:k0 + 16, c0:c0 + 512]
            P.dma("sp", C.W[s], src, r=toks, w=[("W", s)])
            return s

        def dense_group(wb2d, kparts, k0s, c0, toks, mode, src, srcname):
            bs = [P.bank() for _ in range(4)]
            nk = kparts * 16
            for kp in range(kparts):
                s = wload(wb2d, k0s[kp], c0, toks)
                Wt = C.W[s]
                for j in range(4):
                    for kc in range(16):
                        k = kp * 16 + kc
                        if mode == "FM":
                            P.mm(psb(bs[j]), Wt[:, kc, j * 128:(j + 1) * 128], src[:, k, :], k == 0, k == nk - 1,
                                 [("W", s)] + [(srcname, k, b) for b in range(4)], [("ps", bs[j])])
                        else:
                            P.mm(psb(bs[j]), src[:, k, j * 128:(j + 1) * 128], Wt[:, kc, :], k == 0, k == nk - 1,
                                 [("W", s), (srcname, k, j)], [("ps", bs[j])])
            return bs

        def phase1(l, src, src_tok_fn):
            alloc_dense()
            TAB = [A.get([128, 4, 512], F32) for _ in range(2)]
            OTM = [A.get([128, 4, 512], BF16) for _ in range(2)]
            OFM = [A.get([128, 4, 512], BF16) for _ in range(2)]
            RT = [A.get([128, 4, 128], F32) for _ in range(2)]
            for tt in range(NTT):
                tsl = P.count("tab", 2)
                P.dma("sp", TAB[tsl], d_rope[tt * 512:(tt + 1) * 512, :].rearrange("(b p) c -> p b c", p=128), w=[("tab", tsl)])
                norm_tile(src, tt * 512, src_tok_fn(tt), S_G1)
                for cg in range(18):
                    mode = "FM" if cg in CG_FM else "TM"
                    bs = dense_group(wb_in[l], 2, [0, 16], cg * 512, [("wb", l, "in", cg)], mode, C.H, "H")
                    if mode == "FM":
                        so = P.count("ofm", 2)
                        fn = AF.Gelu if cg < 2 else AF.Silu
                        for j in range(4):
                            P.act(OFM[so][:, j, :], psb(bs[j]), fn, [("ps", bs[j]), ("ofm", so)], [("ofmw", so, j)])
                        ch0 = {0: 0, 1: 4, 10: 8, 11: 12}[cg]
                        P.dma("act", pj_fm[ch0:ch0 + 4, :, tt * 512:(tt + 1) * 512].rearrange("c p t -> p c t"), OFM[so],
                              r=[("ofmw", so, j) for j in range(4)], w=[("ofm", so), ("pjfm", tt)])
                        continue
                    so = P.count("otm", 2)
                    ti = CG_TM.index(cg)
                    for j in range(4):
                        dst = OTM[so][:, j, :]
                        rd = [("ps", bs[j]), ("otm", so)]
                        wr = [("otmw", so, j)]
                        if cg in (2, 3):
                            P.act(dst, psb(bs[j]), AF.Gelu, rd, wr)
                        elif cg in (8, 9, 17):
                            if j % 2 == 0:
                                P.cp("act", dst, psb(bs[j]), rd, wr)
                            else:
                                P.cp("dve", dst, psb(bs[j]), rd, wr)
                        else:
                            scaled = cg in (6, 7, 12, 13, 14, 15)
                            tc = 2 if scaled else 0
                            cosb = TAB[tsl][:, j, tc * 128:(tc + 1) * 128].unsqueeze(1).to_broadcast([128, 4, 128])
                            sinb = TAB[tsl][:, j, (tc + 1) * 128:(tc + 2) * 128]
                            sin_lo = sinb[:, 0:64].unsqueeze(1).to_broadcast([128, 4, 64])
                            sin_hi = sinb[:, 64:128].unsqueeze(1).to_broadcast([128, 4, 64])
                            pv = v3(psb(bs[j]), 4)
                            ra, rb = RT[0], RT[1]
                            P.tt("dve", ra, pv, cosb, ALU.mult, [("ps", bs[j]), ("tab", tsl), ("ra",)], [("ra",)])
                            P.tt("dve", rb[:, :, 0:64], pv[:, :, 64:128], sin_lo, ALU.mult,
                                 [("ps", bs[j]), ("tab", tsl), ("rb",)], [("rb0",)])
                            P.tt("dve", rb[:, :, 64:128], pv[:, :, 0:64], sin_hi, ALU.mult,
                                 [("ps", bs[j]), ("tab", tsl), ("rb",)], [("rb1",)])
                            P.tt("dve", v3(dst, 4), ra, rb, ALU.add, rd[1:] + [("ra",), ("rb0",), ("rb1",)], wr + [("ra",), ("rb",)])
                    P.dma("act", pj_tm[tt * 512:(tt + 1) * 512, ti * 512:(ti + 1) * 512].rearrange("(b p) c -> p b c", p=128), OTM[so],
                          r=[("otmw", so, j) for j in range(4)], w=[("otm", so), ("pjtm", tt)])
            P.barrier()
            restore_wb()

        def phase2(l):
            A.reset()
            if l == 0:
                emit_casts(cast_list(0, ["out", "up", "down"]))
            KVT = [[A.get([128, 2048], BF16) for _ in range(2)] for _ in range(2)]
            KW = [A.get([128, 8, 128], BF16) for _ in range(2)]
            SST = [[A.get([128, 8, 128], F32) for _ in range(2)] for _ in range(2)]
            SOUT = [[A.get([128, 1024], BF16) for _ in range(2)] for _ in range(2)]
            for di in range(2):
                P.memset("dve", flat(SST[di][0]), 0.0, [], [("S", di, 0, h) for h in range(8)])
            for step in range(NCH):
                for di in range(2):
                    wv = WF if di == 0 else WB
                    n = step if di == 0 else NCH - 1 - step
                    cb_, nb_ = step % 2, (step + 1) % 2
                    cur, nxs = SST[di][cb_], SST[di][nb_]
                    s = step % 2
                    kv = KVT[di][s]
                    P.dma("sp", kv, pj_tm[n * 128:(n + 1) * 128, RK0:RK0 + 2048], w=[("kvt", di, s)])
                    P.cp("act", SOUT[di][s], flat(cur), [("S", di, cb_, h) for h in range(8)] + [("sout", di, s)], [("soutw", di, s)])
                    P.dma("sp", d_S[di, n], SOUT[di][s], r=[("soutw", di, s)], w=[("sout", di, s), ("dS", di, n)])
                    P.tt("dve", KW[di], v3(kv[:, 0:1024], 8), wv.unsqueeze(2).to_broadcast([128, 8, 128]), ALU.mult,
                         [("kvt", di, s), ("kw", di)], [("kw", di)])
                    bs = [P.bank(), P.bank()]
                    for h in range(8):
                        P.mm(psb(bs[h // 4])[:, (h % 4) * 128:(h % 4 + 1) * 128], KW[di][:, h, :],
                             kv[:, 1024 + h * 128:1024 + (h + 1) * 128], True, True,
                             [("kw", di), ("kvt", di, s)], [("ps", bs[h // 4])])
                    for h in range(8):
                        P.stt("dve", nxs[:, h, :], cur[:, h, :], DEC[:, di * 8 + h:di * 8 + h + 1],
                              psb(bs[h // 4])[:, (h % 4) * 128:(h % 4 + 1) * 128], ALU.mult, ALU.add,
                              [("ps", bs[h // 4]), ("S", di, cb_, h), "dec"], [("S", di, nb_, h)])
            P.barrier()
            restore_wb()
            A.reset()
            TM = [A.get([128, TMW], BF16) for _ in range(2)]
            FMt = [A.get([128, 16, 512], BF16) for _ in range(2)]
            OUT = [A.get([128, 32, 512], BF16) for _ in range(1)]
            SF = [A.get([128, 8, 128], BF16) for _ in range(2)]
            SB = [A.get([128, 8, 128], BF16) for _ in range(2)]
            AKV = [A.get([128, 1024], BF16) for _ in range(4)]
            AKT = [A.get([128, 4, 128], BF16) for _ in range(4)]
            VNF = A.get([128, 1024], F32)
            VN = A.get([128, 1024], BF16)
            J2 = A.get([128, 1024], BF16)
            QT = A.get([128, 8, 128], BF16)
            KT = A.get([128, 8, 128], BF16)
            QWF = A.get([128, 8, 128], BF16)
            QWB = A.get([128, 8, 128], BF16)
            SMT = A.get([128, 8, 128], BF16)
            SQ = A.get([128, 1024], BF16)
            RSTD = A.get([128, 1024], F32)
            TMPF = [A.get([128, 4, 128], F32) for _ in range(2)]
            AQT = A.get([128, 16, 128], BF16)
            PT = [A.get([128, 512], BF16) for _ in range(6)]
            DEN = [A.get([128, 4, 128], F32) for _ in range(2)]
            S_LN = 16

            def prepare(m):
                r = m % 4
                P.dma("sp", AKV[r], pj_tm[m * 128:(m + 1) * 128, AK0:AK0 + 1024], w=[("akv", r)])
                b = P.bank()
                for kh in range(4):
                    P.trn(psbf(b)[:, kh * 128:(kh + 1) * 128], AKV[r][:, kh * 128:(kh + 1) * 128], IDENT,
                          [("akv", r), "ident"], [("ps", b)])
                P.cp("act", flat(AKT[r]), psbf(b)[:, 0:512], [("ps", b)], [("akt", r)])

            prepare(0)
            for n in range(NCH):
                if n + 1 < NCH:
                    prepare(n + 1)
                g4 = n % 4
                ncs = slice(g4 * 128, (g4 + 1) * 128)
                if g4 == 0:
                    fs = P.count("fmt", 2)
                    os_ = P.count("out", 1)
                    P.dma("sp", FMt[fs], pj_fm[:, :, n * 128:n * 128 + 512].rearrange("c p t -> p c t"), w=[("fmt", fs)])
                ts_ = P.count("tm", 2)
                TMs = TM[ts_]
                P.dma("sp", TMs, pj_tm[n * 128:(n + 1) * 128, :], w=[("tm", ts_)])
                ss_ = P.count("sfb", 2)
                P.dma("sp", flat(SF[ss_]), d_S[0, n], w=[("sf", ss_)])
                P.dma("sp", flat(SB[ss_]), d_S[1, n], w=[("sb", ss_)])
                outw = [("outw", os_, n % 4, i) for i in range(8)]
                v = TMs[:, TV0:TV0 + 1024]
                c0 = S_LN
                P.memset("dve", st[:, c0:c0 + 2], 0.0, [("ln",)], [("ln0",)])
                P.act(J2, v, AF.Copy, [("tm", ts_), ("ln0",)], [("lna",)], accum_out=st[:, c0:c0 + 1])
                P.act(J2, v, AF.Square, [("tm", ts_), ("ln0",)], [("lnb",)], accum_out=st[:, c0 + 1:c0 + 2])
                P.ts("dve", st[:, c0 + 2:c0 + 4], st[:, c0:c0 + 2], 1.0 / 1024, None, ALU.mult, None, [("lna",), ("lnb",)], [("ln1",)])
                P.tt("dve", st[:, c0 + 4:c0 + 5], st[:, c0 + 2:c0 + 3], st[:, c0 + 2:c0 + 3], ALU.mult, [("ln1",)], [("ln2",)])
                P.tt("dve", st[:, c0 + 5:c0 + 6], st[:, c0 + 3:c0 + 4], st[:, c0 + 4:c0 + 5], ALU.subtract, [("ln1",), ("ln2",)], [("ln3",)])
                P.act(st[:, c0 + 8:c0 + 9], st[:, c0 + 5:c0 + 6], AF.Ln, [("ln3",)], [("ln3b",)], bias=EPSC)
                P.act(st[:, c0 + 6:c0 + 7], st[:, c0 + 8:c0 + 9], AF.Exp, [("ln3b",)], [("ln4",)], scale=-0.5)
                P.stt("dve", st[:, c0 + 7:c0 + 8], st[:, c0 + 2:c0 + 3], -1.0, st[:, c0 + 6:c0 + 7], ALU.mult, ALU.mult,
                      [("ln1",), ("ln4",)], [("ln5",)])
                P.act(VNF, v, AF.Identity, [("tm", ts_), ("ln4",), ("ln5",), ("vnf",)], [("vnf",)],
                      scale=st[:, c0 + 6:c0 + 7], bias=st[:, c0 + 7:c0 + 8])
                P.tt("dve", VNF, VNF, small[:, S_LNG:S_LNG + 1024], ALU.mult, [("vnf",), "small"], [("vnf",), ("ln",)])
                P.tt("dve", VN, VNF, small[:, S_LNB:S_LNB + 1024], ALU.add, [("vnf",), "small", ("vn",)], [("vn",)])
                bs = [P.bank(), P.bank()]
                for g in range(8):
                    P.mm(psb(bs[g // 4])[:, (g % 4) * 128:(g % 4 + 1) * 128], VN[:, g * 128:(g + 1) * 128],
                         wst_bf[:, g * 128:(g + 1) * 128], True, True, [("vn",), "wst"], [("ps", bs[g // 4])])
                for gb in range(2):
                    tf = TMPF[gb]
                    P.tt("dve", flat(tf), psb(bs[gb]), small[:, S_BSB + gb * 512:S_BSB + (gb + 1) * 512], ALU.add,
                         [("ps", bs[gb]), "small", ("tmpf", gb)], [("tmpf", gb)])
                    P.tt("dve", OUT[os_][:, gb * 4:(gb + 1) * 4, ncs], tf, FMt[fs][:, gb * 4:(gb + 1) * 4, ncs], ALU.mult,
                         [("tmpf", gb), ("fmt", fs), ("out", os_)], [outw[gb]])
                bq, bk = P.bank(), P.bank()
                for h in range(8):
                    P.trn(psbf(bq)[:, h * 128:(h + 1) * 128], TMs[:, RQ0 + h * 128:RQ0 + (h + 1) * 128], IDENT,
                          [("tm", ts_), "ident"], [("ps", bq)])
                for h in range(8):
                    P.trn(psbf(bk)[:, h * 128:(h + 1) * 128], TMs[:, RK0 + h * 128:RK0 + (h + 1) * 128], IDENT,
                          [("tm", ts_), "ident"], [("ps", bk)])
                P.cp("act", flat(QT), psbf(bq), [("ps", bq), ("qt",)], [("qt",)])
                P.cp("dve", flat(KT), psbf(bk), [("ps", bk), ("kt",)], [("kt",)])
                P.tt("dve", flat(QWF), flat(QT), wqf[:], ALU.mult, [("qt",), ("qwf",)] + [("wqf", h) for h in range(8)], [("qwf",)])
                P.tt("dve", flat(QWB), flat(QT), wqb[:], ALU.mult, [("qt",), ("qwb",)] + [("wqb", h) for h in range(8)], [("qwb",)])
                bsc = [P.bank(), P.bank()]
                for h in range(8):
                    P.mm(psb(bsc[h // 4])[:, (h % 4) * 128:(h % 4 + 1) * 128], KT[:, h, :], QT[:, h, :], True, True,
                         [("kt",), ("qt",)], [("ps", bsc[h // 4])])
                for hb in range(2):
                    P.tt("dve", flat(SMT[:, hb * 4:(hb + 1) * 4, :]), psb(bsc[hb]), dmat[:, hb * 512:(hb + 1) * 512], ALU.mult,
                         [("ps", bsc[hb]), ("smt", hb)] + [("dmat", h) for h in range(8)], [("smt", hb)])
                br = [P.bank(), P.bank()]
                for h in range(8):
                    o = psb(br[h // 4])[:, (h % 4) * 128:(h % 4 + 1) * 128]
                    wr = [("ps", br[h // 4])]
                    P.mm(o, TMs[:, RV0 + h * 128:RV0 + (h + 1) * 128], SMT[:, h, :], True, False, [("tm", ts_), ("smt", h // 4)], wr)
                    P.mm(o, SF[ss_][:, h, :], QWF[:, h, :], False, False, [("sf", ss_), ("qwf",)], wr)
                    P.mm(o, SB[ss_][:, h, :], QWB[:, h, :], False, True, [("sb", ss_), ("qwb",)], wr)
                bm = [P.bank(), P.bank()]
                for hb in range(2):
                    hs = slice(hb * 512, (hb + 1) * 512)
                    P.act(SQ[:, hs], psb(br[hb]), AF.Square, [("ps", br[hb]), ("sq", hb)], [("sq", hb)])
                    P.mm(psb(bm[hb]), ONES, SQ[:, hs], True, True, ["ones", ("sq", hb)], [("ps", bm[hb])])
                    P.act(RSTD[:, hs], psb(bm[hb]), AF.Ln, [("ps", bm[hb]), ("rstd", hb)], [("rstd", hb)], scale=1.0 / 128, bias=EPSC)
                    P.act(RSTD[:, hs], RSTD[:, hs], AF.Exp, [("rstd", hb)], [("rstd", hb)], scale=-0.5)
                    tf = TMPF[hb]
                    P.tt("dve", flat(tf), psb(br[hb]), RSTD[:, hs], ALU.mult, [("ps", br[hb]), ("rstd", hb), ("tmpf", hb)], [("tmpf", hb)])
                    P.tt("dve", OUT[os_][:, 8 + hb * 4:8 + (hb + 1) * 4, ncs], tf, FMt[fs][:, 8 + hb * 4:8 + (hb + 1) * 4, ncs], ALU.mult,
                         [("tmpf", hb), ("fmt", fs), ("out", os_)], [outw[2 + hb]])
                ba = [P.bank(), P.bank()]
                for hq in range(16):
                    P.trn(psbf(ba[hq // 8])[:, (hq % 8) * 128:(hq % 8 + 1) * 128], TMs[:, AQ0 + hq * 128:AQ0 + (hq + 1) * 128], IDENT,
                          [("tm", ts_), "ident"], [("ps", ba[hq // 8])])
                P.cp("act", flat(AQT[:, 0:8, :]), psbf(ba[0]), [("ps", ba[0]), ("aqt", 0)], [("aqt", 0)])
                P.cp("dve", flat(AQT[:, 8:16, :]), psbf(ba[1]), [("ps", ba[1]), ("aqt", 1)], [("aqt", 1)])
                for kh in range(4):
                    ms = [m for m in (n - 1, n, n + 1) if 0 <= m < NCH]
                    pts = []
                    for m in ms:
                        b = P.bank()
                        pi = P.count("pt", 6)
                        pts.append(pi)
                        P.mm(psb(b), AKT[m % 4][:, kh, :], flat(AQT[:, kh * 4:(kh + 1) * 4, :]), True, m == n,
                             [("akt", m % 4), ("aqt", kh // 2)], [("ps", b)])
                        if m != n:
                            mk = MPREV4 if m < n else MNEXT4
                            mt = [("mprev" if m < n else "mnext", g) for g in range(4)]
                            P.mm(psb(b), IDENT, mk, False, True, ["ident"] + mt, [("ps", b)])
                        P.act(PT[pi], psb(b), AF.Exp, [("ps", b), ("pt", pi)], [("pt", pi)])
                    bo, bd = P.bank(), P.bank()
                    for i, m in enumerate(ms):
                        P.mm(psb(bo), AKV[m % 4][:, 512 + kh * 128:512 + (kh + 1) * 128], PT[pts[i]], i == 0, i == len(ms) - 1,
                             [("akv", m % 4), ("pt", pts[i])], [("ps", bo)])
                    for i, m in enumerate(ms):
                        P.mm(psb(bd), ONES, PT[pts[i]], i == 0, i == len(ms) - 1, ["ones", ("pt", pts[i])], [("ps", bd)])
                    dn = DEN[kh % 2]
                    for g in range(4):
                        P.act(dn[:, g, :], psb(bd)[:, g * 128:(g + 1) * 128], AF.Ln, [("ps", bd), "se", ("den", kh % 2)], [("dena", kh % 2, g)],
                              bias=SE[:, kh * 4 + g:kh * 4 + g + 1])
                    P.act(flat(dn), flat(dn), AF.Exp, [("dena", kh % 2, g) for g in range(4)], [("den", kh % 2)], scale=-1.0)
                    P.tt("dve", OUT[os_][:, 16 + kh * 4:16 + (kh + 1) * 4, ncs], v3(psb(bo), 4), dn, ALU.mult,
                         [("ps", bo), ("den", kh % 2), ("out", os_)], [outw[4 + kh]])
                if g4 == 3:
                    P.dma("sp", mixT[:, :, (n - 3) * 128:(n + 1) * 128].rearrange("c p t -> p c t"), OUT[os_],
                          r=[("outw", os_, q, i) for q in range(4) for i in range(8)], w=[("out", os_), ("mix", n // 4)])
            P.barrier()
            restore_wb()

        def phase3(l, res_is_x):
            alloc_dense(1)
            AT = A.get([128, 32, 512], BF16)
            XR = [A.get([128, 4, 512], F32) for _ in range(2)]
            RL = [A.get([128, 512], F32) for _ in range(2)]

            def resid_evac(bs, src, src_tok, tt, cg):
                s = P.count("xr", 2)
                view = lambda t: t[tt * 512:(tt + 1) * 512, cg * 512:(cg + 1) * 512].rearrange("(b p) c -> p b c", p=128)
                P.dma("sp", XR[s], view(src), r=src_tok, w=[("xr", s)])
                for j in range(4):
                    P.tt("dve", XR[s][:, j, :], psb(bs[j]), XR[s][:, j, :], ALU.add, [("ps", bs[j]), ("xr", s)], [("xrw", s, j)])
                P.dma("act", view(XB), XR[s], r=[("xrw", s, j) for j in range(4)], w=[("xr", s), ("XB", tt, cg)])

            nxt = cast_list(l + 1, ["in", "out", "up", "down"]) if l + 1 < L else []
            per = (len(nxt) + NTT - 1) // NTT
            for tt in range(NTT):
                emit_casts(nxt[tt * per:(tt + 1) * per], after=[("XB", tt - 1, 7)] if tt > 0 else [])
                P.dma("sp", C.H, mixT[:, :, tt * 512:(tt + 1) * 512].rearrange("c p t -> p c t"), w=Htok_all())
                for cg in range(8):
                    bs = dense_group(wb_out[l], 2, [0, 16], cg * 512, [("wb", l, "out", cg)], "TM", C.H, "H")
                    if res_is_x:
                        resid_evac(bs, x_in, [], tt, cg)
                    else:
                        resid_evac(bs, XB, [("XB", tt, cg)], tt, cg)
                norm_tile(XB, tt * 512, [("XB", tt, cg) for cg in range(8)], S_G2)
                for q in range(4):
                    for cg in range(8):
                        bs = dense_group(wb_up[l], 2, [0, 16], q * 4096 + cg * 512, [("wb", l, "up", q * 8 + cg)], "FM", C.H, "H")
                        for j in range(4):
                            rs = P.count("relu", 2)
                            P.act(RL[rs], psb(bs[j]), AF.Relu, [("ps", bs[j]), ("rl", rs)], [("rl", rs)])
                            P.tt("dve", AT[:, cg * 4 + j, :], RL[rs], RL[rs], ALU.mult,
                                 [("rl", rs)], [("AT", cg * 4 + j, b) for b in range(4)])
                    for cg in range(8):
                        bs = dense_group(wb_down[l], 2, [q * 32, q * 32 + 16], cg * 512,
                                         [("wb", l, "down", q * 2), ("wb", l, "down", q * 2 + 1)], "TM", AT, "AT")
                        resid_evac(bs, XB, [("XB", tt, cg)], tt, cg)
            P.barrier()
            restore_wb()

        def final_norm():
            A.reset()
            GF = A.get([128, D], F32)
            XS = [A.get([128, D], F32) for _ in range(2)]
            JK = A.get([128, 512], BF16)
            P.dma("sp", GF, d_gf, w=["gf"])
            for blk in range(NCH):
                s = P.count("fxs", 2)
                xs = XS[s]
                P.dma("sp", xs, XB[blk * 128:(blk + 1) * 128, :], w=[("fxs", s)])
                P.memset("dve", st[:, 0:8], 0.0, [("ss",)], [("ss",)])
                for c in range(8):
                    P.act(JK, xs[:, c * 512:(c + 1) * 512], AF.Square, [("fxs", s), ("ss",)], [("ssc", c)], accum_out=st[:, c:c + 1])
                P.add("dve", lambda e: e.tensor_reduce(out=st[:, 8:9], in_=st[:, 0:8], axis=AX.X, op=ALU.add),
                      [("ssc", c) for c in range(8)], [("ss",), ("st8",)])
                P.act(st[:, 9:10], st[:, 8:9], AF.Ln, [("st8",)], [("st9",)], scale=1.0 / D, bias=EPSC)
                P.act(st[:, 10:11], st[:, 9:10], AF.Exp, [("st9",)], [("st10",)], scale=-0.5)
                P.act(xs, xs, AF.Copy, [("fxs", s), ("st10",)], [("fxs", s)], scale=st[:, 10:11])
                P.tt("dve", xs, xs, GF, ALU.mult, [("fxs", s), "gf"], [("fxs", s)])
                P.dma("act", y_out[blk * 128:(blk + 1) * 128, :], xs, r=[("fxs", s)], w=[("fxs", s)])
            P.barrier()

        for l in range(L):
            layer_consts(l)
            if l == 0:
                phase1(l, x_in, lambda tt: [])
            else:
                phase1(l, XB, lambda tt: [])
            phase2(l)
            phase3(l, res_is_x=(l == 0))
        final_norm()
        for e in ENGINES:
            P.add(e, lambda eng: eng.nop(), [], [])
        P.emit(nc, es)
    return nc


def _ctab():
    p = np.arange(128, dtype=np.float32)[:, None]
    i = np.arange(128, dtype=np.float32)[None, :]
    t = np.zeros((128, NCT), np.float32)
    t[:, C_PF:C_PF + 128] = np.maximum(i - p, 0)
    t[:, C_MF:C_MF + 128] = (i >= p)
    t[:, C_PB:C_PB + 128] = np.maximum(p - i, 0)
    t[:, C_MB:C_MB + 128] = (p > i)
    t[:, C_IP1:C_IP1 + 128] = i + 1
    t[:, C_I128M:C_I128M + 128] = 128 - i
    t[:, C_C128:C_C128 + 128] = 128.0
    t[:, C_ID:C_ID + 128] = (i == p)
    t[:, C_MPREV:C_MPREV + 128] = (p >= i)
    t[:, C_MNEXT:C_MNEXT + 128] = (p <= i)
    t[:, C_ONES:C_ONES + 128] = 1.0
    t[:, C_127MP] = 127 - p[:, 0]
    t[:, C_P] = p[:, 0]
    t[:, C_EPS] = EPS
    return t


def _rope(NT):
    pos = np.arange(NT, dtype=np.float32)
    inv = (np.float32(10000.0) ** (-np.arange(0, 128, 2, dtype=np.float32) / np.float32(128))).astype(np.float32)
    ang = (pos[:, None] * inv[None, :]).astype(np.float32)
    ang = np.concatenate([ang, ang], -1)
    cos = np.cos(ang).astype(np.float32)
    sin = np.sin(ang).astype(np.float32)
    sins = np.concatenate([-sin[:, :64], sin[:, 64:]], -1)
    sc = np.float32(128 ** -0.5)
    return np.ascontiguousarray(np.concatenate([cos, sins, cos * sc, sins * sc], -1).astype(np.float32))


def _host_params(inp, L):
    small = np.zeros((L, 128, NS), np.float32)
    wst = np.zeros((L, 128, 1024), np.float32)
    for l in range(L):
        small[l, :, S_G1:S_G1 + 32] = inp["ln_mix_g"][l].reshape(32, 128).T
        small[l, :, S_G2:S_G2 + 32] = inp["ln_mlp_g"][l].reshape(32, 128).T
        small[l, :, S_DEC:S_DEC + 16] = inp["ret_log_decay"][l].reshape(1, 16)
        small[l, :, S_SINK:S_SINK + 16] = inp["attn_sink"][l].reshape(1, 16)
        small[l, :, S_LNG:S_LNG + 1024] = inp["sgu_ln_g"][l][None, :]
        small[l, :, S_LNB:S_LNB + 1024] = inp["sgu_ln_b"][l][None, :]
        small[l, :, S_BSB:S_BSB + 1024] = inp["sgu_b"][l].reshape(1, 1024)
        wst[l] = np.transpose(inp["sgu_w"][l], (2, 0, 1)).reshape(128, 1024)
    gf = np.ascontiguousarray(np.broadcast_to(inp["final_norm_g"][None, :], (128, D))).astype(np.float32)
    return small, wst, gf


_NC_CACHE = {}


def run(inp, NT, L, seqs, n_cores):
    key = (NT, L)
    if key not in _NC_CACHE:
        _NC_CACHE[key] = build(NT, L)
    nc = _NC_CACHE[key]
    small, wst, gf = _host_params(inp, L)
    ctab = _ctab()
    rope = _rope(NT)
    f = lambda a: np.ascontiguousarray(np.asarray(a, dtype=np.float32))
    shared = {"w_in": f(inp["w_in"][:L]), "w_out": f(inp["w_out"][:L]), "w_up": f(inp["w_up"][:L]),
              "w_down": f(inp["w_down"][:L]), "small": small, "wst": wst, "ctab": ctab, "rope": rope, "gf": gf}
    zero = None
    in_maps = []
    for c in range(n_cores):
        m = dict(shared)
        if seqs[c] is None:
            if zero is None:
                zero = {k: np.zeros_like(shared[k]) for k in ("w_in", "w_out", "w_up", "w_down")}
                zero["x"] = np.zeros((NT, D), np.float32)
            m.update(zero)
        else:
            m["x"] = f(inp["x"][seqs[c]])
        in_maps.append(m)
    res = run_bass_kernel_spmd(nc, in_maps, core_ids=list(range(n_cores)))
    return [np.asarray(r["y"]) if seqs[c] is not None else None for c, r in enumerate(res.results)]


def kernel(x, ln_mix_g, w_in, sgu_ln_g, sgu_ln_b, sgu_w, sgu_b, ret_log_decay, attn_sink, w_out,
           ln_mlp_g, w_up, w_down, final_norm_g):
    inp = dict(x=np.asarray(x), ln_mix_g=np.asarray(ln_mix_g), w_in=np.asarray(w_in), sgu_ln_g=np.asarray(sgu_ln_g),
               sgu_ln_b=np.asarray(sgu_ln_b), sgu_w=np.asarray(sgu_w), sgu_b=np.asarray(sgu_b),
               ret_log_decay=np.asarray(ret_log_decay), attn_sink=np.asarray(attn_sink), w_out=np.asarray(w_out),
               ln_mlp_g=np.asarray(ln_mlp_g), w_up=np.asarray(w_up), w_down=np.asarray(w_down),
               final_norm_g=np.asarray(final_norm_g))
    B, S, _ = inp["x"].shape
    active = [0, 1, 4, 5]
    seqs = [None] * 8
    for b in range(B):
        seqs[active[b]] = b
    ys = run(inp, S, 2, seqs, 8)
    return np.stack([ys[active[b]] for b in range(B)], 0).astype(np.float32)
```

```python
import numpy as np
from contextlib import ExitStack
import concourse.bass as bass
import concourse.mybir as mybir
from concourse.bass_utils import run_bass_kernel_spmd

F32 = mybir.dt.float32
BF16 = mybir.dt.bfloat16
AF = mybir.ActivationFunctionType
ALU = mybir.AluOpType
AX = mybir.AxisListType

D = 4096
DFF = 16384
INW = 9216
EPS = 1e-5
CG_TM = [2, 3, 4, 5, 6, 7, 8, 9, 12, 13, 14, 15, 16, 17]
CG_FM = [0, 1, 10, 11]
TMW = 7168
TV0, RQ0, RK0, RV0, AQ0, AK0, AV0 = 0, 1024, 2048, 3072, 4096, 6144, 6656
C_PF, C_MF, C_PB, C_MB, C_IP1, C_I128M, C_C128, C_ID, C_MPREV, C_MNEXT, C_ONES = [i * 128 for i in range(11)]
C_127MP = 11 * 128
C_P = 11 * 128 + 1
C_EPS = 11 * 128 + 2
NCT = 11 * 128 + 3
S_G1, S_G2, S_DEC, S_SINK, S_LNG, S_LNB, S_BSB = 0, 32, 64, 80, 96, 96 + 1024, 96 + 2048
NS = 96 + 3072

ENGINES = ["pe", "act", "dve", "pool", "sp"]
DEBUG_LINES = False
NAMES = {}


class Op:
    __slots__ = ("eng", "fn", "deps", "dma", "signal", "sem", "val", "line")

    def __init__(self, eng, fn, deps, dma):
        self.eng, self.fn, self.deps, self.dma = eng, fn, deps, dma
        self.signal = False
        self.sem = None
        self.val = 0


class Prog:
    KDMA = 8

    def __init__(self):
        self.ops = []
        self.tw = {}
        self.tr = {}
        self.pending = {}
        self.last = {}
        self.dmas = {"sp": [], "pool": [], "act": []}
        self.nbank = 0
        self.ctr = {}

    def count(self, name, mod):
        v = self.ctr.get(name, 0)
        self.ctr[name] = v + 1
        return v % mod

    def bank(self):
        b = self.nbank % 8
        self.nbank += 1
        return b

    def add(self, eng, fn, r=(), w=(), dma=False, extra=()):
        idx = len(self.ops)
        deps = set(extra)
        for t in r:
            x = self.tw.get(t)
            if x is not None:
                deps.add(x)
        for t in w:
            x = self.tw.get(t)
            if x is not None:
                deps.add(x)
            rd = self.tr.get(t)
            if rd:
                deps.update(rd[0].values())
                deps.update(rd[1])
        if eng in self.pending:
            deps |= self.pending.pop(eng)
        for d in deps:
            self.ops[d].signal = True
        self.ops.append(Op(eng, fn, deps, dma))
        if DEBUG_LINES:
            import sys as _s
            f = _s._getframe(1)
            while f.f_code.co_name in ("add", "dma", "mm", "trn", "act", "tt", "ts", "stt", "cp", "memset"):
                f = f.f_back
            self.ops[-1].line = f.f_lineno
        for t in w:
            self.tw[t] = idx
            self.tr[t] = [{}, []]
        for t in r:
            rd = self.tr.setdefault(t, [{}, []])
            if dma:
                rd[1].append(idx)
            else:
                rd[0][eng] = idx
        self.last[eng] = idx
        if dma:
            self.dmas[eng].append(idx)
        return idx

    def barrier(self):
        deps = set(v for k, v in self.last.items() if k != "pool")
        for q in ("sp", "act"):
            deps.update(self.dmas[q][-self.KDMA:])
        for e in ENGINES:
            self.pending[e] = set(deps) | self.pending.get(e, set())
        self.tw = {}
        self.tr = {}

    def dma(self, q, out, in_, r=(), w=(), extra=()):
        return self.add(q, lambda e: e.dma_start(out=out, in_=in_), r, w, dma=True, extra=extra)

    def mm(self, out, lhsT, rhs, start, stop, r, w):
        return self.add("pe", lambda e: e.matmul(out, lhsT=lhsT, rhs=rhs, start=start, stop=stop), r, w)

    def trn(self, out, in_, ident, r, w):
        return self.add("pe", lambda e: e.transpose(out=out, in_=in_, identity=ident), r, w)

    def act(self, out, in_, func, r, w, **kw):
        return self.add("act", lambda e: e.activation(out=out, in_=in_, func=func, **kw), r, w)

    def tt(self, eng, out, in0, in1, op, r, w):
        return self.add(eng, lambda e: e.tensor_tensor(out=out, in0=in0, in1=in1, op=op), r, w)

    def ts(self, eng, out, in0, s1, s2, op0, op1, r, w):
        if op1 is None:
            return self.add(eng, lambda e: e.tensor_scalar(out=out, in0=in0, scalar1=s1, scalar2=None, op0=op0), r, w)
        return self.add(eng, lambda e: e.tensor_scalar(out=out, in0=in0, scalar1=s1, scalar2=s2, op0=op0, op1=op1), r, w)

    def stt(self, eng, out, in0, scalar, in1, op0, op1, r, w):
        return self.add(eng, lambda e: e.scalar_tensor_tensor(out=out, in0=in0, scalar=scalar, in1=in1, op0=op0, op1=op1), r, w)

    def cp(self, eng, out, in_, r, w):
        if eng == "act":
            return self.add("act", lambda e: e.activation(out=out, in_=in_, func=AF.Copy), r, w)
        return self.add(eng, lambda e: e.tensor_copy(out=out, in_=in_), r, w)

    def memset(self, eng, ap, val, r, w):
        return self.add(eng, lambda e: e.memset(ap, val), r, w)

    def emit(self, nc, es):
        esem = {e: es.enter_context(nc.semaphore("s_" + e)) for e in ["pe", "act", "dve", "pool"]}
        dsem = {q: [es.enter_context(nc.semaphore("d_%s%d" % (q, i))) for i in range(self.KDMA)] for q in self.dmas}
        cnt = {e: 0 for e in esem}
        dcnt = {q: 0 for q in self.dmas}
        ops = self.ops
        for op in ops:
            if op.dma:
                q = op.eng
                j = dcnt[q]
                op.sem = dsem[q][j % self.KDMA]
                op.val = 16 * (j // self.KDMA + 1)
                if j >= self.KDMA:
                    op.deps.add(self.dmas[q][j - self.KDMA])
                dcnt[q] = j + 1
            elif op.signal and op.eng in esem:
                cnt[op.eng] += 1
                op.sem = esem[op.eng]
                op.val = cnt[op.eng]
        streams = {e: [] for e in ENGINES}
        for op in ops:
            streams[op.eng].append(op)
        block = es.enter_context(nc.Block())
        names = {"pe": "tensor", "act": "scalar", "dve": "vector", "pool": "gpsimd", "sp": "sync"}

        def make_body(ename):
            def body(eng):
                waited = {}
                for op in streams[ename]:
                    need = {}
                    for d in op.deps:
                        dop = ops[d]
                        if dop.sem is None:
                            continue
                        if ename == "pe" and dop.eng == "pe":
                            continue
                        k = dop.sem
                        if need.get(k, (None, 0))[1] < dop.val:
                            need[k] = (dop.sem, dop.val)
                    for k, (sem, val) in need.items():
                        if waited.get(k, 0) < val:
                            eng.wait_ge(sem, val)
                            waited[k] = val
                    ins = op.fn(eng)
                    if DEBUG_LINES:
                        try:
                            NAMES[ins.ins.name] = op.line
                        except Exception:
                            pass
                    if op.dma:
                        ins.then_inc(op.sem, 16)
                    elif op.signal and op.sem is not None:
                        ins.then_inc(op.sem, 1)
            return body

        for e in ENGINES:
            getattr(block, names[e])(make_body(e))


class Arena:
    def __init__(self, ap, nelem):
        self.ap = ap
        self.n = nelem
        self.off = 0

    def reset(self):
        self.off = 0

    def get(self, shape, dt):
        per = int(np.prod(shape[1:]))
        nb = per * (4 if dt == F32 else 2)
        nb = (nb + 63) // 64 * 64
        ne = nb // 2
        assert self.off + ne <= self.n, ("arena overflow", self.off, ne, self.n)
        v = self.ap[:, self.off:self.off + (per * 2 if dt == F32 else per)]
        self.off += ne
        if dt == F32:
            v = v.bitcast(F32)
        if len(shape) == 3:
            v = v.rearrange("p (a b) -> p a b", a=shape[1])
        return v


def v3(ap, a):
    return ap.rearrange("p (a b) -> p a b", a=a)


def flat(ap):
    return ap.rearrange("p a b -> p (a b)")


class Ctx:
    pass


def build(NT, L):
    NTT = NT // 512
    NCH = NT // 128
    nc = bass.Bass("TRN2", target_bir_lowering=False)
    P = Prog()
    C = Ctx()
    dt_in = lambda name, shape, dt=F32: nc.dram_tensor(name, shape, dt, kind="ExternalInput").ap()
    x_in = dt_in("x", [NT, D])
    w_in = dt_in("w_in", [L, D, INW])
    w_out = dt_in("w_out", [L, D, D])
    w_up = dt_in("w_up", [L, D, DFF])
    w_down = dt_in("w_down", [L, DFF, D])
    d_small = dt_in("small", [L, 128, NS])
    d_wst = dt_in("wst", [L, 128, 1024])
    d_ctab = dt_in("ctab", [128, NCT])
    d_rope = dt_in("rope", [NT, 512])
    d_gf = dt_in("gf", [128, D])
    y_out = nc.dram_tensor("y", [NT, D], F32, kind="ExternalOutput").ap()
    scr = lambda name, shape, dt: nc.dram_tensor(name, shape, dt, kind="Internal").ap()
    wb_in = scr("wb_in", [L, D, INW], BF16)
    wb_out = scr("wb_out", [L, D, D], BF16)
    wb_up = scr("wb_up", [L, D, DFF], BF16)
    wb_down = scr("wb_down", [L, DFF, D], BF16)
    pj_tm = scr("pj_tm", [NT, TMW], BF16)
    pj_fm = scr("pj_fm", [16, 128, NT], BF16)
    d_S = scr("d_S", [2, NCH, 128, 1024], BF16)
    mixT = scr("mixT", [32, 128, NT], BF16)
    XB = scr("XB", [NT, D], F32)

    with ExitStack() as es:
        ARN = 86528
        arena_t = es.enter_context(nc.sbuf_tensor("arena", [128, ARN], BF16))
        A = Arena(arena_t, ARN)
        ctab = es.enter_context(nc.sbuf_tensor("ctab_s", [128, NCT], F32))
        small = es.enter_context(nc.sbuf_tensor("small_s", [128, NS], F32))
        cb = es.enter_context(nc.sbuf_tensor("cb", [128, 5 * 128 + 1024], BF16))
        wst_bf = es.enter_context(nc.sbuf_tensor("wst_bf", [128, 1024], BF16))
        lc = es.enter_context(nc.sbuf_tensor("lc", [128, 64], F32))
        dmat = es.enter_context(nc.sbuf_tensor("dmat", [128, 1024], F32))
        wqf = es.enter_context(nc.sbuf_tensor("wqf", [128, 1024], F32))
        wqb = es.enter_context(nc.sbuf_tensor("wqb", [128, 1024], F32))
        st = es.enter_context(nc.sbuf_tensor("st", [128, 32], F32))
        banks = [es.enter_context(nc.psum_tensor("ps%d" % i, [128, 512], F32)) for i in range(8)]
        IDENT = cb[:, 0:128]
        ONES = cb[:, 128:256]
        MPREV4 = cb[:, 256:768]
        MNEXT4 = cb[:, 768:1280]
        EPSC = ctab[:, C_EPS:C_EPS + 1]
        LG, DEC, SE, WF, WB = lc[:, 0:16], lc[:, 16:32], lc[:, 32:48], lc[:, 48:56], lc[:, 56:64]

        def psb(b):
            return banks[b][:]

        def psbf(b):
            return banks[b][:].bitcast(BF16)

        wbtok = {}

        def cast_list(l, names):
            out = []
            if "in" in names:
                for cg in range(18):
                    out.append((wb_in[l, :, cg * 512:(cg + 1) * 512], w_in[l, :, cg * 512:(cg + 1) * 512], ("wb", l, "in", cg)))
            if "out" in names:
                for cg in range(8):
                    out.append((wb_out[l, :, cg * 512:(cg + 1) * 512], w_out[l, :, cg * 512:(cg + 1) * 512], ("wb", l, "out", cg)))
            if "up" in names:
                for cg in range(32):
                    out.append((wb_up[l, :, cg * 512:(cg + 1) * 512], w_up[l, :, cg * 512:(cg + 1) * 512], ("wb", l, "up", cg)))
            if "down" in names:
                for rp in range(8):
                    for cg in range(8):
                        out.append((wb_down[l, rp * 2048:(rp + 1) * 2048, cg * 512:(cg + 1) * 512],
                                    w_down[l, rp * 2048:(rp + 1) * 2048, cg * 512:(cg + 1) * 512], ("wb", l, "down", rp, cg)))
            return out

        def emit_casts(items, extra=()):
            for dst, src, tok in items:
                wbtok[tok] = P.dma("pool", dst, src, w=[tok], extra=extra)

        emit_casts(cast_list(0, ["in"]))

        def restore_wb():
            for k, v in wbtok.items():
                P.tw[k] = v

        P.dma("sp", ctab[:], d_ctab, w=["ctab"])
        P.cp("dve", IDENT, ctab[:, C_ID:C_ID + 128], ["ctab"], ["ident"])
        P.cp("dve", ONES, ctab[:, C_ONES:C_ONES + 128], ["ctab"], ["ones"])
        for g in range(4):
            P.ts("dve", MPREV4[:, g * 128:(g + 1) * 128], ctab[:, C_MPREV:C_MPREV + 128], 30000.0, -30000.0, ALU.mult, ALU.add, ["ctab"], [("mprev", g)])
            P.ts("dve", MNEXT4[:, g * 128:(g + 1) * 128], ctab[:, C_MNEXT:C_MNEXT + 128], 30000.0, -30000.0, ALU.mult, ALU.add, ["ctab"], [("mnext", g)])
        P.barrier()
        restore_wb()

        def layer_consts(l):
            A.reset()
            tmpw = A.get([128, 1024], F32)
            tmp1 = A.get([128, 128], F32)
            tmp2 = A.get([128, 128], F32)
            P.dma("sp", small[:], d_small[l], w=["small"])
            P.dma("sp", tmpw, d_wst[l], w=["tmpw"])
            P.cp("dve", wst_bf[:], tmpw, ["tmpw"], ["wst"])
            P.act(LG, small[:, S_DEC:S_DEC + 16], AF.Exp, ["small"], ["lg0"])
            P.ts("dve", LG, LG, -1.0, None, ALU.mult, None, ["lg0"], ["lg"])
            P.act(DEC, LG, AF.Exp, ["lg"], ["dec"], scale=128.0)
            P.act(SE, small[:, S_SINK:S_SINK + 16], AF.Exp, ["small"], ["se"])
            P.ts("dve", WF, LG[:, 0:8], ctab[:, C_127MP:C_127MP + 1], None, ALU.mult, None, ["lg"], ["wf0"])
            P.act(WF, WF, AF.Exp, ["wf0"], ["wf"])
            P.ts("dve", WB, LG[:, 8:16], ctab[:, C_P:C_P + 1], None, ALU.mult, None, ["lg"], ["wb0"])
            P.act(WB, WB, AF.Exp, ["wb0"], ["wbw"])
            for h in range(8):
                hs = slice(h * 128, (h + 1) * 128)
                P.act(tmp1, ctab[:, C_PF:C_PF + 128], AF.Exp, ["lg", ("t1",)], [("t1",)], scale=LG[:, h:h + 1])
                P.tt("dve", tmp1, tmp1, ctab[:, C_MF:C_MF + 128], ALU.mult, [("t1",)], [("t1",)])
                P.act(tmp2, ctab[:, C_PB:C_PB + 128], AF.Exp, ["lg", ("t2",)], [("t2",)], scale=LG[:, 8 + h:9 + h])
                P.tt("dve", tmp2, tmp2, ctab[:, C_MB:C_MB + 128], ALU.mult, [("t2",)], [("t2",)])
                P.tt("dve", dmat[:, hs], tmp1, tmp2, ALU.add, [("t1",), ("t2",)], [("dmat", h)])
                P.act(wqf[:, hs], ctab[:, C_IP1:C_IP1 + 128], AF.Exp, ["lg"], [("wqf", h)], scale=LG[:, h:h + 1])
                P.act(wqb[:, hs], ctab[:, C_I128M:C_I128M + 128], AF.Exp, ["lg"], [("wqb", h)], scale=LG[:, 8 + h:9 + h])
            P.barrier()
            restore_wb()

        def alloc_dense(nxs=2):
            A.reset()
            C.H = A.get([128, 32, 512], BF16)
            C.W = [A.get([128, 16, 512], BF16) for _ in range(3)]
            C.XS = [A.get([128, D], F32) for _ in range(nxs)]
            C.XN = A.get([128, D], BF16)
            C.JUNK = A.get([128, 512], BF16)

        def Htok_all():
            return [("H", c, b) for c in range(32) for b in range(4)]

        def norm_tile(src, row0, src_tok, gcol):
            for b in range(4):
                s = P.count("xs", len(C.XS))
                xs = C.XS[s]
                P.dma("sp", xs, src[row0 + b * 128:row0 + (b + 1) * 128, :], r=src_tok, w=[("xs", s)])
                P.memset("dve", st[:, 0:8], 0.0, [("ss",)], [("ss",)])
                for c in range(8):
                    P.act(C.JUNK, xs[:, c * 512:(c + 1) * 512], AF.Square, [("xs", s), ("ss",)], [("ssc", c)],
                          accum_out=st[:, c:c + 1])
                P.add("dve", lambda e: e.tensor_reduce(out=st[:, 8:9], in_=st[:, 0:8], axis=AX.X, op=ALU.add),
                      [("ssc", c) for c in range(8)], [("ss",), ("st8",)])
                P.act(st[:, 9:10], st[:, 8:9], AF.Ln, [("st8",)], [("st9",)], scale=1.0 / D, bias=EPSC)
                P.act(st[:, 10:11], st[:, 9:10], AF.Exp, [("st9",)], [("st10",)], scale=-0.5)
                P.act(C.XN, xs, AF.Copy, [("xs", s), ("st10",), ("xn",)], [("xn",)], scale=st[:, 10:11])
                bs = [P.bank() for _ in range(4)]
                for c in range(32):
                    P.trn(psbf(bs[c // 8])[:, (c % 8) * 128:(c % 8 + 1) * 128], C.XN[:, c * 128:(c + 1) * 128], IDENT,
                          [("xn",), "ident"], [("ps", bs[c // 8])])
                for q in range(4):
                    P.tt("dve", C.H[:, q * 8:(q + 1) * 8, b * 128:(b + 1) * 128], v3(psbf(bs[q]), 8),
                         small[:, gcol + q * 8:gcol + (q + 1) * 8].unsqueeze(2).to_broadcast([128, 8, 128]), ALU.mult,
                         [("ps", bs[q]), "small"], [("H", c, b) for c in range(q * 8, (q + 1) * 8)])

        def wload(wb2d, k0, c0, toks):
            s = P.count("W", 3)
            src = wb2d.rearrange("(kc p) n -> p kc n", p=128)[:, k0:k0 + 16, c0:c0 + 512]
            P.dma("sp", C.W[s], src, r=toks, w=[("W", s)])
            return s

        def dense_group(wb2d, kparts, k0s, c0, toks, mode, src, srcname):
            bs = [P.bank() for _ in range(4)]
            nk = kparts * 16
            for kp in range(kparts):
                s = wload(wb2d, k0s[kp], c0, toks)
                Wt = C.W[s]
                for j in range(4):
                    for kc in range(16):
                        k = kp * 16 + kc
                        if mode == "FM":
                            P.mm(psb(bs[j]), Wt[:, kc, j * 128:(j + 1) * 128], src[:, k, :], k == 0, k == nk - 1,
                                 [("W", s)] + [(srcname, k, b) for b in range(4)], [("ps", bs[j])])
                        else:
                            P.mm(psb(bs[j]), src[:, k, j * 128:(j + 1) * 128], Wt[:, kc, :], k == 0, k == nk - 1,
                                 [("W", s), (srcname, k, j)], [("ps", bs[j])])
            return bs

        def phase1(l, src, src_tok_fn):
            alloc_dense()
            TAB = [A.get([128, 4, 512], F32) for _ in range(2)]
            OTM = [A.get([128, 4, 512], BF16) for _ in range(2)]
            OFM = [A.get([128, 4, 512], BF16) for _ in range(2)]
            RT = [A.get([128, 4, 128], F32) for _ in range(2)]
            for tt in range(NTT):
                tsl = P.count("tab", 2)
                P.dma("sp", TAB[tsl], d_rope[tt * 512:(tt + 1) * 512, :].rearrange("(b p) c -> p b c", p=128), w=[("tab", tsl)])
                norm_tile(src, tt * 512, src_tok_fn(tt), S_G1)
                for cg in range(18):
                    mode = "FM" if cg in CG_FM else "TM"
                    bs = dense_group(wb_in[l], 2, [0, 16], cg * 512, [("wb", l, "in", cg)], mode, C.H, "H")
                    if mode == "FM":
                        so = P.count("ofm", 2)
                        fn = AF.Gelu if cg < 2 else AF.Silu
                        for j in range(4):
                            P.act(OFM[so][:, j, :], psb(bs[j]), fn, [("ps", bs[j]), ("ofm", so)], [("ofmw", so, j)])
                        ch0 = {0: 0, 1: 4, 10: 8, 11: 12}[cg]
                        P.dma("act", pj_fm[ch0:ch0 + 4, :, tt * 512:(tt + 1) * 512].rearrange("c p t -> p c t"), OFM[so],
                              r=[("ofmw", so, j) for j in range(4)], w=[("ofm", so), ("pjfm", tt)])
                        continue
                    so = P.count("otm", 2)
                    ti = CG_TM.index(cg)
                    for j in range(4):
                        dst = OTM[so][:, j, :]
                        rd = [("ps", bs[j]), ("otm", so)]
                        wr = [("otmw", so, j)]
                        if cg in (2, 3):
                            P.act(dst, psb(bs[j]), AF.Gelu, rd, wr)
                        elif cg in (8, 9, 17):
                            if j % 2 == 0:
                                P.cp("act", dst, psb(bs[j]), rd, wr)
                            else:
                                P.cp("dve", dst, psb(bs[j]), rd, wr)
                        else:
                            scaled = cg in (6, 7, 12, 13, 14, 15)
                            tc = 2 if scaled else 0
                            cosb = TAB[tsl][:, j, tc * 128:(tc + 1) * 128].unsqueeze(1).to_broadcast([128, 4, 128])
                            sinb = TAB[tsl][:, j, (tc + 1) * 128:(tc + 2) * 128]
                            sin_lo = sinb[:, 0:64].unsqueeze(1).to_broadcast([128, 4, 64])
                            sin_hi = sinb[:, 64:128].unsqueeze(1).to_broadcast([128, 4, 64])
                            pv = v3(psb(bs[j]), 4)
                            ra, rb = RT[0], RT[1]
                            P.tt("dve", ra, pv, cosb, ALU.mult, [("ps", bs[j]), ("tab", tsl), ("ra",)], [("ra",)])
                            P.tt("dve", rb[:, :, 0:64], pv[:, :, 64:128], sin_lo, ALU.mult,
                                 [("ps", bs[j]), ("tab", tsl), ("rb",)], [("rb0",)])
                            P.tt("dve", rb[:, :, 64:128], pv[:, :, 0:64], sin_hi, ALU.mult,
                                 [("ps", bs[j]), ("tab", tsl), ("rb",)], [("rb1",)])
                            P.tt("dve", v3(dst, 4), ra, rb, ALU.add, rd[1:] + [("ra",), ("rb0",), ("rb1",)], wr + [("ra",), ("rb",)])
                    P.dma("act", pj_tm[tt * 512:(tt + 1) * 512, ti * 512:(ti + 1) * 512].rearrange("(b p) c -> p b c", p=128), OTM[so],
                          r=[("otmw", so, j) for j in range(4)], w=[("otm", so), ("pjtm", tt)])
            P.barrier()
            restore_wb()

        def phase2(l):
            A.reset()
            if l == 0:
                emit_casts(cast_list(0, ["out", "up", "down"]))
            KVT = [[A.get([128, 2048], BF16) for _ in range(2)] for _ in range(2)]
            KW = [A.get([128, 8, 128], BF16) for _ in range(2)]
            SST = [[A.get([128, 8, 128], F32) for _ in range(2)] for _ in range(2)]
            SOUT = [[A.get([128, 1024], BF16) for _ in range(2)] for _ in range(2)]
            for di in range(2):
                P.memset("dve", flat(SST[di][0]), 0.0, [], [("S", di, 0, h) for h in range(8)])
            for step in range(NCH):
                for di in range(2):
                    wv = WF if di == 0 else WB
                    n = step if di == 0 else NCH - 1 - step
                    cb_, nb_ = step % 2, (step + 1) % 2
                    cur, nxs = SST[di][cb_], SST[di][nb_]
                    s = step % 2
                    kv = KVT[di][s]
                    P.dma("sp", kv, pj_tm[n * 128:(n + 1) * 128, RK0:RK0 + 2048], w=[("kvt", di, s)])
                    P.cp("act", SOUT[di][s], flat(cur), [("S", di, cb_, h) for h in range(8)] + [("sout", di, s)], [("soutw", di, s)])
                    P.dma("sp", d_S[di, n], SOUT[di][s], r=[("soutw", di, s)], w=[("sout", di, s), ("dS", di, n)])
                    P.tt("dve", KW[di], v3(kv[:, 0:1024], 8), wv.unsqueeze(2).to_broadcast([128, 8, 128]), ALU.mult,
                         [("kvt", di, s), ("kw", di)], [("kw", di)])
                    bs = [P.bank(), P.bank()]
                    for h in range(8):
                        P.mm(psb(bs[h // 4])[:, (h % 4) * 128:(h % 4 + 1) * 128], KW[di][:, h, :],
                             kv[:, 1024 + h * 128:1024 + (h + 1) * 128], True, True,
                             [("kw", di), ("kvt", di, s)], [("ps", bs[h // 4])])
                    for h in range(8):
                        P.stt("dve", nxs[:, h, :], cur[:, h, :], DEC[:, di * 8 + h:di * 8 + h + 1],
                              psb(bs[h // 4])[:, (h % 4) * 128:(h % 4 + 1) * 128], ALU.mult, ALU.add,
                              [("ps", bs[h // 4]), ("S", di, cb_, h), "dec"], [("S", di, nb_, h)])
            P.barrier()
            restore_wb()
            A.reset()
            TM = [A.get([128, TMW], BF16) for _ in range(2)]
            FMt = [A.get([128, 16, 512], BF16) for _ in range(2)]
            OUT = [A.get([128, 32, 512], BF16) for _ in range(1)]
            SF = [A.get([128, 8, 128], BF16) for _ in range(2)]
            SB = [A.get([128, 8, 128], BF16) for _ in range(2)]
            AKV = [A.get([128, 1024], BF16) for _ in range(4)]
            AKT = [A.get([128, 4, 128], BF16) for _ in range(4)]
            VNF = A.get([128, 1024], F32)
            VN = A.get([128, 1024], BF16)
            J2 = A.get([128, 1024], BF16)
            QT = A.get([128, 8, 128], BF16)
            KT = A.get([128, 8, 128], BF16)
            QWF = A.get([128, 8, 128], BF16)
            QWB = A.get([128, 8, 128], BF16)
            SMT = A.get([128, 8, 128], BF16)
            SQ = A.get([128, 1024], BF16)
            RSTD = A.get([128, 1024], F32)
            TMPF = [A.get([128, 4, 128], F32) for _ in range(2)]
            AQT = A.get([128, 16, 128], BF16)
            PT = [A.get([128, 512], BF16) for _ in range(6)]
            DEN = [A.get([128, 4, 128], F32) for _ in range(2)]
            S_LN = 16

            def prepare(m):
                r = m % 4
                P.dma("sp", AKV[r], pj_tm[m * 128:(m + 1) * 128, AK0:AK0 + 1024], w=[("akv", r)])
                b = P.bank()
                for kh in range(4):
                    P.trn(psbf(b)[:, kh * 128:(kh + 1) * 128], AKV[r][:, kh * 128:(kh + 1) * 128], IDENT,
                          [("akv", r), "ident"], [("ps", b)])
                P.cp("act", flat(AKT[r]), psbf(b)[:, 0:512], [("ps", b)], [("akt", r)])

            prepare(0)
            for n in range(NCH):
                if n + 1 < NCH:
                    prepare(n + 1)
                g4 = n % 4
                ncs = slice(g4 * 128, (g4 + 1) * 128)
                if g4 == 0:
                    fs = P.count("fmt", 2)
                    os_ = P.count("out", 1)
                    P.dma("sp", FMt[fs], pj_fm[:, :, n * 128:n * 128 + 512].rearrange("c p t -> p c t"), w=[("fmt", fs)])
                ts_ = P.count("tm", 2)
                TMs = TM[ts_]
                P.dma("sp", TMs, pj_tm[n * 128:(n + 1) * 128, :], w=[("tm", ts_)])
                ss_ = P.count("sfb", 2)
                P.dma("sp", flat(SF[ss_]), d_S[0, n], w=[("sf", ss_)])
                P.dma("sp", flat(SB[ss_]), d_S[1, n], w=[("sb", ss_)])
                outw = [("outw", os_, n % 4, i) for i in range(8)]
                v = TMs[:, TV0:TV0 + 1024]
                c0 = S_LN
                P.memset("dve", st[:, c0:c0 + 2], 0.0, [("ln",)], [("ln0",)])
                P.act(J2, v, AF.Copy, [("tm", ts_), ("ln0",)], [("lna",)], accum_out=st[:, c0:c0 + 1])
                P.act(J2, v, AF.Square, [("tm", ts_), ("ln0",)], [("lnb",)], accum_out=st[:, c0 + 1:c0 + 2])
                P.ts("dve", st[:, c0 + 2:c0 + 4], st[:, c0:c0 + 2], 1.0 / 1024, None, ALU.mult, None, [("lna",), ("lnb",)], [("ln1",)])
                P.tt("dve", st[:, c0 + 4:c0 + 5], st[:, c0 + 2:c0 + 3], st[:, c0 + 2:c0 + 3], ALU.mult, [("ln1",)], [("ln2",)])
                P.tt("dve", st[:, c0 + 5:c0 + 6], st[:, c0 + 3:c0 + 4], st[:, c0 + 4:c0 + 5], ALU.subtract, [("ln1",), ("ln2",)], [("ln3",)])
                P.act(st[:, c0 + 8:c0 + 9], st[:, c0 + 5:c0 + 6], AF.Ln, [("ln3",)], [("ln3b",)], bias=EPSC)
                P.act(st[:, c0 + 6:c0 + 7], st[:, c0 + 8:c0 + 9], AF.Exp, [("ln3b",)], [("ln4",)], scale=-0.5)
                P.stt("dve", st[:, c0 + 7:c0 + 8], st[:, c0 + 2:c0 + 3], -1.0, st[:, c0 + 6:c0 + 7], ALU.mult, ALU.mult,
                      [("ln1",), ("ln4",)], [("ln5",)])
                P.act(VNF, v, AF.Identity, [("tm", ts_), ("ln4",), ("ln5",), ("vnf",)], [("vnf",)],
                      scale=st[:, c0 + 6:c0 + 7], bias=st[:, c0 + 7:c0 + 8])
                P.tt("dve", VNF, VNF, small[:, S_LNG:S_LNG + 1024], ALU.mult, [("vnf",), "small"], [("vnf",), ("ln",)])
                P.tt("dve", VN, VNF, small[:, S_LNB:S_LNB + 1024], ALU.add, [("vnf",), "small", ("vn",)], [("vn",)])
                bs = [P.bank(), P.bank()]
                for g in range(8):
                    P.mm(psb(bs[g // 4])[:, (g % 4) * 128:(g % 4 + 1) * 128], VN[:, g * 128:(g + 1) * 128],
                         wst_bf[:, g * 128:(g + 1) * 128], True, True, [("vn",), "wst"], [("ps", bs[g // 4])])
                for gb in range(2):
                    tf = TMPF[gb]
                    P.tt("dve", flat(tf), psb(bs[gb]), small[:, S_BSB + gb * 512:S_BSB + (gb + 1) * 512], ALU.add,
                         [("ps", bs[gb]), "small", ("tmpf", gb)], [("tmpf", gb)])
                    P.tt("dve", OUT[os_][:, gb * 4:(gb + 1) * 4, ncs], tf, FMt[fs][:, gb * 4:(gb + 1) * 4, ncs], ALU.mult,
                         [("tmpf", gb), ("fmt", fs), ("out", os_)], [outw[gb]])
                bq, bk = P.bank(), P.bank()
                for h in range(8):
                    P.trn(psbf(bq)[:, h * 128:(h + 1) * 128], TMs[:, RQ0 + h * 128:RQ0 + (h + 1) * 128], IDENT,
                          [("tm", ts_), "ident"], [("ps", bq)])
                for h in range(8):
                    P.trn(psbf(bk)[:, h * 128:(h + 1) * 128], TMs[:, RK0 + h * 128:RK0 + (h + 1) * 128], IDENT,
                          [("tm", ts_), "ident"], [("ps", bk)])
                P.cp("act", flat(QT), psbf(bq), [("ps", bq), ("qt",)], [("qt",)])
                P.cp("dve", flat(KT), psbf(bk), [("ps", bk), ("kt",)], [("kt",)])
                P.tt("dve", flat(QWF), flat(QT), wqf[:], ALU.mult, [("qt",), ("qwf",)] + [("wqf", h) for h in range(8)], [("qwf",)])
                P.tt("dve", flat(QWB), flat(QT), wqb[:], ALU.mult, [("qt",), ("qwb",)] + [("wqb", h) for h in range(8)], [("qwb",)])
                bsc = [P.bank(), P.bank()]
                for h in range(8):
                    P.mm(psb(bsc[h // 4])[:, (h % 4) * 128:(h % 4 + 1) * 128], KT[:, h, :], QT[:, h, :], True, True,
                         [("kt",), ("qt",)], [("ps", bsc[h // 4])])
                for hb in range(2):
                    P.tt("dve", flat(SMT[:, hb * 4:(hb + 1) * 4, :]), psb(bsc[hb]), dmat[:, hb * 512:(hb + 1) * 512], ALU.mult,
                         [("ps", bsc[hb]), ("smt", hb)] + [("dmat", h) for h in range(8)], [("smt", hb)])
                br = [P.bank(), P.bank()]
                for h in range(8):
                    o = psb(br[h // 4])[:, (h % 4) * 128:(h % 4 + 1) * 128]
                    wr = [("ps", br[h // 4])]
                    P.mm(o, TMs[:, RV0 + h * 128:RV0 + (h + 1) * 128], SMT[:, h, :], True, False, [("tm", ts_), ("smt", h // 4)], wr)
                    P.mm(o, SF[ss_][:, h, :], QWF[:, h, :], False, False, [("sf", ss_), ("qwf",)], wr)
                    P.mm(o, SB[ss_][:, h, :], QWB[:, h, :], False, True, [("sb", ss_), ("qwb",)], wr)
                bm = [P.bank(), P.bank()]
                for hb in range(2):
                    hs = slice(hb * 512, (hb + 1) * 512)
                    P.act(SQ[:, hs], psb(br[hb]), AF.Square, [("ps", br[hb]), ("sq", hb)], [("sq", hb)])
                    P.mm(psb(bm[hb]), ONES, SQ[:, hs], True, True, ["ones", ("sq", hb)], [("ps", bm[hb])])
                    P.act(RSTD[:, hs], psb(bm[hb]), AF.Ln, [("ps", bm[hb]), ("rstd", hb)], [("rstd", hb)], scale=1.0 / 128, bias=EPSC)
                    P.act(RSTD[:, hs], RSTD[:, hs], AF.Exp, [("rstd", hb)], [("rstd", hb)], scale=-0.5)
                    tf = TMPF[hb]
                    P.tt("dve", flat(tf), psb(br[hb]), RSTD[:, hs], ALU.mult, [("ps", br[hb]), ("rstd", hb), ("tmpf", hb)], [("tmpf", hb)])
                    P.tt("dve", OUT[os_][:, 8 + hb * 4:8 + (hb + 1) * 4, ncs], tf, FMt[fs][:, 8 + hb * 4:8 + (hb + 1) * 4, ncs], ALU.mult,
                         [("tmpf", hb), ("fmt", fs), ("out", os_)], [outw[2 + hb]])
                ba = [P.bank(), P.bank()]
                for hq in range(16):
                    P.trn(psbf(ba[hq // 8])[:, (hq % 8) * 128:(hq % 8 + 1) * 128], TMs[:, AQ0 + hq * 128:AQ0 + (hq + 1) * 128], IDENT,
                          [("tm", ts_), "ident"], [("ps", ba[hq // 8])])
                P.cp("act", flat(AQT[:, 0:8, :]), psbf(ba[0]), [("ps", ba[0]), ("aqt", 0)], [("aqt", 0)])
                P.cp("dve", flat(AQT[:, 8:16, :]), psbf(ba[1]), [("ps", ba[1]), ("aqt", 1)], [("aqt", 1)])
                for kh in range(4):
                    ms = [m for m in (n - 1, n, n + 1) if 0 <= m < NCH]
                    pts = []
                    for m in ms:
                        b = P.bank()
                        pi = P.count("pt", 6)
                        pts.append(pi)
                        P.mm(psb(b), AKT[m % 4][:, kh, :], flat(AQT[:, kh * 4:(kh + 1) * 4, :]), True, m == n,
                             [("akt", m % 4), ("aqt", kh // 2)], [("ps", b)])
                        if m != n:
                            mk = MPREV4 if m < n else MNEXT4
                            mt = [("mprev" if m < n else "mnext", g) for g in range(4)]
                            P.mm(psb(b), IDENT, mk, False, True, ["ident"] + mt, [("ps", b)])
                        P.act(PT[pi], psb(b), AF.Exp, [("ps", b), ("pt", pi)], [("pt", pi)])
                    bo, bd = P.bank(), P.bank()
                    for i, m in enumerate(ms):
                        P.mm(psb(bo), AKV[m % 4][:, 512 + kh * 128:512 + (kh + 1) * 128], PT[pts[i]], i == 0, i == len(ms) - 1,
                             [("akv", m % 4), ("pt", pts[i])], [("ps", bo)])
                    for i, m in enumerate(ms):
                        P.mm(psb(bd), ONES, PT[pts[i]], i == 0, i == len(ms) - 1, ["ones", ("pt", pts[i])], [("ps", bd)])
                    dn = DEN[kh % 2]
                    for g in range(4):
                        P.act(dn[:, g, :], psb(bd)[:, g * 128:(g + 1) * 128], AF.Ln, [("ps", bd), "se", ("den", kh % 2)], [("dena", kh % 2, g)],
                              bias=SE[:, kh * 4 + g:kh * 4 + g + 1])
                    P.act(flat(dn), flat(dn), AF.Exp, [("dena", kh % 2, g) for g in range(4)], [("den", kh % 2)], scale=-1.0)
                    P.tt("dve", OUT[os_][:, 16 + kh * 4:16 + (kh + 1) * 4, ncs], v3(psb(bo), 4), dn, ALU.mult,
                         [("ps", bo), ("den", kh % 2), ("out", os_)], [outw[4 + kh]])
                if g4 == 3:
                    P.dma("sp", mixT[:, :, (n - 3) * 128:(n + 1) * 128].rearrange("c p t -> p c t"), OUT[os_],
                          r=[("outw", os_, q, i) for q in range(4) for i in range(8)], w=[("out", os_), ("mix", n // 4)])
            P.barrier()
            restore_wb()

        def phase3(l, res_is_x):
            alloc_dense(1)
            AT = A.get([128, 32, 512], BF16)
            XR = [A.get([128, 4, 512], F32) for _ in range(2)]
            RL = [A.get([128, 512], F32) for _ in range(2)]

            def resid_evac(bs, src, src_tok, tt, cg):
                s = P.count("xr", 2)
                view = lambda t: t[tt * 512:(tt + 1) * 512, cg * 512:(cg + 1) * 512].rearrange("(b p) c -> p b c", p=128)
                P.dma("sp", XR[s], view(src), r=src_tok, w=[("xr", s)])
                for j in range(4):
                    P.tt("dve", XR[s][:, j, :], psb(bs[j]), XR[s][:, j, :], ALU.add, [("ps", bs[j]), ("xr", s)], [("xrw", s, j)])
                P.dma("act", view(XB), XR[s], r=[("xrw", s, j) for j in range(4)], w=[("xr", s), ("XB", tt, cg)])

            nxt = cast_list(l + 1, ["in", "out", "up", "down"]) if l + 1 < L else []
            pace = {"g": 0, "i": 0}
            stride = 4 if NTT > 1 else 1

            def paced_cast(bs, tt):
                if tt == 0 and NTT > 1:
                    return
                pace["g"] += 1
                if pace["g"] % stride == 0 and pace["i"] < len(nxt):
                    emit_casts(nxt[pace["i"]:pace["i"] + 1], extra=[P.tw[("ps", bs[0])]])
                    pace["i"] += 1

            for tt in range(NTT):
                P.dma("sp", C.H, mixT[:, :, tt * 512:(tt + 1) * 512].rearrange("c p t -> p c t"), w=Htok_all())
                for cg in range(8):
                    bs = dense_group(wb_out[l], 2, [0, 16], cg * 512, [("wb", l, "out", cg)], "TM", C.H, "H")
                    paced_cast(bs, tt)
                    if res_is_x:
                        resid_evac(bs, x_in, [], tt, cg)
                    else:
                        resid_evac(bs, XB, [("XB", tt, cg)], tt, cg)
                norm_tile(XB, tt * 512, [("XB", tt, cg) for cg in range(8)], S_G2)
                for q in range(4):
                    for cg in range(8):
                        bs = dense_group(wb_up[l], 2, [0, 16], q * 4096 + cg * 512, [("wb", l, "up", q * 8 + cg)], "FM", C.H, "H")
                        paced_cast(bs, tt)
                        for j in range(4):
                            rs = P.count("relu", 2)
                            P.act(RL[rs], psb(bs[j]), AF.Relu, [("ps", bs[j]), ("rl", rs)], [("rl", rs)])
                            P.tt("dve", AT[:, cg * 4 + j, :], RL[rs], RL[rs], ALU.mult,
                                 [("rl", rs)], [("AT", cg * 4 + j, b) for b in range(4)])
                    for cg in range(8):
                        bs = dense_group(wb_down[l], 2, [q * 32, q * 32 + 16], cg * 512,
                                         [("wb", l, "down", q * 2, cg), ("wb", l, "down", q * 2 + 1, cg)], "TM", AT, "AT")
                        paced_cast(bs, tt)
                        resid_evac(bs, XB, [("XB", tt, cg)], tt, cg)
            emit_casts(nxt[pace["i"]:])
            P.barrier()
            restore_wb()

        def final_norm():
            A.reset()
            GF = A.get([128, D], F32)
            XS = [A.get([128, D], F32) for _ in range(2)]
            JK = A.get([128, 512], BF16)
            P.dma("sp", GF, d_gf, w=["gf"])
            for blk in range(NCH):
                s = P.count("fxs", 2)
                xs = XS[s]
                P.dma("sp", xs, XB[blk * 128:(blk + 1) * 128, :], w=[("fxs", s)])
                P.memset("dve", st[:, 0:8], 0.0, [("ss",)], [("ss",)])
                for c in range(8):
                    P.act(JK, xs[:, c * 512:(c + 1) * 512], AF.Square, [("fxs", s), ("ss",)], [("ssc", c)], accum_out=st[:, c:c + 1])
                P.add("dve", lambda e: e.tensor_reduce(out=st[:, 8:9], in_=st[:, 0:8], axis=AX.X, op=ALU.add),
                      [("ssc", c) for c in range(8)], [("ss",), ("st8",)])
                P.act(st[:, 9:10], st[:, 8:9], AF.Ln, [("st8",)], [("st9",)], scale=1.0 / D, bias=EPSC)
                P.act(st[:, 10:11], st[:, 9:10], AF.Exp, [("st9",)], [("st10",)], scale=-0.5)
                P.act(xs, xs, AF.Copy, [("fxs", s), ("st10",)], [("fxs", s)], scale=st[:, 10:11])
                P.tt("dve", xs, xs, GF, ALU.mult, [("fxs", s), "gf"], [("fxs", s)])
                P.dma("act", y_out[blk * 128:(blk + 1) * 128, :], xs, r=[("fxs", s)], w=[("fxs", s)])
            P.barrier()

        for l in range(L):
            layer_consts(l)
            if l == 0:
                phase1(l, x_in, lambda tt: [])
            else:
                phase1(l, XB, lambda tt: [])
            phase2(l)
            phase3(l, res_is_x=(l == 0))
        final_norm()
        for e in ENGINES:
            P.add(e, lambda eng: eng.nop(), [], [])
        P.emit(nc, es)
    return nc


def _ctab():
    p = np.arange(128, dtype=np.float32)[:, None]
    i = np.arange(128, dtype=np.float32)[None, :]
    t = np.zeros((128, NCT), np.float32)
    t[:, C_PF:C_PF + 128] = np.maximum(i - p, 0)
    t[:, C_MF:C_MF + 128] = (i >= p)
    t[:, C_PB:C_PB + 128] = np.maximum(p - i, 0)
    t[:, C_MB:C_MB + 128] = (p > i)
    t[:, C_IP1:C_IP1 + 128] = i + 1
    t[:, C_I128M:C_I128M + 128] = 128 - i
    t[:, C_C128:C_C128 + 128] = 128.0
    t[:, C_ID:C_ID + 128] = (i == p)
    t[:, C_MPREV:C_MPREV + 128] = (p >= i)
    t[:, C_MNEXT:C_MNEXT + 128] = (p <= i)
    t[:, C_ONES:C_ONES + 128] = 1.0
    t[:, C_127MP] = 127 - p[:, 0]
    t[:, C_P] = p[:, 0]
    t[:, C_EPS] = EPS
    return t


def _rope(NT):
    pos = np.arange(NT, dtype=np.float32)
    inv = (np.float32(10000.0) ** (-np.arange(0, 128, 2, dtype=np.float32) / np.float32(128))).astype(np.float32)
    ang = (pos[:, None] * inv[None, :]).astype(np.float32)
    ang = np.concatenate([ang, ang], -1)
    cos = np.cos(ang).astype(np.float32)
    sin = np.sin(ang).astype(np.float32)
    sins = np.concatenate([-sin[:, :64], sin[:, 64:]], -1)
    sc = np.float32(128 ** -0.5)
    return np.ascontiguousarray(np.concatenate([cos, sins, cos * sc, sins * sc], -1).astype(np.float32))


def _host_params(inp, L):
    small = np.zeros((L, 128, NS), np.float32)
    wst = np.zeros((L, 128, 1024), np.float32)
    for l in range(L):
        small[l, :, S_G1:S_G1 + 32] = inp["ln_mix_g"][l].reshape(32, 128).T
        small[l, :, S_G2:S_G2 + 32] = inp["ln_mlp_g"][l].reshape(32, 128).T
        small[l, :, S_DEC:S_DEC + 16] = inp["ret_log_decay"][l].reshape(1, 16)
        small[l, :, S_SINK:S_SINK + 16] = inp["attn_sink"][l].reshape(1, 16)
        small[l, :, S_LNG:S_LNG + 1024] = inp["sgu_ln_g"][l][None, :]
        small[l, :, S_LNB:S_LNB + 1024] = inp["sgu_ln_b"][l][None, :]
        small[l, :, S_BSB:S_BSB + 1024] = inp["sgu_b"][l].reshape(1, 1024)
        wst[l] = np.transpose(inp["sgu_w"][l], (2, 0, 1)).reshape(128, 1024)
    gf = np.ascontiguousarray(np.broadcast_to(inp["final_norm_g"][None, :], (128, D))).astype(np.float32)
    return small, wst, gf


_NC_CACHE = {}


def run(inp, NT, L, seqs, n_cores):
    key = (NT, L)
    if key not in _NC_CACHE:
        _NC_CACHE[key] = build(NT, L)
    nc = _NC_CACHE[key]
    small, wst, gf = _host_params(inp, L)
    ctab = _ctab()
    rope = _rope(NT)
    f = lambda a: np.ascontiguousarray(np.asarray(a, dtype=np.float32))
    shared = {"w_in": f(inp["w_in"][:L]), "w_out": f(inp["w_out"][:L]), "w_up": f(inp["w_up"][:L]),
              "w_down": f(inp["w_down"][:L]), "small": small, "wst": wst, "ctab": ctab, "rope": rope, "gf": gf}
    zero = None
    in_maps = []
    for c in range(n_cores):
        m = dict(shared)
        if seqs[c] is None:
            if zero is None:
                zero = {k: np.zeros_like(shared[k]) for k in ("w_in", "w_out", "w_up", "w_down")}
                zero["x"] = np.zeros((NT, D), np.float32)
            m.update(zero)
        else:
            m["x"] = f(inp["x"][seqs[c]])
        in_maps.append(m)
    res = run_bass_kernel_spmd(nc, in_maps, core_ids=list(range(n_cores)))
    return [np.asarray(r["y"]) if seqs[c] is not None else None for c, r in enumerate(res.results)]


def kernel(x, ln_mix_g, w_in, sgu_ln_g, sgu_ln_b, sgu_w, sgu_b, ret_log_decay, attn_sink, w_out,
           ln_mlp_g, w_up, w_down, final_norm_g):
    inp = dict(x=np.asarray(x), ln_mix_g=np.asarray(ln_mix_g), w_in=np.asarray(w_in), sgu_ln_g=np.asarray(sgu_ln_g),
               sgu_ln_b=np.asarray(sgu_ln_b), sgu_w=np.asarray(sgu_w), sgu_b=np.asarray(sgu_b),
               ret_log_decay=np.asarray(ret_log_decay), attn_sink=np.asarray(attn_sink), w_out=np.asarray(w_out),
               ln_mlp_g=np.asarray(ln_mlp_g), w_up=np.asarray(w_up), w_down=np.asarray(w_down),
               final_norm_g=np.asarray(final_norm_g))
    B, S, _ = inp["x"].shape
    active = [0, 1, 4, 5]
    seqs = [None] * 8
    for b in range(B):
        seqs[active[b]] = b
    ys = run(inp, S, 2, seqs, 8)
    return np.stack([ys[active[b]] for b in range(B)], 0).astype(np.float32)
```

```python
import numpy as np
from contextlib import ExitStack
from itertools import chain as _chain
import concourse.bass as bass
import concourse.mybir as mybir
from concourse.bass_utils import run_bass_kernel_spmd

F32 = mybir.dt.float32
BF16 = mybir.dt.bfloat16
AF = mybir.ActivationFunctionType
ALU = mybir.AluOpType
AX = mybir.AxisListType

D = 4096
DFF = 16384
INW = 9216
EPS = 1e-5
CG_TM = [2, 3, 4, 5, 6, 7, 8, 9, 12, 13, 14, 15, 16, 17]
CG_FM = [0, 1, 10, 11]
TMW = 7168
TV0, RQ0, RK0, RV0, AQ0, AK0, AV0 = 0, 1024, 2048, 3072, 4096, 6144, 6656
C_PF, C_MF, C_PB, C_MB, C_IP1, C_I128M, C_C128, C_ID, C_MPREV, C_MNEXT, C_ONES = [i * 128 for i in range(11)]
C_127MP = 11 * 128
C_P = 11 * 128 + 1
C_EPS = 11 * 128 + 2
NCT = 11 * 128 + 3
S_G1, S_G2, S_DEC, S_SINK, S_LNG, S_LNB, S_BSB = 0, 32, 64, 80, 96, 96 + 1024, 96 + 2048
NS = 96 + 3072

ENGINES = ["pe", "act", "dve", "pool", "sp"]
DEBUG_LINES = False
NAMES = {}


class Op:
    __slots__ = ("eng", "fn", "deps", "dma", "signal", "sem", "val", "line")

    def __init__(self, eng, fn, deps, dma):
        self.eng, self.fn, self.deps, self.dma = eng, fn, deps, dma
        self.signal = False
        self.sem = None
        self.val = 0


class Prog:
    KDMA = 8

    def __init__(self):
        self.ops = []
        self.tw = {}
        self.tr = {}
        self.pending = {}
        self.last = {}
        self.dmas = {"sp": [], "pool": [], "act": []}
        self.nbank = 0
        self.ctr = {}

    def count(self, name, mod):
        v = self.ctr.get(name, 0)
        self.ctr[name] = v + 1
        return v % mod

    def bank(self):
        b = self.nbank % 8
        self.nbank += 1
        return b

    def bankr(self, name, banks):
        return banks[self.count(name, len(banks))]

    def add(self, eng, fn, r=(), w=(), dma=False, extra=()):
        idx = len(self.ops)
        deps = set(extra)
        for t in r:
            x = self.tw.get(t)
            if x is not None:
                deps.add(x)
        for t in w:
            x = self.tw.get(t)
            if x is not None:
                deps.add(x)
            rd = self.tr.get(t)
            if rd:
                deps.update(rd[0].values())
                deps.update(rd[1])
        if eng in self.pending:
            deps |= self.pending.pop(eng)
        for d in deps:
            self.ops[d].signal = True
        self.ops.append(Op(eng, fn, deps, dma))
        if DEBUG_LINES:
            import sys as _s
            f = _s._getframe(1)
            while f.f_code.co_name in ("add", "dma", "mm", "trn", "act", "tt", "ts", "stt", "cp", "memset"):
                f = f.f_back
            self.ops[-1].line = f.f_lineno
        for t in w:
            self.tw[t] = idx
            self.tr[t] = [{}, []]
        for t in r:
            rd = self.tr.setdefault(t, [{}, []])
            if dma:
                rd[1].append(idx)
            else:
                rd[0][eng] = idx
        self.last[eng] = idx
        if dma:
            self.dmas[eng].append(idx)
        return idx

    def barrier(self):
        deps = set(v for k, v in self.last.items() if k != "pool")
        for q in ("sp", "act"):
            deps.update(self.dmas[q][-self.KDMA:])
        for e in ENGINES:
            self.pending[e] = set(deps) | self.pending.get(e, set())
        self.tw = {}
        self.tr = {}

    def dma(self, q, out, in_, r=(), w=(), extra=()):
        return self.add(q, lambda e: e.dma_start(out=out, in_=in_), r, w, dma=True, extra=extra)

    def mm(self, out, lhsT, rhs, start, stop, r, w):
        return self.add("pe", lambda e: e.matmul(out, lhsT=lhsT, rhs=rhs, start=start, stop=stop), r, w)

    def trn(self, out, in_, ident, r, w):
        return self.add("pe", lambda e: e.transpose(out=out, in_=in_, identity=ident), r, w)

    def act(self, out, in_, func, r, w, **kw):
        return self.add("act", lambda e: e.activation(out=out, in_=in_, func=func, **kw), r, w)

    def tt(self, eng, out, in0, in1, op, r, w):
        return self.add(eng, lambda e: e.tensor_tensor(out=out, in0=in0, in1=in1, op=op), r, w)

    def ts(self, eng, out, in0, s1, s2, op0, op1, r, w):
        if op1 is None:
            return self.add(eng, lambda e: e.tensor_scalar(out=out, in0=in0, scalar1=s1, scalar2=None, op0=op0), r, w)
        return self.add(eng, lambda e: e.tensor_scalar(out=out, in0=in0, scalar1=s1, scalar2=s2, op0=op0, op1=op1), r, w)

    def stt(self, eng, out, in0, scalar, in1, op0, op1, r, w):
        return self.add(eng, lambda e: e.scalar_tensor_tensor(out=out, in0=in0, scalar=scalar, in1=in1, op0=op0, op1=op1), r, w)

    def cp(self, eng, out, in_, r, w):
        if eng == "act":
            return self.add("act", lambda e: e.activation(out=out, in_=in_, func=AF.Copy), r, w)
        return self.add(eng, lambda e: e.tensor_copy(out=out, in_=in_), r, w)

    def memset(self, eng, ap, val, r, w):
        return self.add(eng, lambda e: e.memset(ap, val), r, w)

    def emit(self, nc, es):
        esem = {e: es.enter_context(nc.semaphore("s_" + e)) for e in ["pe", "act", "dve", "pool"]}
        dsem = {q: [es.enter_context(nc.semaphore("d_%s%d" % (q, i))) for i in range(self.KDMA)] for q in self.dmas}
        cnt = {e: 0 for e in esem}
        dcnt = {q: 0 for q in self.dmas}
        ops = self.ops
        for op in ops:
            if op.dma:
                q = op.eng
                j = dcnt[q]
                op.sem = dsem[q][j % self.KDMA]
                op.val = 16 * (j // self.KDMA + 1)
                if j >= self.KDMA:
                    op.deps.add(self.dmas[q][j - self.KDMA])
                dcnt[q] = j + 1
            elif op.signal and op.eng in esem:
                cnt[op.eng] += 1
                op.sem = esem[op.eng]
                op.val = cnt[op.eng]
        streams = {e: [] for e in ENGINES}
        for op in ops:
            streams[op.eng].append(op)
        block = es.enter_context(nc.Block())
        names = {"pe": "tensor", "act": "scalar", "dve": "vector", "pool": "gpsimd", "sp": "sync"}

        def make_body(ename):
            def body(eng):
                waited = {}
                for op in streams[ename]:
                    need = {}
                    for d in op.deps:
                        dop = ops[d]
                        if dop.sem is None:
                            continue
                        if ename == "pe" and dop.eng == "pe":
                            continue
                        k = dop.sem
                        if need.get(k, (None, 0))[1] < dop.val:
                            need[k] = (dop.sem, dop.val)
                    for k, (sem, val) in need.items():
                        if waited.get(k, 0) < val:
                            eng.wait_ge(sem, val)
                            waited[k] = val
                    ins = op.fn(eng)
                    if DEBUG_LINES:
                        try:
                            NAMES[ins.ins.name] = op.line
                        except Exception:
                            pass
                    if op.dma:
                        ins.then_inc(op.sem, 16)
                    elif op.signal and op.sem is not None:
                        ins.then_inc(op.sem, 1)
            return body

        for e in ENGINES:
            getattr(block, names[e])(make_body(e))


class Arena:
    def __init__(self, ap, nelem):
        self.ap = ap
        self.n = nelem
        self.off = 0

    def reset(self):
        self.off = 0

    def get(self, shape, dt):
        per = int(np.prod(shape[1:]))
        nb = per * (4 if dt == F32 else 2)
        nb = (nb + 63) // 64 * 64
        ne = nb // 2
        assert self.off + ne <= self.n, ("arena overflow", self.off, ne, self.n)
        v = self.ap[:, self.off:self.off + (per * 2 if dt == F32 else per)]
        self.off += ne
        if dt == F32:
            v = v.bitcast(F32)
        if len(shape) == 3:
            v = v.rearrange("p (a b) -> p a b", a=shape[1])
        return v


def v3(ap, a):
    return ap.rearrange("p (a b) -> p a b", a=a)


def flat(ap):
    return ap.rearrange("p a b -> p (a b)")


class Ctx:
    pass


def build(NT, L):
    NTT = NT // 512
    NCH = NT // 128
    nc = bass.Bass("TRN2", target_bir_lowering=False)
    P = Prog()
    C = Ctx()
    dt_in = lambda name, shape, dt=F32: nc.dram_tensor(name, shape, dt, kind="ExternalInput").ap()
    x_in = dt_in("x", [NT, D])
    w_in = dt_in("w_in", [L, D, INW])
    w_out = dt_in("w_out", [L, D, D])
    w_up = dt_in("w_up", [L, D, DFF])
    w_down = dt_in("w_down", [L, DFF, D])
    d_small = dt_in("small", [L, 128, NS])
    d_wst = dt_in("wst", [L, 128, 1024])
    d_ctab = dt_in("ctab", [128, NCT])
    d_rope = dt_in("rope", [NT, 512])
    d_gf = dt_in("gf", [128, D])
    y_out = nc.dram_tensor("y", [NT, D], F32, kind="ExternalOutput").ap()
    scr = lambda name, shape, dt: nc.dram_tensor(name, shape, dt, kind="Internal").ap()
    wb_in = scr("wb_in", [L, D, INW], BF16)
    wb_out = scr("wb_out", [L, D, D], BF16)
    wb_up = scr("wb_up", [L, D, DFF], BF16)
    wb_down = scr("wb_down", [L, DFF, D], BF16)
    pj_tm = scr("pj_tm", [NT, TMW], BF16)
    pj_fm = scr("pj_fm", [16, 128, NT], BF16)
    d_S = scr("d_S", [2, NCH, 128, 1024], BF16)
    mixT = scr("mixT", [32, 128, NT], BF16)
    XB = scr("XB", [NT, D], F32)

    with ExitStack() as es:
        ARN = 86528
        arena_t = es.enter_context(nc.sbuf_tensor("arena", [128, ARN], BF16))
        A = Arena(arena_t, ARN)
        ctab = es.enter_context(nc.sbuf_tensor("ctab_s", [128, NCT], F32))
        small = es.enter_context(nc.sbuf_tensor("small_s", [128, NS], F32))
        cb = es.enter_context(nc.sbuf_tensor("cb", [128, 5 * 128 + 1024], BF16))
        wst_bf = es.enter_context(nc.sbuf_tensor("wst_bf", [128, 1024], BF16))
        lc = es.enter_context(nc.sbuf_tensor("lc", [128, 64], F32))
        dmat = es.enter_context(nc.sbuf_tensor("dmat", [128, 1024], F32))
        wqf = es.enter_context(nc.sbuf_tensor("wqf", [128, 1024], F32))
        wqb = es.enter_context(nc.sbuf_tensor("wqb", [128, 1024], F32))
        st = es.enter_context(nc.sbuf_tensor("st", [128, 32], F32))
        banks = [es.enter_context(nc.psum_tensor("ps%d" % i, [128, 512], F32)) for i in range(8)]
        IDENT = cb[:, 0:128]
        ONES = cb[:, 128:256]
        MPREV4 = cb[:, 256:768]
        MNEXT4 = cb[:, 768:1280]
        EPSC = ctab[:, C_EPS:C_EPS + 1]
        LG, DEC, SE, WF, WB = lc[:, 0:16], lc[:, 16:32], lc[:, 32:48], lc[:, 48:56], lc[:, 56:64]

        def psb(b):
            return banks[b][:]

        def psbf(b):
            return banks[b][:].bitcast(BF16)

        wbtok = {}

        def cast_list(l, names):
            out = []
            if "in" in names:
                for cg in range(18):
                    out.append((wb_in[l, :, cg * 512:(cg + 1) * 512], w_in[l, :, cg * 512:(cg + 1) * 512], ("wb", l, "in", cg)))
            if "out" in names:
                for cg in range(8):
                    out.append((wb_out[l, :, cg * 512:(cg + 1) * 512], w_out[l, :, cg * 512:(cg + 1) * 512], ("wb", l, "out", cg)))
            if "up" in names:
                for cg in range(32):
                    out.append((wb_up[l, :, cg * 512:(cg + 1) * 512], w_up[l, :, cg * 512:(cg + 1) * 512], ("wb", l, "up", cg)))
            if "down" in names:
                for rp in range(8):
                    for cg in range(8):
                        out.append((wb_down[l, rp * 2048:(rp + 1) * 2048, cg * 512:(cg + 1) * 512],
                                    w_down[l, rp * 2048:(rp + 1) * 2048, cg * 512:(cg + 1) * 512], ("wb", l, "down", rp, cg)))
            return out

        def emit_casts(items, extra=()):
            for dst, src, tok in items:
                wbtok[tok] = P.dma("pool", dst, src, w=[tok], extra=extra)

        emit_casts(cast_list(0, ["in"]))

        def restore_wb():
            for k, v in wbtok.items():
                P.tw[k] = v

        P.dma("sp", ctab[:], d_ctab, w=["ctab"])
        P.cp("dve", IDENT, ctab[:, C_ID:C_ID + 128], ["ctab"], ["ident"])
        P.cp("dve", ONES, ctab[:, C_ONES:C_ONES + 128], ["ctab"], ["ones"])
        for g in range(4):
            P.ts("dve", MPREV4[:, g * 128:(g + 1) * 128], ctab[:, C_MPREV:C_MPREV + 128], 30000.0, -30000.0, ALU.mult, ALU.add, ["ctab"], [("mprev", g)])
            P.ts("dve", MNEXT4[:, g * 128:(g + 1) * 128], ctab[:, C_MNEXT:C_MNEXT + 128], 30000.0, -30000.0, ALU.mult, ALU.add, ["ctab"], [("mnext", g)])
        P.barrier()
        restore_wb()

        def layer_consts(l):
            A.reset()
            tmpw = A.get([128, 1024], F32)
            tmp1 = A.get([128, 128], F32)
            tmp2 = A.get([128, 128], F32)
            P.dma("sp", small[:], d_small[l], w=["small"])
            P.dma("sp", tmpw, d_wst[l], w=["tmpw"])
            P.cp("dve", wst_bf[:], tmpw, ["tmpw"], ["wst"])
            P.act(LG, small[:, S_DEC:S_DEC + 16], AF.Exp, ["small"], ["lg0"])
            P.ts("dve", LG, LG, -1.0, None, ALU.mult, None, ["lg0"], ["lg"])
            P.act(DEC, LG, AF.Exp, ["lg"], ["dec"], scale=128.0)
            P.act(SE, small[:, S_SINK:S_SINK + 16], AF.Exp, ["small"], ["se"])
            P.ts("dve", WF, LG[:, 0:8], ctab[:, C_127MP:C_127MP + 1], None, ALU.mult, None, ["lg"], ["wf0"])
            P.act(WF, WF, AF.Exp, ["wf0"], ["wf"])
            P.ts("dve", WB, LG[:, 8:16], ctab[:, C_P:C_P + 1], None, ALU.mult, None, ["lg"], ["wb0"])
            P.act(WB, WB, AF.Exp, ["wb0"], ["wbw"])
            for h in range(8):
                hs = slice(h * 128, (h + 1) * 128)
                P.act(tmp1, ctab[:, C_PF:C_PF + 128], AF.Exp, ["lg", ("t1",)], [("t1",)], scale=LG[:, h:h + 1])
                P.tt("dve", tmp1, tmp1, ctab[:, C_MF:C_MF + 128], ALU.mult, [("t1",)], [("t1",)])
                P.act(tmp2, ctab[:, C_PB:C_PB + 128], AF.Exp, ["lg", ("t2",)], [("t2",)], scale=LG[:, 8 + h:9 + h])
                P.tt("dve", tmp2, tmp2, ctab[:, C_MB:C_MB + 128], ALU.mult, [("t2",)], [("t2",)])
                P.tt("dve", dmat[:, hs], tmp1, tmp2, ALU.add, [("t1",), ("t2",)], [("dmat", h)])
                P.act(wqf[:, hs], ctab[:, C_IP1:C_IP1 + 128], AF.Exp, ["lg"], [("wqf", h)], scale=LG[:, h:h + 1])
                P.act(wqb[:, hs], ctab[:, C_I128M:C_I128M + 128], AF.Exp, ["lg"], [("wqb", h)], scale=LG[:, 8 + h:9 + h])
            P.barrier()
            restore_wb()

        def alloc_dense(nxs=2):
            A.reset()
            C.H = A.get([128, 32, 512], BF16)
            C.W = [A.get([128, 16, 512], BF16) for _ in range(3)]
            C.XS = [A.get([128, D], F32) for _ in range(nxs)]
            C.XN = A.get([128, D], BF16)
            C.JUNK = A.get([128, 512], BF16)

        def Htok_all():
            return [("H", c, b) for c in range(32) for b in range(4)]

        def norm_tile(src, row0, src_tok, gcol):
            for b in range(4):
                s = P.count("xs", len(C.XS))
                xs = C.XS[s]
                P.dma("sp", xs, src[row0 + b * 128:row0 + (b + 1) * 128, :], r=src_tok, w=[("xs", s)])
                P.memset("dve", st[:, 0:8], 0.0, [("ss",)], [("ss",)])
                for c in range(8):
                    P.act(C.JUNK, xs[:, c * 512:(c + 1) * 512], AF.Square, [("xs", s), ("ss",)], [("ssc", c)],
                          accum_out=st[:, c:c + 1])
                P.add("dve", lambda e: e.tensor_reduce(out=st[:, 8:9], in_=st[:, 0:8], axis=AX.X, op=ALU.add),
                      [("ssc", c) for c in range(8)], [("ss",), ("st8",)])
                P.act(st[:, 9:10], st[:, 8:9], AF.Ln, [("st8",)], [("st9",)], scale=1.0 / D, bias=EPSC)
                P.act(st[:, 10:11], st[:, 9:10], AF.Exp, [("st9",)], [("st10",)], scale=-0.5)
                P.act(C.XN, xs, AF.Copy, [("xs", s), ("st10",), ("xn",)], [("xn",)], scale=st[:, 10:11])
                bs = [P.bank() for _ in range(4)]
                for c in range(32):
                    P.trn(psbf(bs[c // 8])[:, (c % 8) * 128:(c % 8 + 1) * 128], C.XN[:, c * 128:(c + 1) * 128], IDENT,
                          [("xn",), "ident"], [("ps", bs[c // 8])])
                for q in range(4):
                    P.tt("dve", C.H[:, q * 8:(q + 1) * 8, b * 128:(b + 1) * 128], v3(psbf(bs[q]), 8),
                         small[:, gcol + q * 8:gcol + (q + 1) * 8].unsqueeze(2).to_broadcast([128, 8, 128]), ALU.mult,
                         [("ps", bs[q]), "small"], [("H", c, b) for c in range(q * 8, (q + 1) * 8)])

        def wload(wb2d, k0, c0, toks):
            s = P.count("W", 3)
            src = wb2d.rearrange("(kc p) n -> p kc n", p=128)[:, k0:k0 + 16, c0:c0 + 512]
            P.dma("sp", C.W[s], src, r=toks, w=[("W", s)])
            return s

        def dense_group(wb2d, kparts, k0s, c0, toks, mode, src, srcname):
            bs = [P.bank() for _ in range(4)]
            nk = kparts * 16
            for kp in range(kparts):
                s = wload(wb2d, k0s[kp], c0, toks)
                Wt = C.W[s]
                for j in range(4):
                    for kc in range(16):
                        k = kp * 16 + kc
                        if mode == "FM":
                            P.mm(psb(bs[j]), Wt[:, kc, j * 128:(j + 1) * 128], src[:, k, :], k == 0, k == nk - 1,
                                 [("W", s)] + [(srcname, k, b) for b in range(4)], [("ps", bs[j])])
                        else:
                            P.mm(psb(bs[j]), src[:, k, j * 128:(j + 1) * 128], Wt[:, kc, :], k == 0, k == nk - 1,
                                 [("W", s), (srcname, k, j)], [("ps", bs[j])])
            return bs

        def phase1(l, src, src_tok_fn):
            alloc_dense()
            TAB = [A.get([128, 4, 512], F32) for _ in range(2)]
            OTM = [A.get([128, 4, 512], BF16) for _ in range(2)]
            OFM = [A.get([128, 4, 512], BF16) for _ in range(2)]
            RT = [A.get([128, 4, 128], F32) for _ in range(2)]
            for tt in range(NTT):
                tsl = P.count("tab", 2)
                P.dma("sp", TAB[tsl], d_rope[tt * 512:(tt + 1) * 512, :].rearrange("(b p) c -> p b c", p=128), w=[("tab", tsl)])
                norm_tile(src, tt * 512, src_tok_fn(tt), S_G1)
                for cg in range(18):
                    mode = "FM" if cg in CG_FM else "TM"
                    bs = dense_group(wb_in[l], 2, [0, 16], cg * 512, [("wb", l, "in", cg)], mode, C.H, "H")
                    if mode == "FM":
                        so = P.count("ofm", 2)
                        fn = AF.Gelu if cg < 2 else AF.Silu
                        for j in range(4):
                            P.act(OFM[so][:, j, :], psb(bs[j]), fn, [("ps", bs[j]), ("ofm", so)], [("ofmw", so, j)])
                        ch0 = {0: 0, 1: 4, 10: 8, 11: 12}[cg]
                        P.dma("act", pj_fm[ch0:ch0 + 4, :, tt * 512:(tt + 1) * 512].rearrange("c p t -> p c t"), OFM[so],
                              r=[("ofmw", so, j) for j in range(4)], w=[("ofm", so), ("pjfm", tt)])
                        continue
                    so = P.count("otm", 2)
                    ti = CG_TM.index(cg)
                    for j in range(4):
                        dst = OTM[so][:, j, :]
                        rd = [("ps", bs[j]), ("otm", so)]
                        wr = [("otmw", so, j)]
                        if cg in (2, 3):
                            P.act(dst, psb(bs[j]), AF.Gelu, rd, wr)
                        elif cg in (8, 9, 17):
                            if j % 2 == 0:
                                P.cp("act", dst, psb(bs[j]), rd, wr)
                            else:
                                P.cp("dve", dst, psb(bs[j]), rd, wr)
                        else:
                            scaled = cg in (6, 7, 12, 13, 14, 15)
                            tc = 2 if scaled else 0
                            cosb = TAB[tsl][:, j, tc * 128:(tc + 1) * 128].unsqueeze(1).to_broadcast([128, 4, 128])
                            sinb = TAB[tsl][:, j, (tc + 1) * 128:(tc + 2) * 128]
                            sin_lo = sinb[:, 0:64].unsqueeze(1).to_broadcast([128, 4, 64])
                            sin_hi = sinb[:, 64:128].unsqueeze(1).to_broadcast([128, 4, 64])
                            pv = v3(psb(bs[j]), 4)
                            ra, rb = RT[0], RT[1]
                            P.tt("dve", ra, pv, cosb, ALU.mult, [("ps", bs[j]), ("tab", tsl), ("ra",)], [("ra",)])
                            P.tt("dve", rb[:, :, 0:64], pv[:, :, 64:128], sin_lo, ALU.mult,
                                 [("ps", bs[j]), ("tab", tsl), ("rb",)], [("rb0",)])
                            P.tt("dve", rb[:, :, 64:128], pv[:, :, 0:64], sin_hi, ALU.mult,
                                 [("ps", bs[j]), ("tab", tsl), ("rb",)], [("rb1",)])
                            P.tt("dve", v3(dst, 4), ra, rb, ALU.add, rd[1:] + [("ra",), ("rb0",), ("rb1",)], wr + [("ra",), ("rb",)])
                    P.dma("act", pj_tm[tt * 512:(tt + 1) * 512, ti * 512:(ti + 1) * 512].rearrange("(b p) c -> p b c", p=128), OTM[so],
                          r=[("otmw", so, j) for j in range(4)], w=[("otm", so), ("pjtm", tt)])
            P.barrier()
            restore_wb()

        def phase2(l):
            A.reset()
            if l == 0:
                emit_casts(cast_list(0, ["out", "up", "down"]))
            KVT = [[A.get([128, 2048], BF16) for _ in range(2)] for _ in range(2)]
            KW = [A.get([128, 8, 128], BF16) for _ in range(2)]
            SST = [[A.get([128, 8, 128], F32) for _ in range(2)] for _ in range(2)]
            SOUT = [[A.get([128, 1024], BF16) for _ in range(2)] for _ in range(2)]
            for di in range(2):
                P.memset("dve", flat(SST[di][0]), 0.0, [], [("S", di, 0, h) for h in range(8)])
            for step in range(NCH):
                for di in range(2):
                    wv = WF if di == 0 else WB
                    n = step if di == 0 else NCH - 1 - step
                    cb_, nb_ = step % 2, (step + 1) % 2
                    cur, nxs = SST[di][cb_], SST[di][nb_]
                    s = step % 2
                    kv = KVT[di][s]
                    P.dma("sp", kv, pj_tm[n * 128:(n + 1) * 128, RK0:RK0 + 2048], w=[("kvt", di, s)])
                    P.cp("act", SOUT[di][s], flat(cur), [("S", di, cb_, h) for h in range(8)] + [("sout", di, s)], [("soutw", di, s)])
                    P.dma("sp", d_S[di, n], SOUT[di][s], r=[("soutw", di, s)], w=[("sout", di, s), ("dS", di, n)])
                    P.tt("dve", KW[di], v3(kv[:, 0:1024], 8), wv.unsqueeze(2).to_broadcast([128, 8, 128]), ALU.mult,
                         [("kvt", di, s), ("kw", di)], [("kw", di)])
                    bs = [P.bank(), P.bank()]
                    for h in range(8):
                        P.mm(psb(bs[h // 4])[:, (h % 4) * 128:(h % 4 + 1) * 128], KW[di][:, h, :],
                             kv[:, 1024 + h * 128:1024 + (h + 1) * 128], True, True,
                             [("kw", di), ("kvt", di, s)], [("ps", bs[h // 4])])
                    for h in range(8):
                        P.stt("dve", nxs[:, h, :], cur[:, h, :], DEC[:, di * 8 + h:di * 8 + h + 1],
                              psb(bs[h // 4])[:, (h % 4) * 128:(h % 4 + 1) * 128], ALU.mult, ALU.add,
                              [("ps", bs[h // 4]), ("S", di, cb_, h), "dec"], [("S", di, nb_, h)])
            P.barrier()
            restore_wb()
            A.reset()
            TM = [A.get([128, TMW], BF16) for _ in range(2)]
            FMt = [A.get([128, 16, 512], BF16) for _ in range(2)]
            OUT = [A.get([128, 32, 512], BF16) for _ in range(1)]
            SF = [A.get([128, 8, 128], BF16) for _ in range(2)]
            SB = [A.get([128, 8, 128], BF16) for _ in range(2)]
            AKV = [A.get([128, 1024], BF16) for _ in range(4)]
            AKT = [A.get([128, 4, 128], BF16) for _ in range(4)]
            VNF = A.get([128, 1024], F32)
            VN = A.get([128, 1024], BF16)
            J2 = A.get([128, 1024], BF16)
            QT = A.get([128, 8, 128], BF16)
            KT = A.get([128, 8, 128], BF16)
            QWF = A.get([128, 8, 128], BF16)
            QWB = A.get([128, 8, 128], BF16)
            SMT = A.get([128, 8, 128], BF16)
            SQ = A.get([128, 1024], BF16)
            RSTD = A.get([128, 1024], F32)
            TMPF = [A.get([128, 4, 128], F32) for _ in range(2)]
            AQT = A.get([128, 16, 128], BF16)
            PT = [A.get([128, 512], BF16) for _ in range(6)]
            DEN = [A.get([128, 4, 128], F32) for _ in range(2)]
            S_LN = 16

            def prepare(m):
                r = m % 4
                P.dma("sp", AKV[r], pj_tm[m * 128:(m + 1) * 128, AK0:AK0 + 1024], w=[("akv", r)])
                b = P.bank()
                for kh in range(4):
                    P.trn(psbf(b)[:, kh * 128:(kh + 1) * 128], AKV[r][:, kh * 128:(kh + 1) * 128], IDENT,
                          [("akv", r), "ident"], [("ps", b)])
                P.cp("act", flat(AKT[r]), psbf(b)[:, 0:512], [("ps", b)], [("akt", r)])

            prepare(0)
            for n in range(NCH):
                if n + 1 < NCH:
                    prepare(n + 1)
                g4 = n % 4
                ncs = slice(g4 * 128, (g4 + 1) * 128)
                if g4 == 0:
                    fs = P.count("fmt", 2)
                    os_ = P.count("out", 1)
                    P.dma("sp", FMt[fs], pj_fm[:, :, n * 128:n * 128 + 512].rearrange("c p t -> p c t"), w=[("fmt", fs)])
                ts_ = P.count("tm", 2)
                TMs = TM[ts_]
                P.dma("sp", TMs, pj_tm[n * 128:(n + 1) * 128, :], w=[("tm", ts_)])
                ss_ = P.count("sfb", 2)
                P.dma("sp", flat(SF[ss_]), d_S[0, n], w=[("sf", ss_)])
                P.dma("sp", flat(SB[ss_]), d_S[1, n], w=[("sb", ss_)])
                outw = [("outw", os_, n % 4, i) for i in range(8)]
                def gen_sgu():
                    v = TMs[:, TV0:TV0 + 1024]
                    c0 = S_LN
                    P.memset("dve", st[:, c0:c0 + 2], 0.0, [("ln",)], [("ln0",)])
                    yield
                    P.act(J2, v, AF.Copy, [("tm", ts_), ("ln0",)], [("lna",)], accum_out=st[:, c0:c0 + 1])
                    yield
                    P.act(J2, v, AF.Square, [("tm", ts_), ("ln0",)], [("lnb",)], accum_out=st[:, c0 + 1:c0 + 2])
                    yield
                    P.ts("dve", st[:, c0 + 2:c0 + 4], st[:, c0:c0 + 2], 1.0 / 1024, None, ALU.mult, None, [("lna",), ("lnb",)], [("ln1",)])
                    yield
                    P.tt("dve", st[:, c0 + 4:c0 + 5], st[:, c0 + 2:c0 + 3], st[:, c0 + 2:c0 + 3], ALU.mult, [("ln1",)], [("ln2",)])
                    yield
                    P.tt("dve", st[:, c0 + 5:c0 + 6], st[:, c0 + 3:c0 + 4], st[:, c0 + 4:c0 + 5], ALU.subtract, [("ln1",), ("ln2",)], [("ln3",)])
                    yield
                    P.act(st[:, c0 + 8:c0 + 9], st[:, c0 + 5:c0 + 6], AF.Ln, [("ln3",)], [("ln3b",)], bias=EPSC)
                    yield
                    P.act(st[:, c0 + 6:c0 + 7], st[:, c0 + 8:c0 + 9], AF.Exp, [("ln3b",)], [("ln4",)], scale=-0.5)
                    yield
                    P.stt("dve", st[:, c0 + 7:c0 + 8], st[:, c0 + 2:c0 + 3], -1.0, st[:, c0 + 6:c0 + 7], ALU.mult, ALU.mult,
                          [("ln1",), ("ln4",)], [("ln5",)])
                    yield
                    P.act(VNF, v, AF.Identity, [("tm", ts_), ("ln4",), ("ln5",), ("vnf",)], [("vnf",)],
                          scale=st[:, c0 + 6:c0 + 7], bias=st[:, c0 + 7:c0 + 8])
                    yield
                    P.tt("dve", VNF, VNF, small[:, S_LNG:S_LNG + 1024], ALU.mult, [("vnf",), "small"], [("vnf",), ("ln",)])
                    yield
                    P.tt("dve", VN, VNF, small[:, S_LNB:S_LNB + 1024], ALU.add, [("vnf",), "small", ("vn",)], [("vn",)])
                    yield
                    bs = [P.bankr("mixA", (0, 1, 2, 3)), P.bankr("mixA", (0, 1, 2, 3))]
                    for g in range(8):
                        P.mm(psb(bs[g // 4])[:, (g % 4) * 128:(g % 4 + 1) * 128], VN[:, g * 128:(g + 1) * 128],
                             wst_bf[:, g * 128:(g + 1) * 128], True, True, [("vn",), "wst"], [("ps", bs[g // 4])])
                        yield
                    for gb in range(2):
                        tf = TMPF[gb]
                        P.tt("dve", flat(tf), psb(bs[gb]), small[:, S_BSB + gb * 512:S_BSB + (gb + 1) * 512], ALU.add,
                             [("ps", bs[gb]), "small", ("tmpf", gb)], [("tmpf", gb)])
                        yield
                        P.tt("dve", OUT[os_][:, gb * 4:(gb + 1) * 4, ncs], tf, FMt[fs][:, gb * 4:(gb + 1) * 4, ncs], ALU.mult,
                             [("tmpf", gb), ("fmt", fs), ("out", os_)], [outw[gb]])
                        yield
                def gen_ret():
                    bq, bk = P.bankr("mixA", (0, 1, 2, 3)), P.bankr("mixA", (0, 1, 2, 3))
                    for h in range(8):
                        P.trn(psbf(bq)[:, h * 128:(h + 1) * 128], TMs[:, RQ0 + h * 128:RQ0 + (h + 1) * 128], IDENT,
                              [("tm", ts_), "ident"], [("ps", bq)])
                        yield
                    for h in range(8):
                        P.trn(psbf(bk)[:, h * 128:(h + 1) * 128], TMs[:, RK0 + h * 128:RK0 + (h + 1) * 128], IDENT,
                              [("tm", ts_), "ident"], [("ps", bk)])
                        yield
                    P.cp("act", flat(QT), psbf(bq), [("ps", bq), ("qt",)], [("qt",)])
                    yield
                    P.cp("dve", flat(KT), psbf(bk), [("ps", bk), ("kt",)], [("kt",)])
                    yield
                    P.tt("dve", flat(QWF), flat(QT), wqf[:], ALU.mult, [("qt",), ("qwf",)] + [("wqf", h) for h in range(8)], [("qwf",)])
                    yield
                    P.tt("dve", flat(QWB), flat(QT), wqb[:], ALU.mult, [("qt",), ("qwb",)] + [("wqb", h) for h in range(8)], [("qwb",)])
                    yield
                    bsc = [P.bankr("mixA", (0, 1, 2, 3)), P.bankr("mixA", (0, 1, 2, 3))]
                    for h in range(8):
                        P.mm(psb(bsc[h // 4])[:, (h % 4) * 128:(h % 4 + 1) * 128], KT[:, h, :], QT[:, h, :], True, True,
                             [("kt",), ("qt",)], [("ps", bsc[h // 4])])
                        yield
                    for hb in range(2):
                        P.tt("dve", flat(SMT[:, hb * 4:(hb + 1) * 4, :]), psb(bsc[hb]), dmat[:, hb * 512:(hb + 1) * 512], ALU.mult,
                             [("ps", bsc[hb]), ("smt", hb)] + [("dmat", h) for h in range(8)], [("smt", hb)])
                        yield
                    br = [P.bankr("mixA", (0, 1, 2, 3)), P.bankr("mixA", (0, 1, 2, 3))]
                    for h in range(8):
                        o = psb(br[h // 4])[:, (h % 4) * 128:(h % 4 + 1) * 128]
                        wr = [("ps", br[h // 4])]
                        P.mm(o, TMs[:, RV0 + h * 128:RV0 + (h + 1) * 128], SMT[:, h, :], True, False, [("tm", ts_), ("smt", h // 4)], wr)
                        yield
                        P.mm(o, SF[ss_][:, h, :], QWF[:, h, :], False, False, [("sf", ss_), ("qwf",)], wr)
                        yield
                        P.mm(o, SB[ss_][:, h, :], QWB[:, h, :], False, True, [("sb", ss_), ("qwb",)], wr)
                        yield
                    bm = [P.bankr("mixA", (0, 1, 2, 3)), P.bankr("mixA", (0, 1, 2, 3))]
                    for hb in range(2):
                        hs = slice(hb * 512, (hb + 1) * 512)
                        P.act(SQ[:, hs], psb(br[hb]), AF.Square, [("ps", br[hb]), ("sq", hb)], [("sq", hb)])
                        yield
                        P.mm(psb(bm[hb]), ONES, SQ[:, hs], True, True, ["ones", ("sq", hb)], [("ps", bm[hb])])
                        yield
                        P.act(RSTD[:, hs], psb(bm[hb]), AF.Ln, [("ps", bm[hb]), ("rstd", hb)], [("rstd", hb)], scale=1.0 / 128, bias=EPSC)
                        yield
                        P.act(RSTD[:, hs], RSTD[:, hs], AF.Exp, [("rstd", hb)], [("rstd", hb)], scale=-0.5)
                        yield
                        tf = TMPF[hb]
                        P.tt("dve", flat(tf), psb(br[hb]), RSTD[:, hs], ALU.mult, [("ps", br[hb]), ("rstd", hb), ("tmpf", hb)], [("tmpf", hb)])
                        yield
                        P.tt("dve", OUT[os_][:, 8 + hb * 4:8 + (hb + 1) * 4, ncs], tf, FMt[fs][:, 8 + hb * 4:8 + (hb + 1) * 4, ncs], ALU.mult,
                             [("tmpf", hb), ("fmt", fs), ("out", os_)], [outw[2 + hb]])
                        yield
                def gen_att():
                    ba = [P.bankr("mixB", (4, 5, 6, 7)), P.bankr("mixB", (4, 5, 6, 7))]
                    for hq in range(16):
                        P.trn(psbf(ba[hq // 8])[:, (hq % 8) * 128:(hq % 8 + 1) * 128], TMs[:, AQ0 + hq * 128:AQ0 + (hq + 1) * 128], IDENT,
                              [("tm", ts_), "ident"], [("ps", ba[hq // 8])])
                        yield
                    P.cp("act", flat(AQT[:, 0:8, :]), psbf(ba[0]), [("ps", ba[0]), ("aqt", 0)], [("aqt", 0)])
                    yield
                    P.cp("dve", flat(AQT[:, 8:16, :]), psbf(ba[1]), [("ps", ba[1]), ("aqt", 1)], [("aqt", 1)])
                    yield
                    for kh in range(4):
                        ms = [m for m in (n - 1, n, n + 1) if 0 <= m < NCH]
                        pts = []
                        for m in ms:
                            b = P.bankr("mixB", (4, 5, 6, 7))
                            pi = P.count("pt", 6)
                            pts.append(pi)
                            P.mm(psb(b), AKT[m % 4][:, kh, :], flat(AQT[:, kh * 4:(kh + 1) * 4, :]), True, m == n,
                                 [("akt", m % 4), ("aqt", kh // 2)], [("ps", b)])
                            yield
                            if m != n:
                                mk = MPREV4 if m < n else MNEXT4
                                mt = [("mprev" if m < n else "mnext", g) for g in range(4)]
                                P.mm(psb(b), IDENT, mk, False, True, ["ident"] + mt, [("ps", b)])
                                yield
                            P.act(PT[pi], psb(b), AF.Exp, [("ps", b), ("pt", pi)], [("pt", pi)])
                            yield
                        bo, bd = P.bankr("mixB", (4, 5, 6, 7)), P.bankr("mixB", (4, 5, 6, 7))
                        for i, m in enumerate(ms):
                            P.mm(psb(bo), AKV[m % 4][:, 512 + kh * 128:512 + (kh + 1) * 128], PT[pts[i]], i == 0, i == len(ms) - 1,
                                 [("akv", m % 4), ("pt", pts[i])], [("ps", bo)])
                            yield
                        for i, m in enumerate(ms):
                            P.mm(psb(bd), ONES, PT[pts[i]], i == 0, i == len(ms) - 1, ["ones", ("pt", pts[i])], [("ps", bd)])
                            yield
                        dn = DEN[kh % 2]
                        for g in range(4):
                            P.act(dn[:, g, :], psb(bd)[:, g * 128:(g + 1) * 128], AF.Ln, [("ps", bd), "se", ("den", kh % 2)], [("dena", kh % 2, g)],
                                  bias=SE[:, kh * 4 + g:kh * 4 + g + 1])
                            yield
                        P.act(flat(dn), flat(dn), AF.Exp, [("dena", kh % 2, g) for g in range(4)], [("den", kh % 2)], scale=-1.0)
                        yield
                        P.tt("dve", OUT[os_][:, 16 + kh * 4:16 + (kh + 1) * 4, ncs], v3(psb(bo), 4), dn, ALU.mult,
                             [("ps", bo), ("den", kh % 2), ("out", os_)], [outw[4 + kh]])
                        yield
                gens = [_chain(gen_sgu(), gen_ret()), gen_att()]
                while gens:
                    for g_ in list(gens):
                        try:
                            next(g_)
                        except StopIteration:
                            gens.remove(g_)
                if g4 == 3:
                    P.dma("sp", mixT[:, :, (n - 3) * 128:(n + 1) * 128].rearrange("c p t -> p c t"), OUT[os_],
                          r=[("outw", os_, q, i) for q in range(4) for i in range(8)], w=[("out", os_), ("mix", n // 4)])
            P.barrier()
            restore_wb()

        def phase3(l, res_is_x):
            alloc_dense(1)
            AT = A.get([128, 32, 512], BF16)
            XR = [A.get([128, 4, 512], F32) for _ in range(2)]
            RL = [A.get([128, 512], F32) for _ in range(2)]

            def resid_evac(bs, src, src_tok, tt, cg):
                s = P.count("xr", 2)
                view = lambda t: t[tt * 512:(tt + 1) * 512, cg * 512:(cg + 1) * 512].rearrange("(b p) c -> p b c", p=128)
                P.dma("sp", XR[s], view(src), r=src_tok, w=[("xr", s)])
                for j in range(4):
                    P.tt("dve", XR[s][:, j, :], psb(bs[j]), XR[s][:, j, :], ALU.add, [("ps", bs[j]), ("xr", s)], [("xrw", s, j)])
                P.dma("act", view(XB), XR[s], r=[("xrw", s, j) for j in range(4)], w=[("xr", s), ("XB", tt, cg)])

            nxt = cast_list(l + 1, ["in", "out", "up", "down"]) if l + 1 < L else []
            pace = {"g": 0, "i": 0}
            stride = 4 if NTT > 1 else 1

            def paced_cast(bs, tt):
                if tt == 0 and NTT > 1:
                    return
                pace["g"] += 1
                if pace["g"] % stride == 0 and pace["i"] < len(nxt):
                    emit_casts(nxt[pace["i"]:pace["i"] + 1], extra=[P.tw[("ps", bs[0])]])
                    pace["i"] += 1

            for tt in range(NTT):
                P.dma("sp", C.H, mixT[:, :, tt * 512:(tt + 1) * 512].rearrange("c p t -> p c t"), w=Htok_all())
                for cg in range(8):
                    bs = dense_group(wb_out[l], 2, [0, 16], cg * 512, [("wb", l, "out", cg)], "TM", C.H, "H")
                    paced_cast(bs, tt)
                    if res_is_x:
                        resid_evac(bs, x_in, [], tt, cg)
                    else:
                        resid_evac(bs, XB, [("XB", tt, cg)], tt, cg)
                norm_tile(XB, tt * 512, [("XB", tt, cg) for cg in range(8)], S_G2)
                for q in range(4):
                    for cg in range(8):
                        bs = dense_group(wb_up[l], 2, [0, 16], q * 4096 + cg * 512, [("wb", l, "up", q * 8 + cg)], "FM", C.H, "H")
                        paced_cast(bs, tt)
                        for j in range(4):
                            rs = P.count("relu", 2)
                            P.act(RL[rs], psb(bs[j]), AF.Relu, [("ps", bs[j]), ("rl", rs)], [("rl", rs)])
                            P.tt("dve", AT[:, cg * 4 + j, :], RL[rs], RL[rs], ALU.mult,
                                 [("rl", rs)], [("AT", cg * 4 + j, b) for b in range(4)])
                    for cg in range(8):
                        bs = dense_group(wb_down[l], 2, [q * 32, q * 32 + 16], cg * 512,
                                         [("wb", l, "down", q * 2, cg), ("wb", l, "down", q * 2 + 1, cg)], "TM", AT, "AT")
                        paced_cast(bs, tt)
                        resid_evac(bs, XB, [("XB", tt, cg)], tt, cg)
            emit_casts(nxt[pace["i"]:])
            P.barrier()
            restore_wb()

        def final_norm():
            A.reset()
            GF = A.get([128, D], F32)
            XS = [A.get([128, D], F32) for _ in range(2)]
            JK = A.get([128, 512], BF16)
            P.dma("sp", GF, d_gf, w=["gf"])
            for blk in range(NCH):
                s = P.count("fxs", 2)
                xs = XS[s]
                P.dma("sp", xs, XB[blk * 128:(blk + 1) * 128, :], w=[("fxs", s)])
                P.memset("dve", st[:, 0:8], 0.0, [("ss",)], [("ss",)])
                for c in range(8):
                    P.act(JK, xs[:, c * 512:(c + 1) * 512], AF.Square, [("fxs", s), ("ss",)], [("ssc", c)], accum_out=st[:, c:c + 1])
                P.add("dve", lambda e: e.tensor_reduce(out=st[:, 8:9], in_=st[:, 0:8], axis=AX.X, op=ALU.add),
                      [("ssc", c) for c in range(8)], [("ss",), ("st8",)])
                P.act(st[:, 9:10], st[:, 8:9], AF.Ln, [("st8",)], [("st9",)], scale=1.0 / D, bias=EPSC)
                P.act(st[:, 10:11], st[:, 9:10], AF.Exp, [("st9",)], [("st10",)], scale=-0.5)
                P.act(xs, xs, AF.Copy, [("fxs", s), ("st10",)], [("fxs", s)], scale=st[:, 10:11])
                P.tt("dve", xs, xs, GF, ALU.mult, [("fxs", s), "gf"], [("fxs", s)])
                P.dma("act", y_out[blk * 128:(blk + 1) * 128, :], xs, r=[("fxs", s)], w=[("fxs", s)])
            P.barrier()

        for l in range(L):
            layer_consts(l)
            if l == 0:
                phase1(l, x_in, lambda tt: [])
            else:
                phase1(l, XB, lambda tt: [])
            phase2(l)
            phase3(l, res_is_x=(l == 0))
        final_norm()
        for e in ENGINES:
            P.add(e, lambda eng: eng.nop(), [], [])
        P.emit(nc, es)
    return nc


def _ctab():
    p = np.arange(128, dtype=np.float32)[:, None]
    i = np.arange(128, dtype=np.float32)[None, :]
    t = np.zeros((128, NCT), np.float32)
    t[:, C_PF:C_PF + 128] = np.maximum(i - p, 0)
    t[:, C_MF:C_MF + 128] = (i >= p)
    t[:, C_PB:C_PB + 128] = np.maximum(p - i, 0)
    t[:, C_MB:C_MB + 128] = (p > i)
    t[:, C_IP1:C_IP1 + 128] = i + 1
    t[:, C_I128M:C_I128M + 128] = 128 - i
    t[:, C_C128:C_C128 + 128] = 128.0
    t[:, C_ID:C_ID + 128] = (i == p)
    t[:, C_MPREV:C_MPREV + 128] = (p >= i)
    t[:, C_MNEXT:C_MNEXT + 128] = (p <= i)
    t[:, C_ONES:C_ONES + 128] = 1.0
    t[:, C_127MP] = 127 - p[:, 0]
    t[:, C_P] = p[:, 0]
    t[:, C_EPS] = EPS
    return t


def _rope(NT):
    pos = np.arange(NT, dtype=np.float32)
    inv = (np.float32(10000.0) ** (-np.arange(0, 128, 2, dtype=np.float32) / np.float32(128))).astype(np.float32)
    ang = (pos[:, None] * inv[None, :]).astype(np.float32)
    ang = np.concatenate([ang, ang], -1)
    cos = np.cos(ang).astype(np.float32)
    sin = np.sin(ang).astype(np.float32)
    sins = np.concatenate([-sin[:, :64], sin[:, 64:]], -1)
    sc = np.float32(128 ** -0.5)
    return np.ascontiguousarray(np.concatenate([cos, sins, cos * sc, sins * sc], -1).astype(np.float32))


def _host_params(inp, L):
    small = np.zeros((L, 128, NS), np.float32)
    wst = np.zeros((L, 128, 1024), np.float32)
    for l in range(L):
        small[l, :, S_G1:S_G1 + 32] = inp["ln_mix_g"][l].reshape(32, 128).T
        small[l, :, S_G2:S_G2 + 32] = inp["ln_mlp_g"][l].reshape(32, 128).T
        small[l, :, S_DEC:S_DEC + 16] = inp["ret_log_decay"][l].reshape(1, 16)
        small[l, :, S_SINK:S_SINK + 16] = inp["attn_sink"][l].reshape(1, 16)
        small[l, :, S_LNG:S_LNG + 1024] = inp["sgu_ln_g"][l][None, :]
        small[l, :, S_LNB:S_LNB + 1024] = inp["sgu_ln_b"][l][None, :]
        small[l, :, S_BSB:S_BSB + 1024] = inp["sgu_b"][l].reshape(1, 1024)
        wst[l] = np.transpose(inp["sgu_w"][l], (2, 0, 1)).reshape(128, 1024)
    gf = np.ascontiguousarray(np.broadcast_to(inp["final_norm_g"][None, :], (128, D))).astype(np.float32)
    return small, wst, gf


_NC_CACHE = {}


def run(inp, NT, L, seqs, n_cores):
    key = (NT, L)
    if key not in _NC_CACHE:
        _NC_CACHE[key] = build(NT, L)
    nc = _NC_CACHE[key]
    small, wst, gf = _host_params(inp, L)
    ctab = _ctab()
    rope = _rope(NT)
    f = lambda a: np.ascontiguousarray(np.asarray(a, dtype=np.float32))
    shared = {"w_in": f(inp["w_in"][:L]), "w_out": f(inp["w_out"][:L]), "w_up": f(inp["w_up"][:L]),
              "w_down": f(inp["w_down"][:L]), "small": small, "wst": wst, "ctab": ctab, "rope": rope, "gf": gf}
    zero = None
    in_maps = []
    for c in range(n_cores):
        m = dict(shared)
        if seqs[c] is None:
            if zero is None:
                zero = {k: np.zeros_like(shared[k]) for k in ("w_in", "w_out", "w_up", "w_down")}
                zero["x"] = np.zeros((NT, D), np.float32)
            m.update(zero)
        else:
            m["x"] = f(inp["x"][seqs[c]])
        in_maps.append(m)
    res = run_bass_kernel_spmd(nc, in_maps, core_ids=list(range(n_cores)))
    return [np.asarray(r["y"]) if seqs[c] is not None else None for c, r in enumerate(res.results)]


def kernel(x, ln_mix_g, w_in, sgu_ln_g, sgu_ln_b, sgu_w, sgu_b, ret_log_decay, attn_sink, w_out,
           ln_mlp_g, w_up, w_down, final_norm_g):
    inp = dict(x=np.asarray(x), ln_mix_g=np.asarray(ln_mix_g), w_in=np.asarray(w_in), sgu_ln_g=np.asarray(sgu_ln_g),
               sgu_ln_b=np.asarray(sgu_ln_b), sgu_w=np.asarray(sgu_w), sgu_b=np.asarray(sgu_b),
               ret_log_decay=np.asarray(ret_log_decay), attn_sink=np.asarray(attn_sink), w_out=np.asarray(w_out),
               ln_mlp_g=np.asarray(ln_mlp_g), w_up=np.asarray(w_up), w_down=np.asarray(w_down),
               final_norm_g=np.asarray(final_norm_g))
    B, S, _ = inp["x"].shape
    active = [0, 1, 4, 5]
    seqs = [None] * 8
    for b in range(B):
        seqs[active[b]] = b
    ys = run(inp, S, 2, seqs, 8)
    return np.stack([ys[active[b]] for b in range(B)], 0).astype(np.float32)
```
